# Optimizing a Trainium2 kernel written in Bass

```python
import numpy as np
import jax
import jax.numpy as jnp
from jax import lax

D_MODEL = 2048
BATCH = 4
SEQ = 2048
DEPTH = 4

EPS = 1e-6
NEG = -1e30
D_FF = 5632
Q_BLOCK = 128

A_GROUPS = ((128, 1), (512, 4), (2048, 16))
A_HEADS_PER_GROUP = 4
A_HEADS = A_HEADS_PER_GROUP * len(A_GROUPS)
A_HEAD_DIM = 64
A_WIDTH = A_HEADS * A_HEAD_DIM
A_OUT = A_HEADS_PER_GROUP * A_HEAD_DIM

B_WINDOWS = (2, 4, 8, 16)
B_GROUP_DIM = 128
B_WIDTH = B_GROUP_DIM * len(B_WINDOWS)

C_KV_GROUPS = 2
C_HEADS_PER_GROUP = 3
C_HEADS = C_KV_GROUPS * C_HEADS_PER_GROUP
C_HEAD_DIM = 128
C_WIDTH = C_HEADS * C_HEAD_DIM
C_KV_WIDTH = C_KV_GROUPS * C_HEAD_DIM
CMP_LEN = 32
CMP_STRIDE = 16
SEL_BLOCK = 64
SEL_TOPN = 8
FORCE_BONUS = 100.0
WIN = 512

N_BRANCH = 3
COL_SIZES = (A_WIDTH, A_WIDTH, A_WIDTH, B_WIDTH, C_WIDTH, C_KV_WIDTH, C_KV_WIDTH, C_KV_WIDTH, C_KV_WIDTH, C_KV_WIDTH, C_KV_WIDTH, 3 * C_HEADS, N_BRANCH * D_MODEL)
IN_COLS = sum(COL_SIZES)

kernel_name = 'hybrid_dilated_pool_nsa_trunk'


def _rmsnorm(x, g):
    xf = x.astype(jnp.float32)
    y = xf * lax.rsqrt(jnp.mean(xf * xf, axis=-1, keepdims=True) + EPS)
    return y.astype(x.dtype) * g


def _swiglu(x, w_i, w_o):
    a, b = jnp.split(x @ w_i, 2, axis=-1)
    return (jax.nn.silu(a) * b) @ w_o


def _dilated_attention(q, k, v):
    bsz, s, _ = q.shape
    nq = s // Q_BLOCK
    shp = (bsz, s, len(A_GROUPS), A_HEADS_PER_GROUP, A_HEAD_DIM)
    qh = q.reshape(shp).transpose(2, 0, 3, 1, 4)
    kh = k.reshape(shp).transpose(2, 0, 3, 1, 4)
    vh = v.reshape(shp).transpose(2, 0, 3, 1, 4)
    scale = A_HEAD_DIM ** -0.5
    outs, lses = [], []
    for gi, (win, dil) in enumerate(A_GROUPS):
        offs = jnp.arange(win // dil + 1) * dil
        kg, vg = kh[gi], vh[gi]
        q_chunks = qh[gi].reshape(bsz, A_HEADS_PER_GROUP, nq, Q_BLOCK, A_HEAD_DIM).transpose(2, 0, 1, 3, 4)

        def block(args, kg=kg, vg=vg, offs=offs):
            q_blk, start = args
            t = start + jnp.arange(Q_BLOCK)
            idx = t[:, None] - offs[None, :]
            valid = idx >= 0
            idx = jnp.maximum(idx, 0)
            kk = jnp.take(kg, idx, axis=2)
            vv = jnp.take(vg, idx, axis=2)
            sc = jnp.einsum('bhqd,bhqkd->bhqk', q_blk, kk, preferred_element_type=jnp.float32) * scale
            sc = jnp.where(valid, sc, NEG)
            m = jnp.max(sc, axis=-1, keepdims=True)
            p = jnp.exp(sc - m)
            l = jnp.sum(p, axis=-1, keepdims=True)
            o = jnp.einsum('bhqk,bhqkd->bhqd', p, vv.astype(jnp.float32)) / l
            return o, (m + jnp.log(l))[..., 0]

        o, lse = lax.map(block, (q_chunks, jnp.arange(nq) * Q_BLOCK))
        outs.append(o.transpose(1, 2, 0, 3, 4).reshape(bsz, A_HEADS_PER_GROUP, s, A_HEAD_DIM))
        lses.append(lse.transpose(1, 2, 0, 3).reshape(bsz, A_HEADS_PER_GROUP, s))
    w = jax.nn.softmax(jnp.stack(lses), axis=0)
    o = jnp.sum(w[..., None] * jnp.stack(outs), axis=0)
    return o.transpose(0, 2, 1, 3).reshape(bsz, s, A_OUT).astype(q.dtype)


def _pool_mixer(v, w_pool, scale):
    bsz, s, _ = v.shape
    vf = v.astype(jnp.float32).reshape(bsz, s, len(B_WINDOWS), B_GROUP_DIM)
    csum = jnp.concatenate([jnp.zeros_like(vf[:, :1]), jnp.cumsum(vf, axis=1)], axis=1)
    t = jnp.arange(s)
    pooled = []
    for gi, w in enumerate(B_WINDOWS):
        lo = jnp.maximum(t + 1 - w, 0)
        cnt = (t + 1 - lo).astype(jnp.float32)
        seg = csum[:, t + 1, gi] - csum[:, lo, gi]
        pooled.append(seg / cnt[None, :, None] - vf[:, :, gi])
    z = jnp.stack(pooled, axis=2).astype(v.dtype)
    z = jnp.einsum('bsgc,gcd->bsgd', z, w_pool)
    return z.reshape(bsz, s, B_WIDTH) * scale


def _nsa(q, k_cmp, v_cmp, k_slc, v_slc, k_win, v_win, gates, pe_k, w1_k, w2_k, pe_v, w1_v, w2_v):
    bsz, s, _ = q.shape
    G, Hg, dh = C_KV_GROUPS, C_HEADS_PER_GROUP, C_HEAD_DIM
    scale = dh ** -0.5
    nq = s // Q_BLOCK
    qh = q.reshape(bsz, s, G, Hg, dh).transpose(0, 2, 3, 1, 4)

    def kvh(z):
        return z.reshape(bsz, s, G, dh).transpose(0, 2, 1, 3)

    t = jnp.arange(s)

    n_cmp = (s - CMP_LEN) // CMP_STRIDE + 1
    blk = np.arange(n_cmp)[:, None] * CMP_STRIDE + np.arange(CMP_LEN)[None, :]

    def compress(z, pe, w1, w2):
        zb = kvh(z)[:, :, blk] + pe
        return jax.nn.gelu(zb.reshape(bsz, G, n_cmp, CMP_LEN * dh) @ w1) @ w2

    kc = compress(k_cmp, pe_k, w1_k, w2_k)
    vc = compress(v_cmp, pe_v, w1_v, w2_v)
    cmp_end = np.arange(n_cmp) * CMP_STRIDE + CMP_LEN - 1
    cvalid = cmp_end[None, :] <= t[:, None]
    sc = jnp.einsum('bgnsd,bgcd->bgnsc', qh, kc, preferred_element_type=jnp.float32) * scale
    sc = jnp.where(cvalid, sc, NEG)
    p_cmp = jnp.where(cvalid, jax.nn.softmax(sc, axis=-1), 0.0)
    o_cmp = jnp.einsum('bgnsc,bgcd->bgnsd', p_cmp, vc.astype(jnp.float32))

    nb = s // SEL_BLOCK
    ci = np.arange(n_cmp)[:, None] * CMP_STRIDE
    bj = np.arange(nb)[None, :] * SEL_BLOCK
    overlap = ((ci < bj + SEL_BLOCK) & (ci + CMP_LEN > bj)).astype(np.float32)
    imp = jnp.einsum('bgnsc,cj->bgsj', p_cmp, overlap)
    jb = np.arange(nb)[None, :]
    tb = (t // SEL_BLOCK)[:, None]
    forced = (jb == 0) | (jb == tb) | (jb == tb - 1)
    score = jnp.where(jb > tb, -1.0, imp + jnp.where(forced, FORCE_BONUS, 0.0))
    k_sel = min(SEL_TOPN, nb)
    _, sel_idx = lax.top_k(score, k_sel)

    ks = kvh(k_slc).reshape(bsz, G, nb, SEL_BLOCK, dh)
    vs = kvh(v_slc).reshape(bsz, G, nb, SEL_BLOCK, dh)
    q_chunks = qh.reshape(bsz, G, Hg, nq, Q_BLOCK, dh).transpose(3, 0, 1, 2, 4, 5)
    idx_chunks = sel_idx.reshape(bsz, G, nq, Q_BLOCK, k_sel).transpose(2, 0, 1, 3, 4)
    gather = jax.vmap(jax.vmap(lambda blocks, ix: blocks[ix]))

    def sel_block(args):
        q_blk, ix, start = args
        tq = start + jnp.arange(Q_BLOCK)
        flat = ix.reshape(bsz, G, Q_BLOCK * k_sel)
        kk = gather(ks, flat).reshape(bsz, G, Q_BLOCK, k_sel * SEL_BLOCK, dh)
        vv = gather(vs, flat).reshape(bsz, G, Q_BLOCK, k_sel * SEL_BLOCK, dh)
        kpos = (ix[..., None] * SEL_BLOCK + jnp.arange(SEL_BLOCK)).reshape(bsz, G, Q_BLOCK, k_sel * SEL_BLOCK)
        valid = kpos <= tq[None, None, :, None]
        sc_s = jnp.einsum('bgnqd,bgqkd->bgnqk', q_blk, kk, preferred_element_type=jnp.float32) * scale
        sc_s = jnp.where(valid[:, :, None], sc_s, NEG)
        p = jax.nn.softmax(sc_s, axis=-1)
        return jnp.einsum('bgnqk,bgqkd->bgnqd', p, vv.astype(jnp.float32))

    o_slc = lax.map(sel_block, (q_chunks, idx_chunks, jnp.arange(nq) * Q_BLOCK))
    o_slc = o_slc.transpose(1, 2, 3, 0, 4, 5).reshape(bsz, G, Hg, s, dh)

    nband = WIN // Q_BLOCK + 1

    def band(z):
        zp = jnp.pad(kvh(z), ((0, 0), (0, 0), (WIN, 0), (0, 0))).reshape(bsz, G, nq + nband - 1, Q_BLOCK, dh)
        return jnp.concatenate([zp[:, :, j:j + nq] for j in range(nband)], axis=3)

    kw = band(k_win)
    vw = band(v_win)
    qb = qh.reshape(bsz, G, Hg, nq, Q_BLOCK, dh)
    qpos = np.arange(nq)[:, None] * Q_BLOCK + np.arange(Q_BLOCK)[None, :]
    kpos = np.arange(nq)[:, None] * Q_BLOCK - WIN + np.arange(nband * Q_BLOCK)[None, :]
    wvalid = (kpos[:, None, :] <= qpos[:, :, None]) & (kpos[:, None, :] > qpos[:, :, None] - WIN) & (kpos[:, None, :] >= 0)
    sc_w = jnp.einsum('bgnicd,bgikd->bgnick', qb, kw, preferred_element_type=jnp.float32) * scale
    sc_w = jnp.where(wvalid, sc_w, NEG)
    p_w = jax.nn.softmax(sc_w, axis=-1)
    o_win = jnp.einsum('bgnick,bgikd->bgnicd', p_w, vw.astype(jnp.float32)).reshape(bsz, G, Hg, s, dh)

    g = jax.nn.sigmoid(gates.astype(jnp.float32)).reshape(bsz, s, G, Hg, 3).transpose(0, 2, 3, 1, 4)
    o = g[..., 0:1] * o_cmp + g[..., 1:2] * o_slc + g[..., 2:3] * o_win
    return o.transpose(0, 3, 1, 2, 4).reshape(bsz, s, C_WIDTH).astype(q.dtype)


def _mixing(u, w_in, pool_w, pool_scale, pe_k, w1_k, w2_k, pe_v, w1_v, w2_v, proj_a, proj_b, proj_c, w_out):
    bsz, s, _ = u.shape
    splits = np.cumsum(COL_SIZES)[:-1].tolist()
    qa, ka, va, xb, qc, kcmp, vcmp, kslc, vslc, kwin, vwin, gc, gm = jnp.split(u @ w_in, splits, axis=-1)
    ya = _dilated_attention(qa, ka, va) @ proj_a
    yb = _pool_mixer(xb, pool_w, pool_scale) @ proj_b
    yc = _nsa(qc, kcmp, vcmp, kslc, vslc, kwin, vwin, gc, pe_k, w1_k, w2_k, pe_v, w1_v, w2_v) @ proj_c
    g = jax.nn.sigmoid(gm.reshape(bsz, s, N_BRANCH, D_MODEL))
    return (g[:, :, 0] * ya + g[:, :, 1] * yb + g[:, :, 2] * yc) @ w_out


def setup_inputs(seed: int = 0) -> dict:
    key = jax.random.key(seed)
    ks = jax.random.split(key, 24)
    f32 = jnp.float32

    def dense(k, shape):
        return jax.random.normal(k, shape, f32) * (shape[-2] ** -0.5)

    def gain(k, shape):
        return 1.0 + 0.02 * jax.random.normal(k, shape, f32)

    L = DEPTH
    cdim = CMP_LEN * C_HEAD_DIM
    return {
        'x': jax.random.normal(ks[0], (BATCH, SEQ, D_MODEL), f32),
        'ffn1_norm': gain(ks[1], (L, D_MODEL)),
        'ffn1_wi': dense(ks[2], (L, D_MODEL, 2 * D_FF)),
        'ffn1_wo': dense(ks[3], (L, D_FF, D_MODEL)),
        'mix_norm': gain(ks[4], (L, D_MODEL)),
        'w_in': dense(ks[5], (L, D_MODEL, IN_COLS)),
        'pool_w': dense(ks[6], (L, len(B_WINDOWS), B_GROUP_DIM, B_GROUP_DIM)),
        'pool_scale': gain(ks[7], (L, B_WIDTH)),
        'cmp_pe_k': 0.02 * jax.random.normal(ks[8], (L, CMP_LEN, C_HEAD_DIM), f32),
        'cmp_w1_k': dense(ks[9], (L, cdim, C_HEAD_DIM)),
        'cmp_w2_k': dense(ks[10], (L, C_HEAD_DIM, C_HEAD_DIM)),
        'cmp_pe_v': 0.02 * jax.random.normal(ks[11], (L, CMP_LEN, C_HEAD_DIM), f32),
        'cmp_w1_v': dense(ks[12], (L, cdim, C_HEAD_DIM)),
        'cmp_w2_v': dense(ks[13], (L, C_HEAD_DIM, C_HEAD_DIM)),
        'proj_a': dense(ks[14], (L, A_OUT, D_MODEL)),
        'proj_b': dense(ks[15], (L, B_WIDTH, D_MODEL)),
        'proj_c': dense(ks[16], (L, C_WIDTH, D_MODEL)),
        'w_out': dense(ks[17], (L, D_MODEL, D_MODEL)),
        'ffn2_norm': gain(ks[18], (L, D_MODEL)),
        'ffn2_wi': dense(ks[19], (L, D_MODEL, 2 * D_FF)),
        'ffn2_wo': dense(ks[20], (L, D_FF, D_MODEL)),
        'final_norm': gain(ks[21], (D_MODEL,)),
    }


def reference(x, ffn1_norm, ffn1_wi, ffn1_wo, mix_norm, w_in, pool_w, pool_scale, cmp_pe_k, cmp_w1_k, cmp_w2_k, cmp_pe_v, cmp_w1_v, cmp_w2_v, proj_a, proj_b, proj_c, w_out, ffn2_norm, ffn2_wi, ffn2_wo, final_norm):
    for l in range(DEPTH):
        x = x + 0.5 * _swiglu(_rmsnorm(x, ffn1_norm[l]), ffn1_wi[l], ffn1_wo[l])
        x = x + _mixing(_rmsnorm(x, mix_norm[l]), w_in[l], pool_w[l], pool_scale[l], cmp_pe_k[l], cmp_w1_k[l], cmp_w2_k[l], cmp_pe_v[l], cmp_w1_v[l], cmp_w2_v[l], proj_a[l], proj_b[l], proj_c[l], w_out[l])
        x = x + 0.5 * _swiglu(_rmsnorm(x, ffn2_norm[l]), ffn2_wi[l], ffn2_wo[l])
    return _rmsnorm(x, final_norm)
```

```python
import numpy as np
import concourse.bass as bass
import concourse.mybir as mybir
from concourse.bass_utils import run_bass_kernel_spmd

F32 = mybir.dt.float32
F32R = mybir.dt.float32r
BF16 = mybir.dt.bfloat16
AF = mybir.ActivationFunctionType
ALU = mybir.AluOpType

D = 2048
NT = 1024
DFF = 5632
NJ = DFF // 128
KC = D // 128
EPS = 1e-6
WQ = "pool"
JPASS = 15


class Tile:
    def __init__(self, name, ap=None):
        self.name = name
        self.ap = ap
        self.w = {}
        self.r = {}
        self.dsem = None
        self.dcount = 0

    def __getitem__(self, idx):
        return self.ap[idx]


class Par:
    def __init__(self, fn):
        self.fn = fn


class Sched:
    CE = ("pe", "act", "dve", "pool")

    def __init__(self, nc):
        self.nc = nc
        self.sem_pool = []
        self.csem = None
        self.tiles = {}
        self._reset()

    def _reset(self):
        self.ops = {e: [] for e in ("pe", "act", "dve", "pool", "sp")}
        self.cnt = {e: 0 for e in self.CE}
        self.waited = {}
        self.sems = {}
        self.dsems = []
        self.needed = {e: set() for e in self.CE}

    def dma_sem(self, tile):
        if tile.dsem is None:
            if self.sem_pool:
                tile.dsem, tile.dcount = self.sem_pool.pop()
            else:
                self.nsem = getattr(self, "nsem", 0) + 1
                tile.dsem, tile.dcount = self.nc.alloc_semaphore(name=f"dq{self.nsem}"), 0
            self.dsems.append(tile)
        return tile.dsem

    def _collect(self, cur, reads, writes):
        for t in list(reads) + list(writes):
            self.tiles[id(t)] = t
        deps = {}
        raw_self = 0
        for t in reads:
            for k, v in t.w.items():
                if k == cur:
                    raw_self = max(raw_self, v)
                elif deps.get(k, 0) < v:
                    deps[k] = v
        for t in writes:
            for k, v in t.w.items():
                if k != cur and deps.get(k, 0) < v:
                    deps[k] = v
            for k, v in t.r.items():
                if k != cur and deps.get(k, 0) < v:
                    deps[k] = v
        if raw_self and cur in ("act", "dve", "pool"):
            deps[cur] = raw_self
        waits = []
        for k, v in deps.items():
            if self.waited.get((cur, k), 0) >= v:
                continue
            self.waited[(cur, k)] = v
            waits.append((k, v))
            if k in self.CE:
                self.needed[k].add(v)
        return waits

    def op(self, eng, fn, reads=(), writes=()):
        waits = self._collect(eng, reads, writes)
        self.cnt[eng] += 1
        idx = self.cnt[eng]
        for t in reads:
            t.r[eng] = idx
        for t in writes:
            t.w[eng] = idx
        self.ops[eng].append(("op", waits, fn, idx))

    def dma(self, q, out, in_, sb, dram=None, load=True, extra_reads=(), **kw):
        sem = self.dma_sem(sb)
        key = ("d", sb.name)
        if load:
            reads, writes = ([dram] if dram is not None else []), [sb]
        else:
            reads, writes = [sb], ([dram] if dram is not None else [])
        reads = list(reads) + list(extra_reads)
        waits = self._collect(q, reads, writes)
        sb.dcount += 16
        val = sb.dcount
        for t in reads:
            t.r[key] = val
        for t in writes:
            t.w[key] = val
        self.sems[key] = sem
        if q in self.CE:
            self.cnt[q] += 0
        self.ops[q].append(("dma", waits, (out, in_, sem, kw), None))

    def par_of(self, eng):
        key = ("par", id(eng))
        if key not in self.eng_ctx:
            self.eng_ctx[key] = eng.partition_id() % 2
        return self.eng_ctx[key]

    def raw(self, eng, fn):
        self.ops[eng].append(("raw", [], fn, None))

    def end_phase(self):
        nc = self.nc
        if self.csem is None:
            self.csem = {e: nc.alloc_semaphore(name="c_" + e) for e in self.CE}
        self.emit()
        for t in self.dsems:
            self.sem_pool.append((t.dsem, t.dcount))
            t.dsem = None
            t.dcount = 0
        for t in self.tiles.values():
            t.w.clear()
            t.r.clear()
        self.tiles = {}
        self._reset()

    def emit(self):
        nc = self.nc
        for e in self.CE:
            if self.csem is not None:
                self.sems[e] = self.csem[e]
            else:
                self.sems[e] = nc.alloc_semaphore(name="c_" + e)
        sig = {}
        for e in self.CE:
            m = {}
            base = getattr(self, "sig_base", {})
            n = base.get(e, 0)
            for i in sorted(self.needed[e]):
                n += 1
                m[i] = n
            sig[e] = m
            base[e] = n
            self.sig_base = base
        engs = {"pe": nc.tensor, "act": nc.scalar, "dve": nc.vector, "pool": nc.gpsimd, "sp": nc.sync}
        with nc.Block() as block:
            self.eng_ctx = {}

            def body(ename):
                eng = engs[ename]
                for kind, waits, payload, idx in self.ops[ename]:
                    for k, v in waits:
                        if k in self.CE:
                            eng.wait_ge(self.sems[k], sig[k][v])
                        else:
                            eng.wait_ge(self.sems[k], v)
                    if kind == "raw":
                        payload(eng)
                    elif kind == "op":
                        ins = payload(eng)
                        if idx in sig[ename]:
                            ins.then_inc(self.sems[ename], 1)
                    else:
                        out, in_, sem, kw = payload
                        if isinstance(out, Par) or isinstance(in_, Par):
                            pv = self.par_of(eng)
                            for p in (0, 1):
                                o = out.fn(p) if isinstance(out, Par) else out
                                i = in_.fn(p) if isinstance(in_, Par) else in_
                                with (eng.If(pv == 0) if p == 0 else eng.Else()):
                                    eng.dma_start(out=o, in_=i, **kw).then_inc(sem, 16)
                        else:
                            eng.dma_start(out=out, in_=in_, **kw).then_inc(sem, 16)
                if ename == "sp":
                    for t in self.dsems:
                        eng.wait_ge(t.dsem, t.dcount)

            @block.tensor
            def _(e):
                body("pe")

            @block.scalar
            def _(e):
                body("act")

            @block.vector
            def _(e):
                body("dve")

            @block.gpsimd
            def _(e):
                body("pool")

            @block.sync
            def _(e):
                body("sp")


class Ctx:
    pass


def alloc_common(nc, S, stack):
    C = Ctx()

    def sb(name, shape, dt=F32):
        t = stack.enter_context(nc.sbuf_tensor(name, shape, dt))
        return Tile(name, t)

    def ps(name, shape, dt=F32):
        t = stack.enter_context(nc.psum_tensor(name, shape, dt))
        return Tile(name, t)

    C.sb = sb
    C.wq = WQ
    C.ps = ps
    C.big = sb("big", [128, KC, NT], F32R)
    C.psum = [ps(f"ps{i}", [128, 512]) for i in range(8)]
    C.ones = sb("ones", [128, 128], F32R)
    C.ones32 = sb("ones32", [128, 128], F32)
    S.op("dve", lambda e: e.memset(C.ones32.ap[:], 1.0), writes=[C.ones32])
    S.op("dve", lambda e: e.tensor_copy(out=C.ones.ap[:], in_=C.ones32.ap[:]), reads=[C.ones32], writes=[C.ones])
    C.rstd = sb("rstd", [128, NT])
    C.sq = [sb(f"sq{i}", [128, NT], F32R) for i in range(2)]
    C.gv = sb("gv", [128, KC])
    return C


def r32(ap):
    return ap if ap.dtype == F32R else ap.bitcast(F32R)


def f32(ap):
    return ap if ap.dtype == F32 else ap.bitcast(F32)


def rmsnorm_fm(S, C, xdram_chunks, gdram):
    big = C.big
    for k in range(KC):
        S.dma(C.wq, big.ap[:, k, :], r32(xdram_chunks[k].ap), sb=big, dram=xdram_chunks[k], load=True)
    S.dma("sp", C.gv.ap[:], gdram.rearrange("(c p) -> p c", p=128), sb=C.gv, load=True,
          allow_slow_non_contiguous=True)
    pa, pb = C.psum[0], C.psum[1]
    for k in range(KC):
        sq = C.sq[k % 2]
        S.op("act", lambda e, k=k, sq=sq: e.activation(out=r32(sq.ap[:]), in_=f32(big.ap[:, k, :]), func=AF.Square),
             reads=[big], writes=[sq])
        for h, p in enumerate((pa, pb)):
            S.op("pe", lambda e, k=k, sq=sq, h=h, p=p: e.matmul(
                p.ap[:], r32(C.ones.ap[:]), r32(sq.ap[:, h * 512:(h + 1) * 512]),
                start=(k == 0), stop=(k == KC - 1)), reads=[C.ones, sq], writes=[p])
    for h, p in enumerate((pa, pb)):
        S.op("act", lambda e, h=h, p=p: e.activation(
            out=C.rstd.ap[:, h * 512:(h + 1) * 512], in_=p.ap[:], func=AF.Sqrt,
            scale=1.0 / D, bias=C.epsb.ap[:]), reads=[p, C.epsb], writes=[C.rstd])
    S.op("dve", lambda e: e.reciprocal(out=C.rstd.ap[:], in_=C.rstd.ap[:]), reads=[C.rstd], writes=[C.rstd])
    for k in range(KC):
        S.op("dve", lambda e, k=k: e.scalar_tensor_tensor(
            out=big.ap[:, k, :], in0=f32(big.ap[:, k, :]), scalar=C.gv.ap[:, k:k + 1], in1=C.rstd.ap[:],
            op0=ALU.mult, op1=ALU.mult), reads=[big, C.gv, C.rstd], writes=[big])


def phase_ffn(S, C, xin_chunks, xout_chunks, gdram, wi, wo, JP=None):
    JP = JP or JPASS
    rmsnorm_fm(S, C, xin_chunks, gdram)
    big = C.big
    wslots = C.wslots
    gbuf = C.gbuf
    ws_i = [0]

    def next_slot():
        s = wslots[ws_i[0] % len(wslots)]
        ws_i[0] += 1
        return s

    wi_v = wi.rearrange("(k p) n -> p k n", p=128)
    wo_v = wo.rearrange("(j p) n -> p j n", p=128)
    npass = (NJ + JP - 1) // JP
    pbank = [0]
    for ps_ in range(npass):
        j0 = ps_ * JP
        nj = min(JP, NJ - j0)
        for jj in range(nj):
            j = j0 + jj
            slot = next_slot()
            wv = slot.ap[:, 0:2 * KC * 128].rearrange("p (a k n) -> p a k n", a=2, k=KC)
            S.dma(C.wq, wv[:, 0], wi_v[:, :, j * 128:(j + 1) * 128], sb=slot, load=True)
            S.dma(C.wq, wv[:, 1], wi_v[:, :, DFF + j * 128:DFF + (j + 1) * 128], sb=slot, load=True)
            base = (jj % 2) * 4
            P = C.psum[base:base + 4]
            for k in range(KC):
                for ab in range(2):
                    for h in range(2):
                        p = P[ab * 2 + h]
                        S.op("pe", lambda e, k=k, ab=ab, h=h, p=p, wv=wv: e.matmul(
                            p.ap[:], r32(wv[:, ab, k, :]), r32(big.ap[:, k, h * 512:(h + 1) * 512]),
                            start=(k == 0), stop=(k == KC - 1)), reads=[slot, big], writes=[p])
            for h in range(2):
                st = C.stmp[h]
                S.op("act", lambda e, h=h, st=st, P=P: e.activation(out=st.ap[:], in_=P[h].ap[:], func=AF.Silu),
                     reads=[P[h]], writes=[st])
                S.op("dve", lambda e, h=h, st=st, P=P, jj=jj: e.tensor_tensor(
                    out=r32(gbuf.ap[:, jj, h * 512:(h + 1) * 512]), in0=st.ap[:], in1=P[2 + h].ap[:], op=ALU.mult),
                    reads=[st, P[2 + h]], writes=[gbuf])
        if ps_ == 0 and getattr(C, "dbg", None) is not None:
            S.dma(C.wq, C.dbg, gbuf.ap[:], sb=gbuf, load=False)
        for fb in range(D // 256):
            slot = next_slot()
            wv = slot.ap[:, 0:nj * 256].rearrange("p (j n) -> p j n", j=nj)
            S.dma(C.wq, wv, wo_v[:, j0:j0 + nj, fb * 256:(fb + 1) * 256], sb=slot, load=True)
            base = (pbank[0] % 2) * 4
            pbank[0] += 1
            P = C.psum[base:base + 4]
            for fs in range(2):
                for jj in range(nj):
                    for h in range(2):
                        p = P[fs * 2 + h]
                        S.op("pe", lambda e, fs=fs, jj=jj, h=h, p=p, wv=wv: e.matmul(
                            p.ap[:], r32(wv[:, jj, fs * 128:(fs + 1) * 128]),
                            r32(gbuf.ap[:, jj, h * 512:(h + 1) * 512]),
                            start=(jj == 0), stop=(jj == nj - 1)), reads=[slot, gbuf], writes=[p])
            for fs in range(2):
                fc = fb * 2 + fs
                xi = C.xin[fc % 2]
                xo = C.xout[fc % 2]
                src = xin_chunks[fc] if ps_ == 0 else xout_chunks[fc]
                S.dma("sp", xi.ap[:], f32(src.ap), sb=xi, dram=src, load=True)
                for h in range(2):
                    p = P[fs * 2 + h]
                    S.op("dve", lambda e, h=h, p=p, xi=xi, xo=xo: e.scalar_tensor_tensor(
                        out=xo.ap[:, h * 512:(h + 1) * 512], in0=p.ap[:], scalar=0.5,
                        in1=xi.ap[:, h * 512:(h + 1) * 512], op0=ALU.mult, op1=ALU.add),
                        reads=[p, xi], writes=[xo])
                S.dma("sp", f32(xout_chunks[fc].ap), xo.ap[:], sb=xo, dram=xout_chunks[fc], load=False)


def phase_final_norm(S, C, xin_chunks, out_chunks, gdram):
    rmsnorm_fm(S, C, xin_chunks, gdram)
    for k in range(KC):
        S.dma("sp", out_chunks[k].ap, f32(C.big.ap[:, k, :]), sb=C.big, dram=out_chunks[k], load=False)


def dram_chunks(name, ap2d, n=KC):
    return [Tile(f"{name}{i}", ap2d[i * 128:(i + 1) * 128, :]) for i in range(n)]


def alloc_ffn(nc, S, C, JP=None):
    JP = JP or JPASS
    C.wslots = [C.sb(f"wslot{i}", [128, 4096], F32R) for i in range(2)]
    C.gbuf = C.sb("gbuf", [128, JP, NT], F32R)
    C.stmp = [C.sb(f"stmp{i}", [128, 512]) for i in range(2)]
    C.xin = [C.sb(f"xin{i}", [128, NT]) for i in range(2)]
    C.xout = [C.sb(f"xout{i}", [128, NT]) for i in range(2)]
    C.epsb = C.sb("epsb", [128, 1])
    S.op("dve", lambda e: e.memset(C.epsb.ap[:], EPS), writes=[C.epsb])


def build_ffn_prog(final_norm=False):
    import contextlib
    nc = bass.Bass("TRN2", target_bir_lowering=False)
    xT = nc.dram_tensor("xT", [D, NT], F32R, kind="ExternalInput").ap()
    g = nc.dram_tensor("g", [D], F32, kind="ExternalInput").ap()
    wi = nc.dram_tensor("wi", [D, 2 * DFF], F32R, kind="ExternalInput").ap()
    wo = nc.dram_tensor("wo", [DFF, D], F32R, kind="ExternalInput").ap()
    yT = nc.dram_tensor("yT", [D, NT], F32, kind="ExternalOutput").ap()
    if final_norm:
        gf = nc.dram_tensor("gf", [D], F32, kind="ExternalInput").ap()
        zT = nc.dram_tensor("zT", [D, NT], F32, kind="ExternalOutput").ap()
    S = Sched(nc)
    with contextlib.ExitStack() as stack:
        C = alloc_common(nc, S, stack)
        alloc_ffn(nc, S, C)
        xin = dram_chunks("xT", xT)
        xout = dram_chunks("yT", yT)
        phase_ffn(S, C, xin, xout, g, wi, wo)
        if final_norm:
            phase_final_norm(S, C, xout, dram_chunks("zT", zT), gf)
        S.emit()
    return nc


IN_COLS = 11282
NCH = (IN_COLS + 127) // 128
SIG_FROM = 40


SHK_KA, SHK_KCMP, SHK_VCMP, SHK_KSLC, SHK_KWIN, SHK_XB, SHK_ROWS = 0, 768, 1024, 1280, 1536, 1792, 2304
SHV_COLS = 1280
VTM_COLS = 1298


def shk_row_of_chunk(c):
    if 6 <= c < 12:
        return SHK_KA + (c - 6) * 128
    if 18 <= c < 22:
        return SHK_XB + (c - 18) * 128
    for lo, base in ((28, SHK_KCMP), (30, SHK_VCMP), (32, SHK_KSLC), (36, SHK_KWIN)):
        if lo <= c < lo + 2:
            return base + (c - lo) * 128
    return None


def phase_proj(S, C, xin_chunks, gdram, w_in, cols_chunks, DEV=None):
    rmsnorm_fm(S, C, xin_chunks, gdram)
    big = C.big
    w_v = w_in.rearrange("(k p) n -> p k n", p=128)
    si = 0
    for c2 in range(0, NCH, 2):
        slot = C.wslots[si % 2]
        si += 1
        c_lo = c2 * 128
        ncol = min(256, IN_COLS - c_lo)
        wv = slot.ap[:, 0:KC * ncol].rearrange("p (k n) -> p k n", k=KC)
        S.dma(C.wq, wv, w_v[:, :, c_lo:c_lo + ncol], sb=slot, load=True)
        for cs in range(2):
            c = c2 + cs
            if c >= NCH:
                break
            m = min(128, IN_COLS - c * 128)
            base = (c % 4) * 2
            P = C.psum[base:base + 2]
            for k in range(KC):
                for h in range(2):
                    S.op("pe", lambda e, k=k, h=h, P=P, wv=wv, cs=cs, m=m: e.matmul(
                        P[h].ap[0:m, :], r32(wv[:, k, cs * 128:cs * 128 + m]),
                        r32(big.ap[:, k, h * 512:(h + 1) * 512]),
                        start=(k == 0), stop=(k == KC - 1)), reads=[slot, big], writes=[P[h]])
            ob = C.cout[c % 2]
            fn = AF.Sigmoid if c >= SIG_FROM else AF.Copy
            for h in range(2):
                if c < SIG_FROM and h == 1:
                    S.op("dve", lambda e, h=h, P=P, ob=ob, m=m: e.tensor_copy(
                        out=ob.ap[0:m, h * 512:(h + 1) * 512], in_=P[h].ap[0:m, :]), reads=[P[h]], writes=[ob])
                else:
                    S.op("act", lambda e, h=h, P=P, ob=ob, m=m, fn=fn: e.activation(
                        out=ob.ap[0:m, h * 512:(h + 1) * 512], in_=P[h].ap[0:m, :], func=fn),
                        reads=[P[h]], writes=[ob])
            S.dma("sp", cols_chunks[c].ap[0:m, :], ob.ap[0:m, :], sb=ob, dram=cols_chunks[c], load=False)
            if DEV is not None and shk_row_of_chunk(c) is not None:
                r0 = shk_row_of_chunk(c)
                S.dma("sp", Par(lambda p, r0=r0: DEV["shk"](p)[r0:r0 + 128, :]), ob.ap[:, :], sb=ob, dram=DEV["t_shared"], load=False)
    if DEV is None:
        return
    vts = [C.sb(f"vt{i}", [128, 256], BF16) for i in range(2)]
    vi = 0
    blocks = [(R_VA[0], 256, 0), (R_VA[0] + 256, 256, 256), (R_VA[0] + 512, 256, 512),
              (R_VSLC[0], 256, 768), (R_VWIN[0], 256, 1024), (R_GC[0], 18, 1280)]
    for (c_lo, ncol, v0) in blocks:
        slot = C.wslots[si % 2]
        si += 1
        wv = slot.ap[:, 0:KC * ncol].rearrange("p (k n) -> p k n", k=KC)
        S.dma(C.wq, wv, w_v[:, :, c_lo:c_lo + ncol], sb=slot, load=True)
        for tt in range(NT // 128):
            p = C.psum[tt % 4]
            for k in range(KC):
                S.op("pe", lambda e, k=k, tt=tt, p=p, wv=wv, ncol=ncol: e.matmul(
                    p.ap[:, 0:ncol], r32(big.ap[:, k, tt * 128:(tt + 1) * 128]), r32(wv[:, k, :]),
                    start=(k == 0), stop=(k == KC - 1)), reads=[slot, big], writes=[p])
            vt = vts[vi % 2]
            vi += 1
            fn = AF.Sigmoid if ncol == 18 else AF.Copy
            S.op("act", lambda e, p=p, vt=vt, ncol=ncol, fn=fn: e.activation(out=vt.ap[:, 0:ncol], in_=p.ap[:, 0:ncol], func=fn),
                 reads=[p], writes=[vt])
            S.dma("sp", DEV["vtm"][tt * 128:(tt + 1) * 128, v0:v0 + ncol], vt.ap[:, 0:ncol], sb=vt, dram=DEV["t_vtm"], load=False)
            if ncol == 256:
                S.dma("sp", Par(lambda p, tt=tt, v0=v0: DEV["shv"](p)[tt * 128:(tt + 1) * 128, v0:v0 + 256]), vt.ap[:, :],
                      sb=vt, dram=DEV["t_shared"], load=False)
    nt = C.sb("noncet", [1, 16], mybir.dt.int32)
    S.dma("sp", nt.ap[:], DEV["nonce"], sb=nt)
    tf = Tile("flagdram")
    S.dma("sp", Par(lambda p: DEV["flag"](p)), nt.ap[:], sb=nt, dram=tf, load=False, extra_reads=[DEV["t_shared"]])


def build_proj_prog():
    import contextlib
    nc = bass.Bass("TRN2", target_bir_lowering=False)
    xT = nc.dram_tensor("xT", [D, NT], F32R, kind="ExternalInput").ap()
    g = nc.dram_tensor("g", [D], F32, kind="ExternalInput").ap()
    w_in = nc.dram_tensor("w_in", [D, IN_COLS], F32R, kind="ExternalInput").ap()
    colsT = nc.dram_tensor("colsT", [NCH * 128, NT], BF16, kind="ExternalOutput").ap()
    S = Sched(nc)
    with contextlib.ExitStack() as stack:
        C = alloc_common(nc, S, stack)
        C.wslots = [C.sb(f"wslot{i}", [128, 4096], F32R) for i in range(2)]
        C.cout = [C.sb(f"cout{i}", [128, NT], BF16) for i in range(2)]
        C.epsb = C.sb("epsb", [128, 1])
        S.op("dve", lambda e: e.memset(C.epsb.ap[:], EPS), writes=[C.epsb])
        phase_proj(S, C, dram_chunks("xT", xT), g, w_in, dram_chunks("colsT", colsT, NCH))
        S.emit()
    return nc


BIG = 30000.0
A_SCALE = 64 ** -0.5
C_SCALE = 128 ** -0.5
NQT = NT // 128
NKB = 16


def mm(S, out_t, out_ap, lhsT_t, lhsT_ap, rhs_t, rhs_ap, start, stop):
    rd = [t for t in (lhsT_t if isinstance(lhsT_t, (list, tuple)) else [lhsT_t])]
    rd += [t for t in (rhs_t if isinstance(rhs_t, (list, tuple)) else [rhs_t])]
    S.op("pe", lambda e: e.matmul(out_ap, lhsT_ap, rhs_ap, start=start, stop=stop), reads=rd, writes=[out_t])


def attn_block(S, C, q_t, q_ap, nq, k_t, k_ap, masks, v_list, accs, acc_w, first, last, scale):
    i = C.sbi
    C.sbi += 1
    sb = C.psum[i % 2]
    pT = C.pT[i % 2]
    nk = k_ap.shape[-1]
    mm(S, sb, sb.ap[0:nk, 0:nq], k_t, k_ap, q_t, q_ap, True, len(masks) == 0)
    for mi, (lt, la, rt, ra) in enumerate(masks):
        mm(S, sb, sb.ap[0:nk, 0:nq], lt, la, rt, ra, False, mi == len(masks) - 1)
    S.op("act", lambda e: e.activation(out=pT.ap[0:nk, 0:nq], in_=sb.ap[0:nk, 0:nq], func=AF.Exp, scale=scale),
         reads=[sb], writes=[pT])
    for n, (acc, (vt, va)) in enumerate(zip(accs, v_list)):
        mm(S, acc, acc.ap[:, 0:acc_w], pT, pT.ap[0:nk, n * 128:(n + 1) * 128], vt, va, first, last)


def phase_mix(S, C, I, oT_chunks):
    sb = C.sb
    DEV = I.get("_dev")

    def ld_act(name, tile, out_ap, sel=None):
        if DEV is None:
            src = I[name]
            if name in ("kA", "qA"):
                src = src[:, sel, :]
            elif name == "vA":
                src = src[:, :, sel, :]
            S.dma("pool" if name == "xb" else "sp", out_ap, src, sb=tile)
            return
        cols, vtm = DEV["cols"], DEV["vtm"]
        def prv(o, apfn):
            S.dma("pool", o, Par(lambda p: apfn(DEV["shk"](1 - p), DEV["shv"](1 - p))), sb=tile, dram=DEV["t_prev"])
        if name == "qA":
            S.dma("sp", out_ap, cols[sel * 64:(sel + 1) * 64, :], sb=tile, dram=DEV["t_cols"])
        elif name == "kA":
            S.dma("sp", out_ap[:, NT:2 * NT], cols[R_KA[0] + sel * 64:R_KA[0] + (sel + 1) * 64, :], sb=tile, dram=DEV["t_cols"])
            prv(out_ap[:, 0:NT], lambda shk, shv: shk[SHK_KA + sel * 64:SHK_KA + (sel + 1) * 64, :])
        elif name == "vA":
            S.dma("sp", out_ap[:, 8:16, :], vtm[:, sel * 64:(sel + 1) * 64].rearrange("(kb p) c -> p kb c", p=128),
                  sb=tile, dram=DEV["t_vtm"])
            prv(out_ap[:, 0:8, :], lambda shk, shv: shv[:, sel * 64:(sel + 1) * 64].rearrange("(kb p) c -> p kb c", p=128))
        elif name == "xb":
            S.dma("pool", out_ap[:, :, 16:16 + NT], cols[R_XB[0]:R_XB[1], :].rearrange("(g p) t -> p g t", p=128),
                  sb=tile, dram=DEV["t_cols"])
            prv(out_ap[:, :, 0:16], lambda shk, shv: shk[SHK_XB:SHK_XB + 512, NT - 16:NT].rearrange("(g p) t -> p g t", p=128))
        elif name == "qC3":
            for g in range(2):
                for n in range(3):
                    r0 = R_QC[0] + (g * 3 + n) * 128
                    S.dma("sp", out_ap[:, g, :, n * 128:(n + 1) * 128],
                          cols[r0:r0 + 128, :].rearrange("p (qt q) -> p qt q", q=128), sb=tile, dram=DEV["t_cols"])
        elif name in ("kcmp", "vcmp", "kslc", "kwin"):
            r = {"kcmp": R_KCMP, "vcmp": R_VCMP, "kslc": R_KSLC, "kwin": R_KWIN}[name]
            so = {"kcmp": SHK_KCMP, "vcmp": SHK_VCMP, "kslc": SHK_KSLC, "kwin": SHK_KWIN}[name]
            S.dma("sp", out_ap[:, :, NT:2 * NT], cols[r[0]:r[1], :].rearrange("(g p) t -> p g t", p=128), sb=tile, dram=DEV["t_cols"])
            prv(out_ap[:, :, 0:NT], lambda shk, shv: shk[so:so + 256, :].rearrange("(g p) t -> p g t", p=128))
        elif name in ("vslc", "vwin"):
            c0 = 768 if name == "vslc" else 1024
            for g_ in range(2):
                cg = c0 + g_ * 128
                S.dma("sp", out_ap[:, 8:16, g_, :], vtm[:, cg:cg + 128].rearrange("(kb p) d -> p kb d", p=128), sb=tile, dram=DEV["t_vtm"])
                prv(out_ap[:, 0:8, g_, :], lambda shk, shv, cg=cg: shv[:, cg:cg + 128].rearrange("(kb p) d -> p kb d", p=128))
        elif name == "gc":
            S.dma("sp", out_ap, vtm[:, 1280:1298].rearrange("(qt p) c -> p qt c", p=128), sb=tile, dram=DEV["t_vtm"])
        else:
            raise KeyError(name)

    if DEV is not None:
        S.raw("pool", DEV["poll"])
    ident = sb("ident", [128, 128], BF16)
    identf = sb("identf", [128, 128], F32)
    S.dma("sp", ident.ap[:], I["ident"], sb=ident)
    S.dma("sp", identf.ap[:], I["identf"], sb=identf)
    am = sb("amask", [128, 7, 128], BF16)
    S.dma("sp", am.ap[:], I["amask"], sb=am)
    pk = sb("pkill", [128, 384], BF16)
    S.dma("sp", pk.ap[:], I["pkill"], sb=pk)
    wm = sb("wmask", [128, 2, 384], BF16)
    S.dma("sp", wm.ap[:], I["wmask"], sb=wm)
    C.pT = [sb(f"pT{i}", [128, 384], BF16) for i in range(2)]
    C.sbi = 0
    ostage = [sb(f"ostage{i}", [128, 128], F32) for i in range(2)]
    osi = [0]
    accs = C.psum[2:5]
    misc = C.psum[5]

    def emit_T(src_t, src_ap, chunk, qt):
        st = ostage[osi[0] % 2]
        osi[0] += 1
        S.op("pe", lambda e: e.transpose(misc.ap[:, 0:128], src_ap, identf.ap[:]), reads=[src_t, identf], writes=[misc])
        S.op("dve", lambda e: e.tensor_copy(out=st.ap[:], in_=misc.ap[:, 0:128]), reads=[misc], writes=[st])
        S.dma("sp", f32(oT_chunks[chunk].ap)[:, qt * 128:(qt + 1) * 128], st.ap[:], sb=st, dram=oT_chunks[chunk], load=False)

    kA3 = sb("kA3", [64, 3, 2 * NT], BF16)
    qA3 = sb("qA3", [64, 3, NT], BF16)
    vA3 = sb("vA3", [128, NKB, 3, 65], BF16)
    oA = [sb(f"oA{i}", [128, 256], F32) for i in range(NQT)]
    rl = sb("rl", [128, 8], F32)
    S.op("pool", lambda e: e.memset(vA3.ap[:, :, :, 64:65], 1.0), writes=[vA3])
    def a_mask(g, delta):
        if g == 0:
            return 0 if delta == 0 else 1
        if g == 1:
            return 2 if delta == 0 else (4 if delta == 4 else 3)
        return 5 if delta == 0 else 6
    maxd = (1, 4, 15)
    for h in range(4):
        for g in range(3):
            hh = g * 4 + h
            ld_act("kA", kA3, kA3.ap[:, g, :], hh)
            ld_act("qA", qA3, qA3.ap[:, g, :], hh)
            ld_act("vA", vA3, vA3.ap[:, :, g, 0:64], hh)
        for qt in range(NQT):
            qb = 8 + qt
            blocks = [(g, kb) for g in range(3) for kb in range(max(0, qb - maxd[g]), qb + 1)]
            acc = accs[qt % 3]
            for bi, (g, kb) in enumerate(blocks):
                masks = [(ident, ident.ap[:], am, am.ap[:, a_mask(g, qb - kb), :])]
                if kb < 8:
                    masks.append((ident, ident.ap[:], pk, pk.ap[:, 0:128]))
                attn_block(S, C, qA3, qA3.ap[:, g, qt * 128:(qt + 1) * 128], 128,
                           kA3, kA3.ap[:, g, kb * 128:(kb + 1) * 128], masks,
                           [(vA3, vA3.ap[:, kb, g, :])], [acc], 65, bi == 0, bi == len(blocks) - 1, A_SCALE)
            S.op("dve", lambda e, acc=acc: e.reciprocal(out=rl.ap[:, 0:1], in_=acc.ap[:, 64:65]), reads=[acc], writes=[rl])
            S.op("dve", lambda e, acc=acc, qt=qt, h=h: e.tensor_scalar(
                out=oA[qt].ap[:, h * 64:(h + 1) * 64], in0=acc.ap[:, 0:64], scalar1=rl.ap[:, 0:1], scalar2=None,
                op0=ALU.mult), reads=[acc, rl], writes=[oA[qt]])
    for qt in range(NQT):
        for c in range(2):
            emit_T(oA[qt], oA[qt].ap[:, c * 128:(c + 1) * 128], c, qt)

    xb = sb("xb", [128, 4, 16 + NT], F32)
    ld_act("xb", xb, xb.ap[:])
    if DEV is not None:
        hfl = sb("hflag", [128, 1], F32)
        S.dma("sp", hfl.ap[:], I["hflag"], sb=hfl)
        S.op("pool", lambda e: e.tensor_scalar(out=xb.ap[:, :, 0:16], in0=xb.ap[:, :, 0:16], scalar1=hfl.ap[:, 0:1], scalar2=None, op0=ALU.mult), reads=[xb, hfl], writes=[xb])
    icnt = sb("icnt", [128, 4, NT], F32)
    S.dma("sp", icnt.ap[:], I["invcnt"], sb=icnt)
    pw = sb("poolw", [128, 4, 128], F32)
    S.dma("sp", pw.ap[:], I["pool_w"].rearrange("g c d -> c g d"), sb=pw)
    pwb = sb("poolwb", [128, 4, 128], BF16)
    S.op("pool", lambda e: e.tensor_copy(out=pwb.ap[:], in_=pw.ap[:]), reads=[pw], writes=[pwb])
    psc = sb("poolsc", [128, 4], F32)
    S.dma("sp", psc.ap[:], I["pool_scale"].rearrange("(g p) -> p g", p=128), sb=psc, allow_slow_non_contiguous=True)
    t1 = sb("pl1", [128, 16 + NT], F32)
    t2 = sb("pl2", [128, 16 + NT], F32)
    zb = sb("zb", [128, NT], BF16)
    ybs = sb("ybs", [128, NT], F32)
    S.op("pool", lambda e: e.memset(t1.ap[:, 0:16], 0.0), writes=[t1])
    S.op("pool", lambda e: e.memset(t2.ap[:, 0:16], 0.0), writes=[t2])
    L = 16 + NT
    for gi in range(4):
        src_t, src = xb, xb.ap[:, gi, :]
        tt = [t1, t2]
        a0 = 0
        for s_ in range(gi + 1):
            sh = 1 << s_
            dst = tt[s_ % 2]
            a0 += sh
            S.op("pool", lambda e, dst=dst, src=src, sh=sh, a0=a0: e.tensor_tensor(
                out=dst.ap[:, a0:L], in0=src[:, a0:L], in1=src[:, a0 - sh:L - sh], op=ALU.add),
                reads=[src_t], writes=[dst])
            src_t, src = dst, dst.ap[:, :]
        S.op("dve", lambda e, src=src, gi=gi: e.tensor_tensor(out=ybs.ap[:], in0=src[:, 16:L], in1=icnt.ap[:, gi, :], op=ALU.mult),
             reads=[src_t, icnt], writes=[ybs])
        S.op("dve", lambda e, gi=gi: e.tensor_tensor(out=zb.ap[:], in0=ybs.ap[:], in1=xb.ap[:, gi, 16:L], op=ALU.subtract),
             reads=[ybs, xb], writes=[zb])
        for hf in range(2):
            P = C.psum[6 + hf]
            mm(S, P, P.ap[:, :], pwb, pwb.ap[:, gi, :], zb, zb.ap[:, hf * 512:(hf + 1) * 512], True, True)
            S.op("dve", lambda e, P=P, hf=hf, gi=gi: e.tensor_scalar(
                out=ybs.ap[:, hf * 512:(hf + 1) * 512], in0=P.ap[:, :], scalar1=psc.ap[:, gi:gi + 1], scalar2=None,
                op0=ALU.mult), reads=[P, psc], writes=[ybs])
        S.dma("sp", f32(oT_chunks[2 + gi].ap)[:, :], ybs.ap[:], sb=ybs, dram=oT_chunks[2 + gi], load=False)

    qC3 = sb("qC3", [128, 2, NQT, 384], BF16)
    ld_act("qC3", qC3, qC3.ap[:])
    kcmp = sb("kcmp", [128, 2, 2, 2 * NT], BF16)
    ld_act("kcmp", kcmp, kcmp.ap[:, 0])
    ld_act("vcmp", kcmp, kcmp.ap[:, 1])
    kslc = sb("kslc", [128, 2, 2 * NT], BF16)
    ld_act("kslc", kslc, kslc.ap[:])
    kwin = sb("kwin", [128, 2, 2 * NT], BF16)
    ld_act("kwin", kwin, kwin.ap[:])
    vslc = sb("vslc", [128, NKB, 2, 129], BF16)
    vwin = sb("vwin", [128, NKB, 2, 129], BF16)
    S.op("pool", lambda e: e.memset(vslc.ap[:, :, :, 128:129], 1.0), writes=[vslc])
    S.op("pool", lambda e: e.memset(vwin.ap[:, :, :, 128:129], 1.0), writes=[vwin])
    ld_act("vslc", vslc, vslc.ap[:, :, :, 0:128])
    ld_act("vwin", vwin, vwin.ap[:, :, :, 0:128])
    gc = sb("gc", [128, NQT, 18], BF16)
    ld_act("gc", gc, gc.ap[:])
    gcf = sb("gcf", [128, NQT, 18], F32)
    S.op("pool", lambda e: e.tensor_copy(out=gcf.ap[:], in_=gc.ap[:]), reads=[gc], writes=[gcf])
    cm = sb("cmask", [128, NQT, 384], BF16)
    S.dma("sp", cm.ap[:], I["cmask"], sb=cm)
    Eh = sb("Eh", [33, 2 * NT], BF16)
    S.dma("sp", Eh.ap[:], I["Eh"], sb=Eh)
    vblk = sb("vblk", [128, NQT, 32], F32)
    S.dma("sp", vblk.ap[:], I["validblk"], sb=vblk)
    cstb = sb("cstb", [128, NQT, 32], F32)
    S.dma("sp", cstb.ap[:], I["constb"], sb=cstb)
    w1b = sb("w1b", [128, 2, 32, 128], BF16)
    S.dma("pool", w1b.ap[:, 0], I["w1_k"].rearrange("(j p) n -> p j n", p=128), sb=w1b)
    S.dma("pool", w1b.ap[:, 1], I["w1_v"].rearrange("(j p) n -> p j n", p=128), sb=w1b)
    w2 = sb("w2", [128, 2, 128], F32)
    S.dma("sp", w2.ap[:, 0], I["w2_k"], sb=w2)
    S.dma("sp", w2.ap[:, 1], I["w2_v"], sb=w2)
    w2b = sb("w2b", [128, 2, 128], BF16)
    S.op("pool", lambda e: e.tensor_copy(out=w2b.ap[:], in_=w2.ap[:]), reads=[w2], writes=[w2b])
    peT = sb("peT", [128, 2, 32], F32)
    S.dma("sp", peT.ap[:, 0], I["pe_k"].rearrange("j d -> d j"), sb=peT, allow_slow_non_contiguous=True)
    S.dma("sp", peT.ap[:, 1], I["pe_v"].rearrange("j d -> d j"), sb=peT, allow_slow_non_contiguous=True)
    peTb = sb("peTb", [128, 2, 32], BF16)
    S.op("pool", lambda e: e.tensor_copy(out=peTb.ap[:], in_=peT.ap[:]), reads=[peT], writes=[peTb])
    onesb = sb("onesb", [1, 128], BF16)
    S.op("pool", lambda e: e.memset(onesb.ap[:], 1.0), writes=[onesb])
    brow = sb("brow", [1, 2, 128], BF16)
    kcT = sb("kcT", [128, 2, 128], BF16)
    vca = sb("vca", [128, 2, 161], BF16)
    S.op("pool", lambda e: e.memset(vca.ap[:, :, 128:129], 1.0), writes=[vca])
    for g in range(2):
        S.dma("sp", vca.ap[0:127, g, 129:161], I["overlap"], sb=vca)
    hx = sb("hx", [128, 128], F32)
    hy = sb("hy", [128, 128], F32)
    hz = sb("hz", [128, 128], F32)
    hg = sb("hg", [128, 128], BF16)
    for kv in range(2):
        for j in range(32):
            mm(S, misc, misc.ap[0:1, 0:128], peTb, peTb.ap[:, kv, j:j + 1], w1b, w1b.ap[:, kv, j, :], j == 0, j == 31)
        S.op("dve", lambda e, kv=kv: e.tensor_copy(out=brow.ap[:, kv, :], in_=misc.ap[0:1, 0:128]), reads=[misc], writes=[brow])
        for g in range(2):
            P = C.psum[6]
            for j in range(32):
                mm(S, P, P.ap[:, 0:127], w1b, w1b.ap[:, kv, j, :], kcmp,
                   kcmp.ap[:, kv, g, j:j + 16 * 126 + 1:16], j == 0, False)
            mm(S, P, P.ap[:, 0:127], brow, brow.ap[:, kv, :], onesb, onesb.ap[:, 0:127], False, True)
            S.op("dve", lambda e, P=P: e.tensor_copy(out=hx.ap[:, 0:127], in_=P.ap[:, 0:127]), reads=[P], writes=[hx])
            S.op("dve", lambda e: e.tensor_tensor(out=hy.ap[:, 0:127], in0=hx.ap[:, 0:127], in1=hx.ap[:, 0:127], op=ALU.mult),
                 reads=[hx], writes=[hy])
            S.op("dve", lambda e: e.tensor_scalar(out=hy.ap[:, 0:127], in0=hy.ap[:, 0:127], scalar1=0.044715, scalar2=1.0,
                                                  op0=ALU.mult, op1=ALU.add), reads=[hy], writes=[hy])
            S.op("dve", lambda e: e.tensor_tensor(out=hz.ap[:, 0:127], in0=hy.ap[:, 0:127], in1=hx.ap[:, 0:127], op=ALU.mult),
                 reads=[hy, hx], writes=[hz])
            S.op("act", lambda e: e.activation(out=hy.ap[:, 0:127], in_=hz.ap[:, 0:127], func=AF.Sigmoid, scale=1.5957691216),
                 reads=[hz], writes=[hy])
            S.op("dve", lambda e: e.tensor_tensor(out=hg.ap[:, 0:127], in0=hy.ap[:, 0:127], in1=hx.ap[:, 0:127], op=ALU.mult),
                 reads=[hy, hx], writes=[hg])
            P2 = C.psum[7]
            if kv == 0:
                mm(S, P2, P2.ap[:, 0:127], w2b, w2b.ap[:, 0, :], hg, hg.ap[:, 0:127], True, True)
                S.op("dve", lambda e, g=g, P2=P2: e.tensor_copy(out=kcT.ap[:, g, 0:127], in_=P2.ap[:, 0:127]), reads=[P2], writes=[kcT])
            else:
                mm(S, P2, P2.ap[0:127, 0:128], hg, hg.ap[:, 0:127], w2b, w2b.ap[:, 1, :], True, True)
                S.op("dve", lambda e, g=g, P2=P2: e.tensor_copy(out=vca.ap[0:127, g, 0:128], in_=P2.ap[0:127, 0:128]), reads=[P2], writes=[vca])

    imp = sb("imp", [128, 32], F32)
    sc = sb("score", [128, 32], F32)
    top8 = sb("top8", [128, 8], F32)
    selb = sb("selb", [128, 33], F32)
    S.op("pool", lambda e: e.memset(selb.ap[:, 32:33], -1.0), writes=[selb])
    selT3 = sb("selT3", [33, 384], BF16)
    sg_ = sb("sgate", [128, 16], F32)
    oC = [sb(f"oC{i}", [128, 384], F32) for i in range(2)]
    oci = 0
    for g in range(2):
        for qt in range(NQT):
            qb = 8 + qt
            q_ap = qC3.ap[:, g, qt, :]
            ocur = oC[oci % 2]
            oci += 1
            attn_block(S, C, qC3, q_ap, 384, kcT, kcT.ap[:, g, 0:127],
                       [(ident, ident.ap[0:127, 0:127], cm, cm.ap[0:127, qt, :])],
                       [(vca, vca.ap[0:127, g, :])] * 3, accs, 161, True, True, C_SCALE)
            for n in range(3):
                acc = accs[n]
                gi0 = (g * 3 + n) * 3
                S.op("dve", lambda e, acc=acc, n=n: e.tensor_scalar(
                    out=rl.ap[:, n:n + 1], in0=acc.ap[:, 128:129], scalar1=1e-30, scalar2=None, op0=ALU.max),
                    reads=[acc], writes=[rl])
                S.op("dve", lambda e, n=n: e.reciprocal(out=rl.ap[:, n:n + 1], in_=rl.ap[:, n:n + 1]), reads=[rl], writes=[rl])
                if n == 0:
                    S.op("dve", lambda e, acc=acc, n=n: e.tensor_scalar(
                        out=imp.ap[:], in0=acc.ap[:, 129:161], scalar1=rl.ap[:, n:n + 1], scalar2=None, op0=ALU.mult),
                        reads=[acc, rl], writes=[imp])
                else:
                    S.op("dve", lambda e, acc=acc, n=n: e.scalar_tensor_tensor(
                        out=imp.ap[:], in0=acc.ap[:, 129:161], scalar=rl.ap[:, n:n + 1], in1=imp.ap[:],
                        op0=ALU.mult, op1=ALU.add), reads=[acc, rl, imp], writes=[imp])
                S.op("dve", lambda e, n=n, gi0=gi0, qt=qt: e.tensor_tensor(
                    out=sg_.ap[:, n:n + 1], in0=rl.ap[:, n:n + 1], in1=gcf.ap[:, qt, gi0:gi0 + 1], op=ALU.mult),
                    reads=[rl, gcf], writes=[sg_])
                S.op("dve", lambda e, acc=acc, n=n, ocur=ocur: e.tensor_scalar(
                    out=ocur.ap[:, n * 128:(n + 1) * 128], in0=acc.ap[:, 0:128], scalar1=sg_.ap[:, n:n + 1], scalar2=None,
                    op0=ALU.mult), reads=[acc, sg_], writes=[ocur])
            S.op("dve", lambda e, qt=qt: e.tensor_tensor(out=sc.ap[:], in0=imp.ap[:], in1=vblk.ap[:, qt, :], op=ALU.mult),
                 reads=[imp, vblk], writes=[sc])
            S.op("dve", lambda e, qt=qt: e.tensor_tensor(out=sc.ap[:], in0=sc.ap[:], in1=cstb.ap[:, qt, :], op=ALU.add),
                 reads=[sc, cstb], writes=[sc])
            S.op("dve", lambda e: e.max(out=top8.ap[:], in_=sc.ap[:]), reads=[sc], writes=[top8])
            S.op("dve", lambda e: e.tensor_scalar(out=selb.ap[:, 0:32], in0=sc.ap[:], scalar1=top8.ap[:, 7:8], scalar2=-1.0,
                                                  op0=ALU.is_ge, op1=ALU.add), reads=[sc, top8], writes=[selb])
            S.op("pe", lambda e: e.transpose(misc.ap[0:33, 0:128], selb.ap[:, 0:33], identf.ap[:]),
                 reads=[selb, identf], writes=[misc])
            for n in range(3):
                S.op("dve", lambda e, n=n: e.tensor_copy(out=selT3.ap[:, n * 128:(n + 1) * 128], in_=misc.ap[0:33, 0:128]),
                     reads=[misc], writes=[selT3])
            for kb in range(qb + 1):
                masks = [(Eh, Eh.ap[:, kb * 128:(kb + 1) * 128], selT3, selT3.ap[:, :])]
                if kb == qb:
                    masks.append((ident, ident.ap[:], wm, wm.ap[:, 0, :]))
                attn_block(S, C, qC3, q_ap, 384, kslc, kslc.ap[:, g, kb * 128:(kb + 1) * 128], masks,
                           [(vslc, vslc.ap[:, kb, g, :])] * 3, accs, 129, kb == 0, kb == qb, C_SCALE)
            for n in range(3):
                acc = accs[n]
                gi1 = (g * 3 + n) * 3 + 1
                S.op("dve", lambda e, acc=acc, n=n: e.reciprocal(out=rl.ap[:, n:n + 1], in_=acc.ap[:, 128:129]), reads=[acc], writes=[rl])
                S.op("dve", lambda e, n=n, gi1=gi1, qt=qt: e.tensor_tensor(
                    out=sg_.ap[:, n:n + 1], in0=rl.ap[:, n:n + 1], in1=gcf.ap[:, qt, gi1:gi1 + 1], op=ALU.mult),
                    reads=[rl, gcf], writes=[sg_])
                S.op("dve", lambda e, acc=acc, n=n, ocur=ocur: e.scalar_tensor_tensor(
                    out=ocur.ap[:, n * 128:(n + 1) * 128], in0=acc.ap[:, 0:128], scalar=sg_.ap[:, n:n + 1],
                    in1=ocur.ap[:, n * 128:(n + 1) * 128], op0=ALU.mult, op1=ALU.add), reads=[acc, sg_, ocur], writes=[ocur])
            for kb in range(qb - 4, qb + 1):
                masks = []
                if kb == qb:
                    masks.append((ident, ident.ap[:], wm, wm.ap[:, 0, :]))
                if kb == qb - 4:
                    masks.append((ident, ident.ap[:], wm, wm.ap[:, 1, :]))
                if kb < 8:
                    masks.append((ident, ident.ap[:], pk, pk.ap[:, :]))
                attn_block(S, C, qC3, q_ap, 384, kwin, kwin.ap[:, g, kb * 128:(kb + 1) * 128], masks,
                           [(vwin, vwin.ap[:, kb, g, :])] * 3, accs, 129, kb == qb - 4, kb == qb, C_SCALE)
            for n in range(3):
                acc = accs[n]
                gi2 = (g * 3 + n) * 3 + 2
                S.op("dve", lambda e, acc=acc, n=n: e.reciprocal(out=rl.ap[:, n:n + 1], in_=acc.ap[:, 128:129]), reads=[acc], writes=[rl])
                S.op("dve", lambda e, n=n, gi2=gi2, qt=qt: e.tensor_tensor(
                    out=sg_.ap[:, n:n + 1], in0=rl.ap[:, n:n + 1], in1=gcf.ap[:, qt, gi2:gi2 + 1], op=ALU.mult),
                    reads=[rl, gcf], writes=[sg_])
                S.op("dve", lambda e, acc=acc, n=n, ocur=ocur: e.scalar_tensor_tensor(
                    out=ocur.ap[:, n * 128:(n + 1) * 128], in0=acc.ap[:, 0:128], scalar=sg_.ap[:, n:n + 1],
                    in1=ocur.ap[:, n * 128:(n + 1) * 128], op0=ALU.mult, op1=ALU.add), reads=[acc, sg_, ocur], writes=[ocur])
            for n in range(3):
                emit_T(ocur, ocur.ap[:, n * 128:(n + 1) * 128], 6 + g * 3 + n, qt)


MIX_INPUTS = {
    "ident": ([128, 128], BF16), "identf": ([128, 128], F32), "amask": ([128, 7, 128], BF16),
    "pkill": ([128, 384], BF16), "wmask": ([128, 2, 384], BF16),
    "kA": ([64, 12, 2 * NT], BF16), "qA": ([64, 12, NT], BF16), "vA": ([128, NKB, 12, 64], BF16),
    "xb": ([128, 4, 16 + NT], BF16), "invcnt": ([128, 4, NT], F32),
    "pool_w": ([4, 128, 128], F32), "pool_scale": ([512], F32),
    "qC3": ([128, 2, NQT, 384], BF16), "kcmp": ([128, 2, 2 * NT], BF16), "vcmp": ([128, 2, 2 * NT], BF16),
    "kslc": ([128, 2, 2 * NT], BF16), "kwin": ([128, 2, 2 * NT], BF16),
    "vslc": ([128, NKB, 2, 128], BF16), "vwin": ([128, NKB, 2, 128], BF16),
    "gc": ([128, NQT, 18], BF16), "cmask": ([128, NQT, 384], BF16), "Eh": ([33, 2 * NT], BF16),
    "validblk": ([128, NQT, 32], F32), "constb": ([128, NQT, 32], F32), "overlap": ([127, 32], BF16),
    "w1_k": ([4096, 128], F32), "w1_v": ([4096, 128], F32), "w2_k": ([128, 128], F32), "w2_v": ([128, 128], F32),
    "pe_k": ([32, 128], F32), "pe_v": ([32, 128], F32),
}


def build_mix_prog():
    import contextlib
    nc = bass.Bass("TRN2", target_bir_lowering=False)
    I = {k: nc.dram_tensor(k, shp, dt, kind="ExternalInput").ap() for k, (shp, dt) in MIX_INPUTS.items()}
    oT = nc.dram_tensor("oT", [12 * 128, NT], F32, kind="ExternalOutput").ap()
    S = Sched(nc)
    with contextlib.ExitStack() as stack:
        C = Ctx()

        def sb(name, shape, dt=F32):
            return Tile("s_" + name, stack.enter_context(nc.sbuf_tensor("s_" + name, shape, dt)))
        C.sb = sb
        C.psum = [Tile(f"ps{i}", stack.enter_context(nc.psum_tensor(f"ps{i}", [128, 512], F32))) for i in range(8)]
        phase_mix(S, C, I, dram_chunks("oT", oT, 12))
        S.emit()
    return nc


def phase_out(S, C, oT_chunks, sg_chunks, xin_chunks, xout_chunks, pa, pb, pc, w_out):
    sb = C.sb
    big = C.big
    ot = sb("ot", [128, 12, NT], F32R)
    for k in range(12):
        S.dma(C.wq, ot.ap[:, k, :], r32(oT_chunks[k].ap), sb=ot, dram=oT_chunks[k], load=True)
    pslots = [sb(f"pslot{i}", [128, 12, 128], F32R) for i in range(2)]
    sgt = [sb(f"sgt{i}", [128, 3, NT], BF16) for i in range(2)]
    tm = [sb(f"otm{i}", [128, 512], F32) for i in range(2)]
    pa_v = pa.rearrange("(k p) n -> p k n", p=128)
    pb_v = pb.rearrange("(k p) n -> p k n", p=128)
    pc_v = pc.rearrange("(k p) n -> p k n", p=128)
    for f in range(KC):
        slot = pslots[f % 2]
        fs = slice(f * 128, (f + 1) * 128)
        S.dma(C.wq, slot.ap[:, 0:2, :], pa_v[:, :, fs], sb=slot)
        S.dma(C.wq, slot.ap[:, 2:6, :], pb_v[:, :, fs], sb=slot)
        S.dma(C.wq, slot.ap[:, 6:12, :], pc_v[:, :, fs], sb=slot)
        sg = sgt[f % 2]
        for b in range(3):
            S.dma("sp", sg.ap[:, b, :], sg_chunks[b * 16 + f].ap, sb=sg, dram=sg_chunks[b * 16 + f])
        P = C.psum[0:6]
        for b, (k0, k1) in enumerate(((0, 2), (2, 6), (6, 12))):
            for k in range(k0, k1):
                for h in range(2):
                    p = P[b * 2 + h]
                    mm(S, p, p.ap[:], slot, slot.ap[:, k, :], ot, ot.ap[:, k, h * 512:(h + 1) * 512], k == k0, k == k1 - 1)
        for h in range(2):
            hs = slice(h * 512, (h + 1) * 512)
            t0, t1 = tm
            S.op("dve", lambda e, h=h, hs=hs, sg=sg, t0=t0, P=P: e.tensor_tensor(out=t0.ap[:], in0=P[0 + h].ap[:], in1=sg.ap[:, 0, hs], op=ALU.mult),
                 reads=[P[0 + h], sg], writes=[t0])
            S.op("dve", lambda e, h=h, hs=hs, sg=sg, t1=t1, P=P: e.tensor_tensor(out=t1.ap[:], in0=P[2 + h].ap[:], in1=sg.ap[:, 1, hs], op=ALU.mult),
                 reads=[P[2 + h], sg], writes=[t1])
            S.op("dve", lambda e, t0=t0, t1=t1: e.tensor_tensor(out=t0.ap[:], in0=t0.ap[:], in1=t1.ap[:], op=ALU.add),
                 reads=[t0, t1], writes=[t0])
            S.op("dve", lambda e, h=h, hs=hs, sg=sg, t1=t1, P=P: e.tensor_tensor(out=t1.ap[:], in0=P[4 + h].ap[:], in1=sg.ap[:, 2, hs], op=ALU.mult),
                 reads=[P[4 + h], sg], writes=[t1])
            S.op("dve", lambda e, f=f, hs=hs, t0=t0, t1=t1: e.tensor_tensor(out=big.ap[:, f, hs], in0=t0.ap[:], in1=t1.ap[:], op=ALU.add),
                 reads=[t0, t1], writes=[big])
    wo_v = w_out.rearrange("(k p) n -> p k n", p=128)
    for fb in range(D // 256):
        slot = C.wslots[fb % 2]
        wv = slot.ap[:, 0:KC * 256].rearrange("p (k n) -> p k n", k=KC)
        S.dma(C.wq, wv, wo_v[:, :, fb * 256:(fb + 1) * 256], sb=slot)
        P = C.psum[(fb % 2) * 4:(fb % 2) * 4 + 4]
        for fs in range(2):
            for k in range(KC):
                for h in range(2):
                    p = P[fs * 2 + h]
                    mm(S, p, p.ap[:], slot, wv[:, k, fs * 128:(fs + 1) * 128], big, big.ap[:, k, h * 512:(h + 1) * 512],
                       k == 0, k == KC - 1)
        for fs in range(2):
            fc = fb * 2 + fs
            xi = C.xin[fc % 2]
            xo = C.xout[fc % 2]
            S.dma("sp", xi.ap[:], f32(xin_chunks[fc].ap), sb=xi, dram=xin_chunks[fc], load=True)
            for h in range(2):
                p = P[fs * 2 + h]
                S.op("dve", lambda e, h=h, p=p, xi=xi, xo=xo: e.tensor_tensor(
                    out=xo.ap[:, h * 512:(h + 1) * 512], in0=p.ap[:], in1=xi.ap[:, h * 512:(h + 1) * 512], op=ALU.add),
                    reads=[p, xi], writes=[xo])
            S.dma("sp", f32(xout_chunks[fc].ap), xo.ap[:], sb=xo, dram=xout_chunks[fc], load=False)


def build_out_prog():
    import contextlib
    nc = bass.Bass("TRN2", target_bir_lowering=False)
    oT = nc.dram_tensor("oT", [12 * 128, NT], F32R, kind="ExternalInput").ap()
    sg = nc.dram_tensor("sg", [48 * 128, NT], BF16, kind="ExternalInput").ap()
    xT = nc.dram_tensor("xT", [D, NT], F32R, kind="ExternalInput").ap()
    pa = nc.dram_tensor("proj_a", [256, D], F32R, kind="ExternalInput").ap()
    pb = nc.dram_tensor("proj_b", [512, D], F32R, kind="ExternalInput").ap()
    pc = nc.dram_tensor("proj_c", [768, D], F32R, kind="ExternalInput").ap()
    wo = nc.dram_tensor("w_out", [D, D], F32R, kind="ExternalInput").ap()
    yT = nc.dram_tensor("yT", [D, NT], F32, kind="ExternalOutput").ap()
    S = Sched(nc)
    with contextlib.ExitStack() as stack:
        C = Ctx()

        def sb(name, shape, dt=F32):
            return Tile("s_" + name, stack.enter_context(nc.sbuf_tensor("s_" + name, shape, dt)))
        C.sb = sb
        C.wq = WQ
        C.psum = [Tile(f"ps{i}", stack.enter_context(nc.psum_tensor(f"ps{i}", [128, 512], F32))) for i in range(8)]
        C.big = sb("big", [128, KC, NT], F32R)
        C.wslots = [sb(f"wslot{i}", [128, 4096], F32R) for i in range(2)]
        C.xin = [sb(f"xin{i}", [128, NT]) for i in range(2)]
        C.xout = [sb(f"xout{i}", [128, NT]) for i in range(2)]
        phase_out(S, C, dram_chunks("oT", oT, 12), dram_chunks("sg", sg, 48), dram_chunks("xT", xT),
                  dram_chunks("yT", yT), pa, pb, pc, wo)
        S.emit()
    return nc


import ml_dtypes
BF = ml_dtypes.bfloat16
R_QA, R_KA, R_VA, R_XB, R_QC = (0, 768), (768, 1536), (1536, 2304), (2304, 2816), (2816, 3584)
R_KCMP, R_VCMP, R_KSLC, R_VSLC, R_KWIN, R_VWIN = (3584, 3840), (3840, 4096), (4096, 4352), (4352, 4608), (4608, 4864), (4864, 5120)
R_GC, R_GM = (5120, 5138), (5138, 11282)


def mix_consts(h):
    p = np.arange(128)[:, None]
    f = np.arange(128)[None, :]
    nb = lambda valid: np.where(valid, 0.0, -BIG).astype(BF)
    c = {}
    c["ident"] = np.eye(128, dtype=np.float32).astype(BF)
    c["identf"] = np.eye(128, dtype=np.float32)
    am = np.stack([p <= f, f <= p,
                   (p <= f) & ((f - p) % 4 == 0), (f - p) % 4 == 0, (f <= p) & ((f - p) % 4 == 0),
                   (p <= f) & ((f - p) % 16 == 0), (f - p) % 16 == 0], axis=1)
    c["amask"] = nb(am)
    c["pkill"] = np.full((128, 384), 0.0 if h == 1 else -BIG, dtype=np.float32).astype(BF)
    c["wmask"] = nb(np.stack([np.tile(p <= f, (1, 3)), np.tile(f < p, (1, 3))], axis=1))
    cl = np.arange(128)[:, None, None]
    cg = cl - (0 if h == 1 else 64)
    tg = h * 1024 + np.arange(8)[None, :, None] * 128 + (np.arange(384) % 128)[None, None, :]
    c["cmask"] = nb((cg >= 0) & (cl < 127) & (16 * cg + 31 <= tg))
    cgl = np.arange(127)[:, None] - (0 if h == 1 else 64)
    j = np.arange(32)[None, :]
    c["overlap"] = ((cgl >= 0) & (16 * cgl < 64 * j + 64) & (16 * cgl + 32 > 64 * j)).astype(np.float32).astype(BF)
    kg = np.arange(2048)[None, :] - (0 if h == 1 else 1024)
    E = np.zeros((33, 2048), np.float32)
    E[:32] = np.where((kg >= 0) & (kg // 64 == np.arange(32)[:, None]), BIG, 0.0)
    E[32] = np.where(kg[0] < 0, BIG, 0.0)
    c["Eh"] = E.astype(BF)
    t = h * 1024 + np.arange(8)[None, :, None] * 128 + np.arange(128)[:, None, None]
    tb = t // 64
    jj = np.arange(32)[None, None, :]
    c["validblk"] = (jj <= tb).astype(np.float32)
    forced = (jj == 0) | (jj == tb) | (jj == tb - 1)
    c["constb"] = np.where(jj > tb, -1.0, np.where(forced, 100.0, 0.0)).astype(np.float32)
    tt = h * 1024 + np.arange(1024)[None, None, :]
    w = np.array([2, 4, 8, 16])[None, :, None]
    c["invcnt"] = np.broadcast_to(1.0 / np.minimum(tt + 1, w), (128, 4, 1024)).astype(np.float32).copy()
    return c


def prep_mix_inputs(own, prev, h):
    def full(r):
        o = own[r[0]:r[1]]
        pv = prev[r[0]:r[1]] if h == 1 else np.zeros_like(o)
        return np.concatenate([pv, o], axis=1)
    d = {}
    d["qA"] = own[R_QA[0]:R_QA[1]].reshape(12, 64, NT).transpose(1, 0, 2)
    d["kA"] = full(R_KA).reshape(12, 64, 2 * NT).transpose(1, 0, 2)
    d["vA"] = full(R_VA).T.reshape(16, 128, 12, 64).transpose(1, 0, 2, 3)
    d["xb"] = full(R_XB)[:, NT - 16:].reshape(4, 128, 16 + NT).transpose(1, 0, 2)
    d["qC3"] = own[R_QC[0]:R_QC[1]].reshape(2, 3, 128, 8, 128).transpose(2, 0, 3, 1, 4).reshape(128, 2, 8, 384)
    for nm, r in (("kcmp", R_KCMP), ("vcmp", R_VCMP), ("kslc", R_KSLC), ("kwin", R_KWIN)):
        d[nm] = full(r).reshape(2, 128, 2 * NT).transpose(1, 0, 2)
    for nm, r in (("vslc", R_VSLC), ("vwin", R_VWIN)):
        d[nm] = full(r).T.reshape(16, 128, 2, 128).transpose(1, 0, 2, 3)
    d["gc"] = own[R_GC[0]:R_GC[1]].T.reshape(8, 128, 18).transpose(1, 0, 2)
    return {k: np.ascontiguousarray(v) for k, v in d.items()}


_PROGS = {}


def _prog(name):
    if name not in _PROGS:
        _PROGS[name] = {"ffn": lambda: build_ffn_prog(False), "ffn_final": lambda: build_ffn_prog(True),
                        "proj": build_proj_prog, "mix": build_mix_prog, "out": build_out_prog}[name]()
    return _PROGS[name]


def _run(name, in_maps):
    res = run_bass_kernel_spmd(_prog(name), in_maps, core_ids=list(range(8)))
    return res.results


def kernel_unfused(x, ffn1_norm, ffn1_wi, ffn1_wo, mix_norm, w_in, pool_w, pool_scale, cmp_pe_k, cmp_w1_k, cmp_w2_k,
           cmp_pe_v, cmp_w1_v, cmp_w2_v, proj_a, proj_b, proj_c, w_out, ffn2_norm, ffn2_wi, ffn2_wo, final_norm):
    f = lambda a: np.ascontiguousarray(np.asarray(a, dtype=np.float32))
    x = f(x)
    xT = [np.ascontiguousarray(x[c // 2, (c % 2) * NT:(c % 2 + 1) * NT, :].T) for c in range(8)]
    consts = [mix_consts(h) for h in range(2)]
    depth = ffn1_norm.shape[0]
    for l in range(depth):
        r = _run("ffn", [{"xT": xT[c], "g": f(ffn1_norm[l]), "wi": f(ffn1_wi[l]), "wo": f(ffn1_wo[l])} for c in range(8)])
        xT = [r[c]["yT"] for c in range(8)]
        r = _run("proj", [{"xT": xT[c], "g": f(mix_norm[l]), "w_in": f(w_in[l])} for c in range(8)])
        cols = [r[c]["colsT"] for c in range(8)]
        wts = dict(pool_w=f(pool_w[l]), pool_scale=f(pool_scale[l]), w1_k=f(cmp_w1_k[l]), w1_v=f(cmp_w1_v[l]),
                   w2_k=f(cmp_w2_k[l]), w2_v=f(cmp_w2_v[l]), pe_k=f(cmp_pe_k[l]), pe_v=f(cmp_pe_v[l]))
        ims = []
        for c in range(8):
            h = c % 2
            d = prep_mix_inputs(cols[c], cols[c - 1] if h == 1 else None, h)
            d.update(consts[h])
            d.update(wts)
            ims.append(d)
        r = _run("mix", ims)
        oT = [r[c]["oT"] for c in range(8)]
        r = _run("out", [{"oT": oT[c], "sg": np.ascontiguousarray(cols[c][R_GM[0]:R_GM[1]]), "xT": xT[c],
                          "proj_a": f(proj_a[l]), "proj_b": f(proj_b[l]), "proj_c": f(proj_c[l]), "w_out": f(w_out[l])}
                         for c in range(8)])
        xT = [r[c]["yT"] for c in range(8)]
        if l < depth - 1:
            r = _run("ffn", [{"xT": xT[c], "g": f(ffn2_norm[l]), "wi": f(ffn2_wi[l]), "wo": f(ffn2_wo[l])} for c in range(8)])
            xT = [r[c]["yT"] for c in range(8)]
        else:
            r = _run("ffn_final", [{"xT": xT[c], "g": f(ffn2_norm[l]), "wi": f(ffn2_wi[l]), "wo": f(ffn2_wo[l]),
                                    "gf": f(final_norm)} for c in range(8)])
            xT = [r[c]["zT"] for c in range(8)]
    out = np.empty((4, 2 * NT, D), np.float32)
    for c in range(8):
        out[c // 2, (c % 2) * NT:(c % 2 + 1) * NT, :] = xT[c].T
    return out


I32 = mybir.dt.int32
CONST_INPUTS = ["ident", "identf", "amask", "pkill", "wmask", "invcnt", "cmask", "Eh", "validblk", "constb", "overlap"]
WEIGHT_SPECS = {
    "ffn1_norm": ([4, D], F32), "ffn1_wi": ([4, D, 2 * DFF], F32R), "ffn1_wo": ([4, DFF, D], F32R),
    "mix_norm": ([4, D], F32), "w_in": ([4, D, IN_COLS], F32R), "pool_w": ([4, 4, 128, 128], F32),
    "pool_scale": ([4, 512], F32), "cmp_pe_k": ([4, 32, 128], F32), "cmp_w1_k": ([4, 4096, 128], F32),
    "cmp_w2_k": ([4, 128, 128], F32), "cmp_pe_v": ([4, 32, 128], F32), "cmp_w1_v": ([4, 4096, 128], F32),
    "cmp_w2_v": ([4, 128, 128], F32), "proj_a": ([4, 256, D], F32R), "proj_b": ([4, 512, D], F32R),
    "proj_c": ([4, 768, D], F32R), "w_out": ([4, D, D], F32R), "ffn2_norm": ([4, D], F32),
    "ffn2_wi": ([4, D, 2 * DFF], F32R), "ffn2_wo": ([4, DFF, D], F32R), "final_norm": ([D], F32),
}


def build_fused(depth=4, upto=None, skip=()):
    import contextlib
    nc = bass.Bass("TRN2", target_bir_lowering=False)
    E = {k: nc.dram_tensor(k, ([depth] + shp[1:]) if len(shp) > 1 else shp, dt, kind="ExternalInput").ap()
         for k, (shp, dt) in WEIGHT_SPECS.items()}
    for k in CONST_INPUTS:
        shp, dt = MIX_INPUTS[k]
        E[k] = nc.dram_tensor(k, shp, dt, kind="ExternalInput").ap()
    E["hflag"] = nc.dram_tensor("hflag", [128, 1], F32, kind="ExternalInput").ap()
    nonce = nc.dram_tensor("nonce", [1, 16], I32, kind="ExternalInput").ap()
    xT = nc.dram_tensor("xT", [D, NT], F32R, kind="ExternalInput").ap()
    yT = nc.dram_tensor("yT", [D, NT], F32, kind="ExternalOutput").ap()
    xs = [nc.dram_tensor(f"xs{i}", [D, NT], F32R).ap() for i in range(2)]
    cols = nc.dram_tensor("cols_s", [NCH * 128, NT], BF16).ap()
    vtm = nc.dram_tensor("vtm_s", [NT, 1408], BF16).ap()
    oTs = nc.dram_tensor("oT_s", [12 * 128, NT], F32R).ap()
    shk = nc.dram_tensor("shk_s", [2 * 4 * SHK_ROWS, NT], BF16, addr_space="Shared").ap()
    shv = nc.dram_tensor("shv_s", [2 * 4 * NT, SHV_COLS], BF16, addr_space="Shared").ap()
    flags = nc.dram_tensor("flags_s", [8, 16], I32, addr_space="Shared").ap()
    S = Sched(nc)

    def par(eng):
        key = ("par", id(eng))
        if key not in S.eng_ctx:
            S.eng_ctx[key] = eng.partition_id() % 2
        return S.eng_ctx[key]

    with contextlib.ExitStack() as top:
        psum = [Tile(f"ps{i}", top.enter_context(nc.psum_tensor(f"ps{i}", [128, 512], F32))) for i in range(8)]
        phase_no = [0]

        def new_ctx(stack, norm=True):
            C = Ctx()
            pn = phase_no[0]
            phase_no[0] += 1

            def sb(name, shape, dt=F32):
                nm = f"p{pn}_{name}"
                return Tile(nm, stack.enter_context(nc.sbuf_tensor(nm, shape, dt)))
            C.sb = sb
            C.wq = WQ
            C.psum = psum
            if norm:
                C.big = sb("big", [128, KC, NT], F32R)
                C.ones = sb("ones", [128, 128], F32R)
                C.ones32 = sb("ones32", [128, 128], F32)
                S.op("dve", lambda e: e.memset(C.ones32.ap[:], 1.0), writes=[C.ones32])
                S.op("dve", lambda e: e.tensor_copy(out=C.ones.ap[:], in_=C.ones32.ap[:]), reads=[C.ones32], writes=[C.ones])
                C.rstd = sb("rstd", [128, NT])
                C.sq = [sb(f"sq{i}", [128, NT], F32R) for i in range(2)]
                C.gv = sb("gv", [128, KC])
                C.epsb = sb("epsb", [128, 1])
                S.op("dve", lambda e: e.memset(C.epsb.ap[:], EPS), writes=[C.epsb])
            return C

        def ffn_tiles(C, JP=None):
            JP = JP or JPASS
            C.wslots = [C.sb(f"wslot{i}", [128, 4096], F32R) for i in range(2)]
            C.gbuf = C.sb("gbuf", [128, JP, NT], F32R)
            C.stmp = [C.sb(f"stmp{i}", [128, 512]) for i in range(2)]
            C.xin = [C.sb(f"xin{i}", [128, NT]) for i in range(2)]
            C.xout = [C.sb(f"xout{i}", [128, NT]) for i in range(2)]

        cur = dram_chunks("xT", xT)
        bufs = [dram_chunks("xs0_", xs[0]), dram_chunks("xs1_", xs[1])]
        bi = 0
        for l in range(depth):
            if "ffn1" not in skip:
                with contextlib.ExitStack() as ph:
                    C = new_ctx(ph)
                    ffn_tiles(C)
                    phase_ffn(S, C, cur, bufs[bi], E["ffn1_norm"][l], E["ffn1_wi"][l], E["ffn1_wo"][l])
                    S.end_phase()
                cur = bufs[bi]
                bi ^= 1
            if "ffn_twice" in skip:
                with contextlib.ExitStack() as ph:
                    C = new_ctx(ph)
                    ffn_tiles(C)
                    phase_ffn(S, C, cur, bufs[bi], E["ffn1_norm"][l], E["ffn1_wi"][l], E["ffn1_wo"][l])
                    S.end_phase()
                cur = bufs[bi]
                bi ^= 1
            if upto == "ffn1":
                break
            t_shared, t_vtm, t_cols, t_prev = Tile("t_shared"), Tile("t_vtm"), Tile("t_cols"), Tile("t_prev")
            DEV = {
                "cols": cols, "vtm": vtm, "t_shared": t_shared, "t_vtm": t_vtm, "t_cols": t_cols, "t_prev": t_prev,
                "nonce": nonce,
                "shk": lambda p, l=l: shk[(p * 4 + l) * SHK_ROWS:(p * 4 + l + 1) * SHK_ROWS, :],
                "shv": lambda p, l=l: shv[(p * 4 + l) * NT:(p * 4 + l + 1) * NT, :],
                "flag": lambda p, l=l: flags[p * 4 + l:p * 4 + l + 1, :],
            }

            def poll(e, l=l):
                pv = S.par_of(e)
                for p in (0, 1):
                    with (e.If(pv == 0) if p == 0 else e.Else()):
                        with e.register(f"pr{l}_{p}") as r, e.register(f"pn{l}_{p}") as nv, e.register(f"pf{l}_{p}") as fv:
                            e.load(nv, nonce[0:1, 0:1])
                            e.reg_mov(r, 1)
                            with e.While(r):
                                e.load(fv, flags[(1 - p) * 4 + l:(1 - p) * 4 + l + 1, 0:1])
                                e.reg_sub(r, fv, nv)
            DEV["poll"] = poll
            with contextlib.ExitStack() as ph:
                C = new_ctx(ph)
                C.wslots = [C.sb(f"wslot{i}", [128, 4096], F32R) for i in range(2)]
                C.cout = [C.sb(f"cout{i}", [128, NT], BF16) for i in range(2)]
                phase_proj(S, C, cur, E["mix_norm"][l], E["w_in"][l], dram_chunks("cols", cols, NCH), DEV=(None if "nodev" in skip else DEV))
                S.end_phase()
            if "proj_twice" in skip:
                with contextlib.ExitStack() as ph:
                    C = new_ctx(ph)
                    C.wslots = [C.sb(f"wslot{i}", [128, 4096], F32R) for i in range(2)]
                    C.cout = [C.sb(f"cout{i}", [128, NT], BF16) for i in range(2)]
                    phase_proj(S, C, cur, E["mix_norm"][l], E["w_in"][l], dram_chunks("cols", cols, NCH), DEV=None)
                    S.end_phase()
            if upto == "proj":
                break
            with contextlib.ExitStack() as ph:
                C = new_ctx(ph, norm=False)
                I = {k: E[k] for k in CONST_INPUTS}
                I["hflag"] = E["hflag"]
                I.update(pool_w=E["pool_w"][l], pool_scale=E["pool_scale"][l], w1_k=E["cmp_w1_k"][l], w1_v=E["cmp_w1_v"][l],
                         w2_k=E["cmp_w2_k"][l], w2_v=E["cmp_w2_v"][l], pe_k=E["cmp_pe_k"][l], pe_v=E["cmp_pe_v"][l])
                I["_dev"] = DEV
                oT_ch = dram_chunks("oTs", oTs, 12)
                phase_mix(S, C, I, oT_ch)
                S.end_phase()
            if upto == "mix":
                break
            with contextlib.ExitStack() as ph:
                C = new_ctx(ph, norm=False)
                C.big = C.sb("big", [128, KC, NT], F32R)
                C.wslots = [C.sb(f"wslot{i}", [128, 4096], F32R) for i in range(2)]
                C.xin = [C.sb(f"xin{i}", [128, NT]) for i in range(2)]
                C.xout = [C.sb(f"xout{i}", [128, NT]) for i in range(2)]
                sg_ch = [Tile(f"sg{i}", cols[R_GM[0] + i * 128:R_GM[0] + (i + 1) * 128, :]) for i in range(48)]
                phase_out(S, C, dram_chunks("oTs", oTs, 12), sg_ch, cur, bufs[bi], E["proj_a"][l], E["proj_b"][l],
                          E["proj_c"][l], E["w_out"][l])
                S.end_phase()
            cur = bufs[bi]
            bi ^= 1
            with contextlib.ExitStack() as ph:
                C = new_ctx(ph)
                ffn_tiles(C)
                phase_ffn(S, C, cur, bufs[bi], E["ffn2_norm"][l], E["ffn2_wi"][l], E["ffn2_wo"][l])
                S.end_phase()
            cur = bufs[bi]
            bi ^= 1
        with contextlib.ExitStack() as ph:
            C = new_ctx(ph)
            phase_final_norm(S, C, cur, dram_chunks("yT", yT), E["final_norm"])
            S.end_phase()
    return nc


_FUSED = {}


def kernel_fused(depth=4, upto=None, ncores=8, skip=(), **inp):
    f = lambda a: np.ascontiguousarray(np.asarray(a, dtype=np.float32))
    x = f(inp["x"])
    if (depth, upto) not in _FUSED:
        _FUSED[(depth, upto)] = build_fused(depth, upto, skip)
    nc = _FUSED[(depth, upto)]
    shared = {k: (f(inp[k][:depth]) if len(WEIGHT_SPECS[k][0]) > 1 else f(inp[k])) for k in WEIGHT_SPECS}
    consts = [mix_consts(h) for h in range(2)]
    nonce = np.full((1, 16), np.random.randint(1, 2 ** 30), np.int32)
    ims = []
    for c in range(ncores):
        h = c % 2
        d = dict(shared)
        d.update({k: consts[h][k] for k in CONST_INPUTS})
        d["hflag"] = np.full((128, 1), float(h), np.float32)
        d["nonce"] = nonce
        d["xT"] = np.ascontiguousarray(x[c // 2, h * NT:(h + 1) * NT, :].T)
        ims.append(d)
    res = run_bass_kernel_spmd(nc, ims, core_ids=list(range(ncores)))
    out = np.zeros((4, 2 * NT, D), np.float32)
    for c in range(ncores):
        out[c // 2, (c % 2) * NT:(c % 2 + 1) * NT, :] = res.results[c]["yT"].T
    return out


def kernel(**inputs):
    return kernel_fused(depth=4, **inputs)
```

```python
import numpy as np
import concourse.bass as bass
import concourse.mybir as mybir
from concourse.bass_utils import run_bass_kernel_spmd

F32 = mybir.dt.float32
F32R = mybir.dt.float32r
BF16 = mybir.dt.bfloat16
AF = mybir.ActivationFunctionType
ALU = mybir.AluOpType

D = 2048
NT = 1024
DFF = 5632
NJ = DFF // 128
KC = D // 128
EPS = 1e-6
WQ = "pool"
JPASS = 15


class Tile:
    def __init__(self, name, ap=None):
        self.name = name
        self.ap = ap
        self.w = {}
        self.r = {}
        self.dsem = None
        self.dcount = 0

    def __getitem__(self, idx):
        return self.ap[idx]


class Par:
    def __init__(self, fn):
        self.fn = fn


class Sched:
    CE = ("pe", "act", "dve", "pool")

    def __init__(self, nc):
        self.nc = nc
        self.sem_pool = []
        self.csem = None
        self.tiles = {}
        self._reset()

    def _reset(self):
        self.ops = {e: [] for e in ("pe", "act", "dve", "pool", "sp")}
        self.cnt = {e: 0 for e in self.CE}
        self.waited = {}
        self.sems = {}
        self.dsems = []
        self.needed = {e: set() for e in self.CE}

    def dma_sem(self, tile):
        if tile.dsem is None:
            if self.sem_pool:
                tile.dsem, tile.dcount = self.sem_pool.pop()
            else:
                self.nsem = getattr(self, "nsem", 0) + 1
                tile.dsem, tile.dcount = self.nc.alloc_semaphore(name=f"dq{self.nsem}"), 0
            self.dsems.append(tile)
        return tile.dsem

    def _collect(self, cur, reads, writes):
        for t in list(reads) + list(writes):
            self.tiles[id(t)] = t
        deps = {}
        raw_self = 0
        for t in reads:
            for k, v in t.w.items():
                if k == cur:
                    raw_self = max(raw_self, v)
                elif deps.get(k, 0) < v:
                    deps[k] = v
        for t in writes:
            for k, v in t.w.items():
                if k != cur and deps.get(k, 0) < v:
                    deps[k] = v
            for k, v in t.r.items():
                if k != cur and deps.get(k, 0) < v:
                    deps[k] = v
        if raw_self and cur in ("act", "dve", "pool"):
            deps[cur] = raw_self
        waits = []
        for k, v in deps.items():
            if self.waited.get((cur, k), 0) >= v:
                continue
            self.waited[(cur, k)] = v
            waits.append((k, v))
            if k in self.CE:
                self.needed[k].add(v)
        return waits

    def op(self, eng, fn, reads=(), writes=()):
        waits = self._collect(eng, reads, writes)
        self.cnt[eng] += 1
        idx = self.cnt[eng]
        for t in reads:
            t.r[eng] = idx
        for t in writes:
            t.w[eng] = idx
        self.ops[eng].append(("op", waits, fn, idx))

    def dma(self, q, out, in_, sb, dram=None, load=True, extra_reads=(), **kw):
        sem = self.dma_sem(sb)
        key = ("d", sb.name)
        if load:
            reads, writes = ([dram] if dram is not None else []), [sb]
        else:
            reads, writes = [sb], ([dram] if dram is not None else [])
        reads = list(reads) + list(extra_reads)
        waits = self._collect(q, reads, writes)
        sb.dcount += 16
        val = sb.dcount
        for t in reads:
            t.r[key] = val
        for t in writes:
            t.w[key] = val
        self.sems[key] = sem
        if q in self.CE:
            self.cnt[q] += 0
        self.ops[q].append(("dma", waits, (out, in_, sem, kw), None))

    def par_of(self, eng):
        key = ("par", id(eng))
        if key not in self.eng_ctx:
            self.eng_ctx[key] = eng.partition_id() % 2
        return self.eng_ctx[key]

    def raw(self, eng, fn):
        self.ops[eng].append(("raw", [], fn, None))

    def end_phase(self):
        nc = self.nc
        if self.csem is None:
            self.csem = {e: nc.alloc_semaphore(name="c_" + e) for e in self.CE}
        self.emit()
        for t in self.dsems:
            self.sem_pool.append((t.dsem, t.dcount))
            t.dsem = None
            t.dcount = 0
        for t in self.tiles.values():
            t.w.clear()
            t.r.clear()
        self.tiles = {}
        self._reset()

    def emit(self):
        nc = self.nc
        for e in self.CE:
            if self.csem is not None:
                self.sems[e] = self.csem[e]
            else:
                self.sems[e] = nc.alloc_semaphore(name="c_" + e)
        sig = {}
        for e in self.CE:
            m = {}
            base = getattr(self, "sig_base", {})
            n = base.get(e, 0)
            for i in sorted(self.needed[e]):
                n += 1
                m[i] = n
            sig[e] = m
            base[e] = n
            self.sig_base = base
        engs = {"pe": nc.tensor, "act": nc.scalar, "dve": nc.vector, "pool": nc.gpsimd, "sp": nc.sync}
        with nc.Block() as block:
            self.eng_ctx = {}

            def body(ename):
                eng = engs[ename]
                for kind, waits, payload, idx in self.ops[ename]:
                    for k, v in waits:
                        if k in self.CE:
                            eng.wait_ge(self.sems[k], sig[k][v])
                        else:
                            eng.wait_ge(self.sems[k], v)
                    if kind == "raw":
                        payload(eng)
                    elif kind == "op":
                        ins = payload(eng)
                        if idx in sig[ename]:
                            ins.then_inc(self.sems[ename], 1)
                    else:
                        out, in_, sem, kw = payload
                        if isinstance(out, Par) or isinstance(in_, Par):
                            pv = self.par_of(eng)
                            for p in (0, 1):
                                o = out.fn(p) if isinstance(out, Par) else out
                                i = in_.fn(p) if isinstance(in_, Par) else in_
                                with (eng.If(pv == 0) if p == 0 else eng.Else()):
                                    eng.dma_start(out=o, in_=i, **kw).then_inc(sem, 16)
                        else:
                            eng.dma_start(out=out, in_=in_, **kw).then_inc(sem, 16)
                if ename == "sp":
                    for t in self.dsems:
                        eng.wait_ge(t.dsem, t.dcount)

            @block.tensor
            def _(e):
                body("pe")

            @block.scalar
            def _(e):
                body("act")

            @block.vector
            def _(e):
                body("dve")

            @block.gpsimd
            def _(e):
                body("pool")

            @block.sync
            def _(e):
                body("sp")


class Ctx:
    pass


def alloc_common(nc, S, stack):
    C = Ctx()

    def sb(name, shape, dt=F32):
        t = stack.enter_context(nc.sbuf_tensor(name, shape, dt))
        return Tile(name, t)

    def ps(name, shape, dt=F32):
        t = stack.enter_context(nc.psum_tensor(name, shape, dt))
        return Tile(name, t)

    C.sb = sb
    C.wq = WQ
    C.ps = ps
    C.big = sb("big", [128, KC, NT], F32R)
    C.psum = [ps(f"ps{i}", [128, 512]) for i in range(8)]
    C.ones = sb("ones", [128, 128], F32R)
    C.ones32 = sb("ones32", [128, 128], F32)
    S.op("dve", lambda e: e.memset(C.ones32.ap[:], 1.0), writes=[C.ones32])
    S.op("dve", lambda e: e.tensor_copy(out=C.ones.ap[:], in_=C.ones32.ap[:]), reads=[C.ones32], writes=[C.ones])
    C.rstd = sb("rstd", [128, NT])
    C.sq = [sb(f"sq{i}", [128, NT], F32R) for i in range(2)]
    C.gv = sb("gv", [128, KC])
    return C


def r32(ap):
    return ap if ap.dtype == F32R else ap.bitcast(F32R)


def f32(ap):
    return ap if ap.dtype == F32 else ap.bitcast(F32)


def rmsnorm_fm(S, C, xdram_chunks, gdram):
    big = C.big
    for k in range(KC):
        S.dma(C.wq, big.ap[:, k, :], r32(xdram_chunks[k].ap), sb=big, dram=xdram_chunks[k], load=True)
    S.dma("sp", C.gv.ap[:], gdram.rearrange("(c p) -> p c", p=128), sb=C.gv, load=True,
          allow_slow_non_contiguous=True)
    pa, pb = C.psum[0], C.psum[1]
    for k in range(KC):
        sq = C.sq[k % 2]
        S.op("act", lambda e, k=k, sq=sq: e.activation(out=r32(sq.ap[:]), in_=f32(big.ap[:, k, :]), func=AF.Square),
             reads=[big], writes=[sq])
        for h, p in enumerate((pa, pb)):
            S.op("pe", lambda e, k=k, sq=sq, h=h, p=p: e.matmul(
                p.ap[:], r32(C.ones.ap[:]), r32(sq.ap[:, h * 512:(h + 1) * 512]),
                start=(k == 0), stop=(k == KC - 1)), reads=[C.ones, sq], writes=[p])
    for h, p in enumerate((pa, pb)):
        S.op("act", lambda e, h=h, p=p: e.activation(
            out=C.rstd.ap[:, h * 512:(h + 1) * 512], in_=p.ap[:], func=AF.Sqrt,
            scale=1.0 / D, bias=C.epsb.ap[:]), reads=[p, C.epsb], writes=[C.rstd])
    S.op("dve", lambda e: e.reciprocal(out=C.rstd.ap[:], in_=C.rstd.ap[:]), reads=[C.rstd], writes=[C.rstd])
    for k in range(KC):
        S.op("dve", lambda e, k=k: e.scalar_tensor_tensor(
            out=big.ap[:, k, :], in0=f32(big.ap[:, k, :]), scalar=C.gv.ap[:, k:k + 1], in1=C.rstd.ap[:],
            op0=ALU.mult, op1=ALU.mult), reads=[big, C.gv, C.rstd], writes=[big])


def phase_ffn(S, C, xin_chunks, xout_chunks, gdram, wi, wo, JP=None):
    JP = JP or JPASS
    rmsnorm_fm(S, C, xin_chunks, gdram)
    big = C.big
    wslots = C.wslots
    gbuf = C.gbuf
    ws_i = [0]

    def next_slot():
        s = wslots[ws_i[0] % len(wslots)]
        ws_i[0] += 1
        return s

    wi_v = wi.rearrange("(k p) n -> p k n", p=128)
    wo_v = wo.rearrange("(j p) n -> p j n", p=128)
    npass = (NJ + JP - 1) // JP
    pbank = [0]
    for ps_ in range(npass):
        j0 = ps_ * JP
        nj = min(JP, NJ - j0)
        for jj in range(nj):
            j = j0 + jj
            slot = next_slot()
            wv = slot.ap[:, 0:2 * KC * 128].rearrange("p (a k n) -> p a k n", a=2, k=KC)
            S.dma(C.wq, wv[:, 0], wi_v[:, :, j * 128:(j + 1) * 128], sb=slot, load=True)
            S.dma(C.wq, wv[:, 1], wi_v[:, :, DFF + j * 128:DFF + (j + 1) * 128], sb=slot, load=True)
            base = (jj % 2) * 4
            P = C.psum[base:base + 4]
            for k in range(KC):
                for ab in range(2):
                    for h in range(2):
                        p = P[ab * 2 + h]
                        S.op("pe", lambda e, k=k, ab=ab, h=h, p=p, wv=wv: e.matmul(
                            p.ap[:], r32(wv[:, ab, k, :]), r32(big.ap[:, k, h * 512:(h + 1) * 512]),
                            start=(k == 0), stop=(k == KC - 1)), reads=[slot, big], writes=[p])
            for h in range(2):
                st = C.stmp[h]
                S.op("act", lambda e, h=h, st=st, P=P: e.activation(out=st.ap[:], in_=P[h].ap[:], func=AF.Silu),
                     reads=[P[h]], writes=[st])
                S.op("dve", lambda e, h=h, st=st, P=P, jj=jj: e.tensor_tensor(
                    out=r32(gbuf.ap[:, jj, h * 512:(h + 1) * 512]), in0=st.ap[:], in1=P[2 + h].ap[:], op=ALU.mult),
                    reads=[st, P[2 + h]], writes=[gbuf])
        if ps_ == 0 and getattr(C, "dbg", None) is not None:
            S.dma(C.wq, C.dbg, gbuf.ap[:], sb=gbuf, load=False)
        for fb in range(D // 256):
            slot = next_slot()
            wv = slot.ap[:, 0:nj * 256].rearrange("p (j n) -> p j n", j=nj)
            S.dma(C.wq, wv, wo_v[:, j0:j0 + nj, fb * 256:(fb + 1) * 256], sb=slot, load=True)
            base = (pbank[0] % 2) * 4
            pbank[0] += 1
            P = C.psum[base:base + 4]
            for fs in range(2):
                for jj in range(nj):
                    for h in range(2):
                        p = P[fs * 2 + h]
                        S.op("pe", lambda e, fs=fs, jj=jj, h=h, p=p, wv=wv: e.matmul(
                            p.ap[:], r32(wv[:, jj, fs * 128:(fs + 1) * 128]),
                            r32(gbuf.ap[:, jj, h * 512:(h + 1) * 512]),
                            start=(jj == 0), stop=(jj == nj - 1)), reads=[slot, gbuf], writes=[p])
            for fs in range(2):
                fc = fb * 2 + fs
                xi = C.xin[fc % 2]
                xo = C.xout[fc % 2]
                src = xin_chunks[fc] if ps_ == 0 else xout_chunks[fc]
                S.dma("sp", xi.ap[:], f32(src.ap), sb=xi, dram=src, load=True)
                for h in range(2):
                    p = P[fs * 2 + h]
                    S.op("dve", lambda e, h=h, p=p, xi=xi, xo=xo: e.scalar_tensor_tensor(
                        out=xo.ap[:, h * 512:(h + 1) * 512], in0=p.ap[:], scalar=0.5,
                        in1=xi.ap[:, h * 512:(h + 1) * 512], op0=ALU.mult, op1=ALU.add),
                        reads=[p, xi], writes=[xo])
                S.dma("sp", f32(xout_chunks[fc].ap), xo.ap[:], sb=xo, dram=xout_chunks[fc], load=False)


def phase_final_norm(S, C, xin_chunks, out_chunks, gdram):
    rmsnorm_fm(S, C, xin_chunks, gdram)
    for k in range(KC):
        S.dma("sp", out_chunks[k].ap, f32(C.big.ap[:, k, :]), sb=C.big, dram=out_chunks[k], load=False)


def dram_chunks(name, ap2d, n=KC):
    return [Tile(f"{name}{i}", ap2d[i * 128:(i + 1) * 128, :]) for i in range(n)]


def alloc_ffn(nc, S, C, JP=None):
    JP = JP or JPASS
    C.wslots = [C.sb(f"wslot{i}", [128, 4096], F32R) for i in range(2)]
    C.gbuf = C.sb("gbuf", [128, JP, NT], F32R)
    C.stmp = [C.sb(f"stmp{i}", [128, 512]) for i in range(2)]
    C.xin = [C.sb(f"xin{i}", [128, NT]) for i in range(2)]
    C.xout = [C.sb(f"xout{i}", [128, NT]) for i in range(2)]
    C.epsb = C.sb("epsb", [128, 1])
    S.op("dve", lambda e: e.memset(C.epsb.ap[:], EPS), writes=[C.epsb])


def build_ffn_prog(final_norm=False):
    import contextlib
    nc = bass.Bass("TRN2", target_bir_lowering=False)
    xT = nc.dram_tensor("xT", [D, NT], F32R, kind="ExternalInput").ap()
    g = nc.dram_tensor("g", [D], F32, kind="ExternalInput").ap()
    wi = nc.dram_tensor("wi", [D, 2 * DFF], F32R, kind="ExternalInput").ap()
    wo = nc.dram_tensor("wo", [DFF, D], F32R, kind="ExternalInput").ap()
    yT = nc.dram_tensor("yT", [D, NT], F32, kind="ExternalOutput").ap()
    if final_norm:
        gf = nc.dram_tensor("gf", [D], F32, kind="ExternalInput").ap()
        zT = nc.dram_tensor("zT", [D, NT], F32, kind="ExternalOutput").ap()
    S = Sched(nc)
    with contextlib.ExitStack() as stack:
        C = alloc_common(nc, S, stack)
        alloc_ffn(nc, S, C)
        xin = dram_chunks("xT", xT)
        xout = dram_chunks("yT", yT)
        phase_ffn(S, C, xin, xout, g, wi, wo)
        if final_norm:
            phase_final_norm(S, C, xout, dram_chunks("zT", zT), gf)
        S.emit()
    return nc


IN_COLS = 11282
NCH = (IN_COLS + 127) // 128
SIG_FROM = 40


SHK_KA, SHK_KCMP, SHK_VCMP, SHK_KSLC, SHK_KWIN, SHK_XB, SHK_ROWS = 0, 768, 1024, 1280, 1536, 1792, 2304
SHV_COLS = 1280
VTM_COLS = 1298


def shk_row_of_chunk(c):
    if 6 <= c < 12:
        return SHK_KA + (c - 6) * 128
    if 18 <= c < 22:
        return SHK_XB + (c - 18) * 128
    for lo, base in ((28, SHK_KCMP), (30, SHK_VCMP), (32, SHK_KSLC), (36, SHK_KWIN)):
        if lo <= c < lo + 2:
            return base + (c - lo) * 128
    return None


def phase_proj(S, C, xin_chunks, gdram, w_in, cols_chunks, DEV=None):
    rmsnorm_fm(S, C, xin_chunks, gdram)
    big = C.big
    w_v = w_in.rearrange("(k p) n -> p k n", p=128)
    si = 0
    for c2 in range(0, NCH, 2):
        slot = C.wslots[si % 2]
        si += 1
        c_lo = c2 * 128
        ncol = min(256, IN_COLS - c_lo)
        wv = slot.ap[:, 0:KC * ncol].rearrange("p (k n) -> p k n", k=KC)
        S.dma(C.wq, wv, w_v[:, :, c_lo:c_lo + ncol], sb=slot, load=True)
        for cs in range(2):
            c = c2 + cs
            if c >= NCH:
                break
            m = min(128, IN_COLS - c * 128)
            base = (c % 4) * 2
            P = C.psum[base:base + 2]
            for k in range(KC):
                for h in range(2):
                    S.op("pe", lambda e, k=k, h=h, P=P, wv=wv, cs=cs, m=m: e.matmul(
                        P[h].ap[0:m, :], r32(wv[:, k, cs * 128:cs * 128 + m]),
                        r32(big.ap[:, k, h * 512:(h + 1) * 512]),
                        start=(k == 0), stop=(k == KC - 1)), reads=[slot, big], writes=[P[h]])
            ob = C.cout[c % 2]
            fn = AF.Sigmoid if c >= SIG_FROM else AF.Copy
            for h in range(2):
                if c < SIG_FROM and h == 1:
                    S.op("dve", lambda e, h=h, P=P, ob=ob, m=m: e.tensor_copy(
                        out=ob.ap[0:m, h * 512:(h + 1) * 512], in_=P[h].ap[0:m, :]), reads=[P[h]], writes=[ob])
                else:
                    S.op("act", lambda e, h=h, P=P, ob=ob, m=m, fn=fn: e.activation(
                        out=ob.ap[0:m, h * 512:(h + 1) * 512], in_=P[h].ap[0:m, :], func=fn),
                        reads=[P[h]], writes=[ob])
            S.dma("sp", cols_chunks[c].ap[0:m, :], ob.ap[0:m, :], sb=ob, dram=cols_chunks[c], load=False)
            if DEV is not None and shk_row_of_chunk(c) is not None:
                r0 = shk_row_of_chunk(c)
                S.dma("sp", Par(lambda p, r0=r0: DEV["shk"](p)[r0:r0 + 128, :]), ob.ap[:, :], sb=ob, dram=DEV["t_shared"], load=False)
    if DEV is None:
        return
    vts = [C.sb(f"vt{i}", [128, 256], BF16) for i in range(2)]
    vi = 0
    blocks = [(R_VA[0], 256, 0), (R_VA[0] + 256, 256, 256), (R_VA[0] + 512, 256, 512),
              (R_VSLC[0], 256, 768), (R_VWIN[0], 256, 1024), (R_GC[0], 18, 1280)]
    for (c_lo, ncol, v0) in blocks:
        slot = C.wslots[si % 2]
        si += 1
        wv = slot.ap[:, 0:KC * ncol].rearrange("p (k n) -> p k n", k=KC)
        S.dma(C.wq, wv, w_v[:, :, c_lo:c_lo + ncol], sb=slot, load=True)
        for tt in range(NT // 128):
            p = C.psum[tt % 4]
            for k in range(KC):
                S.op("pe", lambda e, k=k, tt=tt, p=p, wv=wv, ncol=ncol: e.matmul(
                    p.ap[:, 0:ncol], r32(big.ap[:, k, tt * 128:(tt + 1) * 128]), r32(wv[:, k, :]),
                    start=(k == 0), stop=(k == KC - 1)), reads=[slot, big], writes=[p])
            vt = vts[vi % 2]
            vi += 1
            fn = AF.Sigmoid if ncol == 18 else AF.Copy
            S.op("act", lambda e, p=p, vt=vt, ncol=ncol, fn=fn: e.activation(out=vt.ap[:, 0:ncol], in_=p.ap[:, 0:ncol], func=fn),
                 reads=[p], writes=[vt])
            S.dma("sp", DEV["vtm"][tt * 128:(tt + 1) * 128, v0:v0 + ncol], vt.ap[:, 0:ncol], sb=vt, dram=DEV["t_vtm"], load=False)
            if ncol == 256:
                S.dma("sp", Par(lambda p, tt=tt, v0=v0: DEV["shv"](p)[tt * 128:(tt + 1) * 128, v0:v0 + 256]), vt.ap[:, :],
                      sb=vt, dram=DEV["t_shared"], load=False)
    nt = C.sb("noncet", [1, 16], mybir.dt.int32)
    S.dma("sp", nt.ap[:], DEV["nonce"], sb=nt)
    tf = Tile("flagdram")
    S.dma("sp", Par(lambda p: DEV["flag"](p)), nt.ap[:], sb=nt, dram=tf, load=False, extra_reads=[DEV["t_shared"]])


def build_proj_prog():
    import contextlib
    nc = bass.Bass("TRN2", target_bir_lowering=False)
    xT = nc.dram_tensor("xT", [D, NT], F32R, kind="ExternalInput").ap()
    g = nc.dram_tensor("g", [D], F32, kind="ExternalInput").ap()
    w_in = nc.dram_tensor("w_in", [D, IN_COLS], F32R, kind="ExternalInput").ap()
    colsT = nc.dram_tensor("colsT", [NCH * 128, NT], BF16, kind="ExternalOutput").ap()
    S = Sched(nc)
    with contextlib.ExitStack() as stack:
        C = alloc_common(nc, S, stack)
        C.wslots = [C.sb(f"wslot{i}", [128, 4096], F32R) for i in range(2)]
        C.cout = [C.sb(f"cout{i}", [128, NT], BF16) for i in range(2)]
        C.epsb = C.sb("epsb", [128, 1])
        S.op("dve", lambda e: e.memset(C.epsb.ap[:], EPS), writes=[C.epsb])
        phase_proj(S, C, dram_chunks("xT", xT), g, w_in, dram_chunks("colsT", colsT, NCH))
        S.emit()
    return nc


BIG = 30000.0
A_SCALE = 64 ** -0.5
C_SCALE = 128 ** -0.5
NQT = NT // 128
NKB = 16


def mm(S, out_t, out_ap, lhsT_t, lhsT_ap, rhs_t, rhs_ap, start, stop):
    rd = [t for t in (lhsT_t if isinstance(lhsT_t, (list, tuple)) else [lhsT_t])]
    rd += [t for t in (rhs_t if isinstance(rhs_t, (list, tuple)) else [rhs_t])]
    S.op("pe", lambda e: e.matmul(out_ap, lhsT_ap, rhs_ap, start=start, stop=stop), reads=rd, writes=[out_t])


def attn_block(S, C, q_t, q_ap, nq, k_t, k_ap, masks, v_list, accs, acc_w, first, last, scale):
    i = C.sbi
    C.sbi += 1
    sb = C.psum[i % 2]
    pT = C.pT[i % 2]
    nk = k_ap.shape[-1]
    mm(S, sb, sb.ap[0:nk, 0:nq], k_t, k_ap, q_t, q_ap, True, len(masks) == 0)
    for mi, (lt, la, rt, ra) in enumerate(masks):
        mm(S, sb, sb.ap[0:nk, 0:nq], lt, la, rt, ra, False, mi == len(masks) - 1)
    S.op("act", lambda e: e.activation(out=pT.ap[0:nk, 0:nq], in_=sb.ap[0:nk, 0:nq], func=AF.Exp, scale=scale),
         reads=[sb], writes=[pT])

    def pv():
        for n, (acc, (vt, va)) in enumerate(zip(accs, v_list)):
            mm(S, acc, acc.ap[:, 0:acc_w], pT, pT.ap[0:nk, n * 128:(n + 1) * 128], vt, va, first, last)
    prev = getattr(C, "pending_pv", None)
    C.pending_pv = pv
    if prev is not None:
        prev()


def attn_flush(S, C):
    prev = getattr(C, "pending_pv", None)
    C.pending_pv = None
    if prev is not None:
        prev()


def phase_mix(S, C, I, oT_chunks):
    sb = C.sb
    DEV = I.get("_dev")

    def ld_act(name, tile, out_ap, sel=None):
        if DEV is None:
            src = I[name]
            if name in ("kA", "qA"):
                src = src[:, sel, :]
            elif name == "vA":
                src = src[:, :, sel, :]
            S.dma("pool" if name == "xb" else "sp", out_ap, src, sb=tile)
            return
        cols, vtm = DEV["cols"], DEV["vtm"]
        def prv(o, apfn):
            S.dma("pool", o, Par(lambda p: apfn(DEV["shk"](1 - p), DEV["shv"](1 - p))), sb=tile, dram=DEV["t_prev"])
        if name == "qA":
            S.dma("sp", out_ap, cols[sel * 64:(sel + 1) * 64, :], sb=tile, dram=DEV["t_cols"])
        elif name == "kA":
            S.dma("sp", out_ap[:, NT:2 * NT], cols[R_KA[0] + sel * 64:R_KA[0] + (sel + 1) * 64, :], sb=tile, dram=DEV["t_cols"])
            prv(out_ap[:, 0:NT], lambda shk, shv: shk[SHK_KA + sel * 64:SHK_KA + (sel + 1) * 64, :])
        elif name == "vA":
            S.dma("sp", out_ap[:, 8:16, :], vtm[:, sel * 64:(sel + 1) * 64].rearrange("(kb p) c -> p kb c", p=128),
                  sb=tile, dram=DEV["t_vtm"])
            prv(out_ap[:, 0:8, :], lambda shk, shv: shv[:, sel * 64:(sel + 1) * 64].rearrange("(kb p) c -> p kb c", p=128))
        elif name == "xb":
            S.dma("pool", out_ap[:, :, 16:16 + NT], cols[R_XB[0]:R_XB[1], :].rearrange("(g p) t -> p g t", p=128),
                  sb=tile, dram=DEV["t_cols"])
            prv(out_ap[:, :, 0:16], lambda shk, shv: shk[SHK_XB:SHK_XB + 512, NT - 16:NT].rearrange("(g p) t -> p g t", p=128))
        elif name == "qC3":
            for g in range(2):
                for n in range(3):
                    r0 = R_QC[0] + (g * 3 + n) * 128
                    S.dma("sp", out_ap[:, g, :, n * 128:(n + 1) * 128],
                          cols[r0:r0 + 128, :].rearrange("p (qt q) -> p qt q", q=128), sb=tile, dram=DEV["t_cols"])
        elif name in ("kcmp", "vcmp", "kslc", "kwin"):
            r = {"kcmp": R_KCMP, "vcmp": R_VCMP, "kslc": R_KSLC, "kwin": R_KWIN}[name]
            so = {"kcmp": SHK_KCMP, "vcmp": SHK_VCMP, "kslc": SHK_KSLC, "kwin": SHK_KWIN}[name]
            S.dma("sp", out_ap[:, :, NT:2 * NT], cols[r[0]:r[1], :].rearrange("(g p) t -> p g t", p=128), sb=tile, dram=DEV["t_cols"])
            prv(out_ap[:, :, 0:NT], lambda shk, shv: shk[so:so + 256, :].rearrange("(g p) t -> p g t", p=128))
        elif name in ("vslc", "vwin"):
            c0 = 768 if name == "vslc" else 1024
            for g_ in range(2):
                cg = c0 + g_ * 128
                S.dma("sp", out_ap[:, 8:16, g_, :], vtm[:, cg:cg + 128].rearrange("(kb p) d -> p kb d", p=128), sb=tile, dram=DEV["t_vtm"])
                prv(out_ap[:, 0:8, g_, :], lambda shk, shv, cg=cg: shv[:, cg:cg + 128].rearrange("(kb p) d -> p kb d", p=128))
        elif name == "gc":
            S.dma("sp", out_ap, vtm[:, 1280:1298].rearrange("(qt p) c -> p qt c", p=128), sb=tile, dram=DEV["t_vtm"])
        else:
            raise KeyError(name)

    if DEV is not None:
        S.raw("pool", DEV["poll"])
    ident = sb("ident", [128, 128], BF16)
    identf = sb("identf", [128, 128], F32)
    S.dma("sp", ident.ap[:], I["ident"], sb=ident)
    S.dma("sp", identf.ap[:], I["identf"], sb=identf)
    am = sb("amask", [128, 7, 128], BF16)
    S.dma("sp", am.ap[:], I["amask"], sb=am)
    pk = sb("pkill", [128, 384], BF16)
    S.dma("sp", pk.ap[:], I["pkill"], sb=pk)
    wm = sb("wmask", [128, 2, 384], BF16)
    S.dma("sp", wm.ap[:], I["wmask"], sb=wm)
    C.pT = [sb(f"pT{i}", [128, 384], BF16) for i in range(2)]
    C.sbi = 0
    ostage = [sb(f"ostage{i}", [128, 128], F32) for i in range(2)]
    osi = [0]
    accs = C.psum[2:5]
    misc = C.psum[5]

    def emit_T(src_t, src_ap, chunk, qt):
        st = ostage[osi[0] % 2]
        osi[0] += 1
        S.op("pe", lambda e: e.transpose(misc.ap[:, 0:128], src_ap, identf.ap[:]), reads=[src_t, identf], writes=[misc])
        S.op("dve", lambda e: e.tensor_copy(out=st.ap[:], in_=misc.ap[:, 0:128]), reads=[misc], writes=[st])
        S.dma("sp", f32(oT_chunks[chunk].ap)[:, qt * 128:(qt + 1) * 128], st.ap[:], sb=st, dram=oT_chunks[chunk], load=False)

    kA3 = sb("kA3", [64, 3, 2 * NT], BF16)
    qA3 = sb("qA3", [64, 3, NT], BF16)
    vA3 = sb("vA3", [128, NKB, 3, 65], BF16)
    oA = [sb(f"oA{i}", [128, 256], F32) for i in range(NQT)]
    rl = sb("rl", [128, 8], F32)
    S.op("pool", lambda e: e.memset(vA3.ap[:, :, :, 64:65], 1.0), writes=[vA3])
    def a_mask(g, delta):
        if g == 0:
            return 0 if delta == 0 else 1
        if g == 1:
            return 2 if delta == 0 else (4 if delta == 4 else 3)
        return 5 if delta == 0 else 6
    maxd = (1, 4, 15)
    for h in range(4):
        for g in range(3):
            hh = g * 4 + h
            ld_act("kA", kA3, kA3.ap[:, g, :], hh)
            ld_act("qA", qA3, qA3.ap[:, g, :], hh)
            ld_act("vA", vA3, vA3.ap[:, :, g, 0:64], hh)
        for qt in range(NQT):
            qb = 8 + qt
            blocks = [(g, kb) for g in range(3) for kb in range(max(0, qb - maxd[g]), qb + 1)]
            acc = accs[qt % 3]
            for bi, (g, kb) in enumerate(blocks):
                masks = [(ident, ident.ap[:], am, am.ap[:, a_mask(g, qb - kb), :])]
                if kb < 8:
                    masks.append((ident, ident.ap[:], pk, pk.ap[:, 0:128]))
                attn_block(S, C, qA3, qA3.ap[:, g, qt * 128:(qt + 1) * 128], 128,
                           kA3, kA3.ap[:, g, kb * 128:(kb + 1) * 128], masks,
                           [(vA3, vA3.ap[:, kb, g, :])], [acc], 65, bi == 0, bi == len(blocks) - 1, A_SCALE)
            attn_flush(S, C)
            S.op("dve", lambda e, acc=acc: e.reciprocal(out=rl.ap[:, 0:1], in_=acc.ap[:, 64:65]), reads=[acc], writes=[rl])
            S.op("dve", lambda e, acc=acc, qt=qt, h=h: e.tensor_scalar(
                out=oA[qt].ap[:, h * 64:(h + 1) * 64], in0=acc.ap[:, 0:64], scalar1=rl.ap[:, 0:1], scalar2=None,
                op0=ALU.mult), reads=[acc, rl], writes=[oA[qt]])
    for qt in range(NQT):
        for c in range(2):
            emit_T(oA[qt], oA[qt].ap[:, c * 128:(c + 1) * 128], c, qt)

    xb = sb("xb", [128, 4, 16 + NT], F32)
    ld_act("xb", xb, xb.ap[:])
    if DEV is not None:
        hfl = sb("hflag", [128, 1], F32)
        S.dma("sp", hfl.ap[:], I["hflag"], sb=hfl)
        S.op("pool", lambda e: e.tensor_scalar(out=xb.ap[:, :, 0:16], in0=xb.ap[:, :, 0:16], scalar1=hfl.ap[:, 0:1], scalar2=None, op0=ALU.mult), reads=[xb, hfl], writes=[xb])
    icnt = sb("icnt", [128, 4, NT], F32)
    S.dma("sp", icnt.ap[:], I["invcnt"], sb=icnt)
    pw = sb("poolw", [128, 4, 128], F32)
    S.dma("sp", pw.ap[:], I["pool_w"].rearrange("g c d -> c g d"), sb=pw)
    pwb = sb("poolwb", [128, 4, 128], BF16)
    S.op("pool", lambda e: e.tensor_copy(out=pwb.ap[:], in_=pw.ap[:]), reads=[pw], writes=[pwb])
    psc = sb("poolsc", [128, 4], F32)
    S.dma("sp", psc.ap[:], I["pool_scale"].rearrange("(g p) -> p g", p=128), sb=psc, allow_slow_non_contiguous=True)
    t1 = sb("pl1", [128, 16 + NT], F32)
    t2 = sb("pl2", [128, 16 + NT], F32)
    zb = sb("zb", [128, NT], BF16)
    ybs = sb("ybs", [128, NT], F32)
    S.op("pool", lambda e: e.memset(t1.ap[:, 0:16], 0.0), writes=[t1])
    S.op("pool", lambda e: e.memset(t2.ap[:, 0:16], 0.0), writes=[t2])
    L = 16 + NT
    for gi in range(4):
        src_t, src = xb, xb.ap[:, gi, :]
        tt = [t1, t2]
        a0 = 0
        for s_ in range(gi + 1):
            sh = 1 << s_
            dst = tt[s_ % 2]
            a0 += sh
            S.op("pool", lambda e, dst=dst, src=src, sh=sh, a0=a0: e.tensor_tensor(
                out=dst.ap[:, a0:L], in0=src[:, a0:L], in1=src[:, a0 - sh:L - sh], op=ALU.add),
                reads=[src_t], writes=[dst])
            src_t, src = dst, dst.ap[:, :]
        S.op("dve", lambda e, src=src, gi=gi: e.tensor_tensor(out=ybs.ap[:], in0=src[:, 16:L], in1=icnt.ap[:, gi, :], op=ALU.mult),
             reads=[src_t, icnt], writes=[ybs])
        S.op("dve", lambda e, gi=gi: e.tensor_tensor(out=zb.ap[:], in0=ybs.ap[:], in1=xb.ap[:, gi, 16:L], op=ALU.subtract),
             reads=[ybs, xb], writes=[zb])
        for hf in range(2):
            P = C.psum[6 + hf]
            mm(S, P, P.ap[:, :], pwb, pwb.ap[:, gi, :], zb, zb.ap[:, hf * 512:(hf + 1) * 512], True, True)
            S.op("dve", lambda e, P=P, hf=hf, gi=gi: e.tensor_scalar(
                out=ybs.ap[:, hf * 512:(hf + 1) * 512], in0=P.ap[:, :], scalar1=psc.ap[:, gi:gi + 1], scalar2=None,
                op0=ALU.mult), reads=[P, psc], writes=[ybs])
        S.dma("sp", f32(oT_chunks[2 + gi].ap)[:, :], ybs.ap[:], sb=ybs, dram=oT_chunks[2 + gi], load=False)

    qC3 = sb("qC3", [128, 2, NQT, 384], BF16)
    ld_act("qC3", qC3, qC3.ap[:])
    kcmp = sb("kcmp", [128, 2, 2, 2 * NT], BF16)
    ld_act("kcmp", kcmp, kcmp.ap[:, 0])
    ld_act("vcmp", kcmp, kcmp.ap[:, 1])
    kslc = sb("kslc", [128, 2, 2 * NT], BF16)
    ld_act("kslc", kslc, kslc.ap[:])
    kwin = sb("kwin", [128, 2, 2 * NT], BF16)
    ld_act("kwin", kwin, kwin.ap[:])
    vslc = sb("vslc", [128, NKB, 2, 129], BF16)
    vwin = sb("vwin", [128, NKB, 2, 129], BF16)
    S.op("pool", lambda e: e.memset(vslc.ap[:, :, :, 128:129], 1.0), writes=[vslc])
    S.op("pool", lambda e: e.memset(vwin.ap[:, :, :, 128:129], 1.0), writes=[vwin])
    ld_act("vslc", vslc, vslc.ap[:, :, :, 0:128])
    ld_act("vwin", vwin, vwin.ap[:, :, :, 0:128])
    gc = sb("gc", [128, NQT, 18], BF16)
    ld_act("gc", gc, gc.ap[:])
    gcf = sb("gcf", [128, NQT, 18], F32)
    S.op("pool", lambda e: e.tensor_copy(out=gcf.ap[:], in_=gc.ap[:]), reads=[gc], writes=[gcf])
    cm = sb("cmask", [128, NQT, 384], BF16)
    S.dma("sp", cm.ap[:], I["cmask"], sb=cm)
    Eh = sb("Eh", [33, 2 * NT], BF16)
    S.dma("sp", Eh.ap[:], I["Eh"], sb=Eh)
    vblk = sb("vblk", [128, NQT, 32], F32)
    S.dma("sp", vblk.ap[:], I["validblk"], sb=vblk)
    cstb = sb("cstb", [128, NQT, 32], F32)
    S.dma("sp", cstb.ap[:], I["constb"], sb=cstb)
    w1b = sb("w1b", [128, 2, 32, 128], BF16)
    S.dma("pool", w1b.ap[:, 0], I["w1_k"].rearrange("(j p) n -> p j n", p=128), sb=w1b)
    S.dma("pool", w1b.ap[:, 1], I["w1_v"].rearrange("(j p) n -> p j n", p=128), sb=w1b)
    w2 = sb("w2", [128, 2, 128], F32)
    S.dma("sp", w2.ap[:, 0], I["w2_k"], sb=w2)
    S.dma("sp", w2.ap[:, 1], I["w2_v"], sb=w2)
    w2b = sb("w2b", [128, 2, 128], BF16)
    S.op("pool", lambda e: e.tensor_copy(out=w2b.ap[:], in_=w2.ap[:]), reads=[w2], writes=[w2b])
    peT = sb("peT", [128, 2, 32], F32)
    S.dma("sp", peT.ap[:, 0], I["pe_k"].rearrange("j d -> d j"), sb=peT, allow_slow_non_contiguous=True)
    S.dma("sp", peT.ap[:, 1], I["pe_v"].rearrange("j d -> d j"), sb=peT, allow_slow_non_contiguous=True)
    peTb = sb("peTb", [128, 2, 32], BF16)
    S.op("pool", lambda e: e.tensor_copy(out=peTb.ap[:], in_=peT.ap[:]), reads=[peT], writes=[peTb])
    onesb = sb("onesb", [1, 128], BF16)
    S.op("pool", lambda e: e.memset(onesb.ap[:], 1.0), writes=[onesb])
    brow = sb("brow", [1, 2, 128], BF16)
    kcT = sb("kcT", [128, 2, 128], BF16)
    vca = sb("vca", [128, 2, 161], BF16)
    S.op("pool", lambda e: e.memset(vca.ap[:, :, 128:129], 1.0), writes=[vca])
    for g in range(2):
        S.dma("sp", vca.ap[0:127, g, 129:161], I["overlap"], sb=vca)
    hx = sb("hx", [128, 128], F32)
    hy = sb("hy", [128, 128], F32)
    hz = sb("hz", [128, 128], F32)
    hg = sb("hg", [128, 128], BF16)
    for kv in range(2):
        for j in range(32):
            mm(S, misc, misc.ap[0:1, 0:128], peTb, peTb.ap[:, kv, j:j + 1], w1b, w1b.ap[:, kv, j, :], j == 0, j == 31)
        S.op("dve", lambda e, kv=kv: e.tensor_copy(out=brow.ap[:, kv, :], in_=misc.ap[0:1, 0:128]), reads=[misc], writes=[brow])
        for g in range(2):
            P = C.psum[6]
            for j in range(32):
                mm(S, P, P.ap[:, 0:127], w1b, w1b.ap[:, kv, j, :], kcmp,
                   kcmp.ap[:, kv, g, j:j + 16 * 126 + 1:16], j == 0, False)
            mm(S, P, P.ap[:, 0:127], brow, brow.ap[:, kv, :], onesb, onesb.ap[:, 0:127], False, True)
            S.op("dve", lambda e, P=P: e.tensor_copy(out=hx.ap[:, 0:127], in_=P.ap[:, 0:127]), reads=[P], writes=[hx])
            S.op("dve", lambda e: e.tensor_tensor(out=hy.ap[:, 0:127], in0=hx.ap[:, 0:127], in1=hx.ap[:, 0:127], op=ALU.mult),
                 reads=[hx], writes=[hy])
            S.op("dve", lambda e: e.tensor_scalar(out=hy.ap[:, 0:127], in0=hy.ap[:, 0:127], scalar1=0.044715, scalar2=1.0,
                                                  op0=ALU.mult, op1=ALU.add), reads=[hy], writes=[hy])
            S.op("dve", lambda e: e.tensor_tensor(out=hz.ap[:, 0:127], in0=hy.ap[:, 0:127], in1=hx.ap[:, 0:127], op=ALU.mult),
                 reads=[hy, hx], writes=[hz])
            S.op("act", lambda e: e.activation(out=hy.ap[:, 0:127], in_=hz.ap[:, 0:127], func=AF.Sigmoid, scale=1.5957691216),
                 reads=[hz], writes=[hy])
            S.op("dve", lambda e: e.tensor_tensor(out=hg.ap[:, 0:127], in0=hy.ap[:, 0:127], in1=hx.ap[:, 0:127], op=ALU.mult),
                 reads=[hy, hx], writes=[hg])
            P2 = C.psum[7]
            if kv == 0:
                mm(S, P2, P2.ap[:, 0:127], w2b, w2b.ap[:, 0, :], hg, hg.ap[:, 0:127], True, True)
                S.op("dve", lambda e, g=g, P2=P2: e.tensor_copy(out=kcT.ap[:, g, 0:127], in_=P2.ap[:, 0:127]), reads=[P2], writes=[kcT])
            else:
                mm(S, P2, P2.ap[0:127, 0:128], hg, hg.ap[:, 0:127], w2b, w2b.ap[:, 1, :], True, True)
                S.op("dve", lambda e, g=g, P2=P2: e.tensor_copy(out=vca.ap[0:127, g, 0:128], in_=P2.ap[0:127, 0:128]), reads=[P2], writes=[vca])

    imp = sb("imp", [128, 32], F32)
    sc = sb("score", [128, 32], F32)
    top8 = sb("top8", [128, 8], F32)
    selb = sb("selb", [128, 33], F32)
    S.op("pool", lambda e: e.memset(selb.ap[:, 32:33], -1.0), writes=[selb])
    selT3 = sb("selT3", [33, 384], BF16)
    sg_ = sb("sgate", [128, 16], F32)
    oC = [sb(f"oC{i}", [128, 384], F32) for i in range(2)]
    oci = 0
    for g in range(2):
        for qt in range(NQT):
            qb = 8 + qt
            q_ap = qC3.ap[:, g, qt, :]
            ocur = oC[oci % 2]
            oci += 1
            attn_block(S, C, qC3, q_ap, 384, kcT, kcT.ap[:, g, 0:127],
                       [(ident, ident.ap[0:127, 0:127], cm, cm.ap[0:127, qt, :])],
                       [(vca, vca.ap[0:127, g, :])] * 3, accs, 161, True, True, C_SCALE)
            attn_flush(S, C)
            for n in range(3):
                acc = accs[n]
                gi0 = (g * 3 + n) * 3
                S.op("dve", lambda e, acc=acc, n=n: e.tensor_scalar(
                    out=rl.ap[:, n:n + 1], in0=acc.ap[:, 128:129], scalar1=1e-30, scalar2=None, op0=ALU.max),
                    reads=[acc], writes=[rl])
                S.op("dve", lambda e, n=n: e.reciprocal(out=rl.ap[:, n:n + 1], in_=rl.ap[:, n:n + 1]), reads=[rl], writes=[rl])
                if n == 0:
                    S.op("dve", lambda e, acc=acc, n=n: e.tensor_scalar(
                        out=imp.ap[:], in0=acc.ap[:, 129:161], scalar1=rl.ap[:, n:n + 1], scalar2=None, op0=ALU.mult),
                        reads=[acc, rl], writes=[imp])
                else:
                    S.op("dve", lambda e, acc=acc, n=n: e.scalar_tensor_tensor(
                        out=imp.ap[:], in0=acc.ap[:, 129:161], scalar=rl.ap[:, n:n + 1], in1=imp.ap[:],
                        op0=ALU.mult, op1=ALU.add), reads=[acc, rl, imp], writes=[imp])
                S.op("dve", lambda e, n=n, gi0=gi0, qt=qt: e.tensor_tensor(
                    out=sg_.ap[:, n:n + 1], in0=rl.ap[:, n:n + 1], in1=gcf.ap[:, qt, gi0:gi0 + 1], op=ALU.mult),
                    reads=[rl, gcf], writes=[sg_])
                S.op("dve", lambda e, acc=acc, n=n, ocur=ocur: e.tensor_scalar(
                    out=ocur.ap[:, n * 128:(n + 1) * 128], in0=acc.ap[:, 0:128], scalar1=sg_.ap[:, n:n + 1], scalar2=None,
                    op0=ALU.mult), reads=[acc, sg_], writes=[ocur])
            S.op("dve", lambda e, qt=qt: e.tensor_tensor(out=sc.ap[:], in0=imp.ap[:], in1=vblk.ap[:, qt, :], op=ALU.mult),
                 reads=[imp, vblk], writes=[sc])
            S.op("dve", lambda e, qt=qt: e.tensor_tensor(out=sc.ap[:], in0=sc.ap[:], in1=cstb.ap[:, qt, :], op=ALU.add),
                 reads=[sc, cstb], writes=[sc])
            S.op("dve", lambda e: e.max(out=top8.ap[:], in_=sc.ap[:]), reads=[sc], writes=[top8])
            S.op("dve", lambda e: e.tensor_scalar(out=selb.ap[:, 0:32], in0=sc.ap[:], scalar1=top8.ap[:, 7:8], scalar2=-1.0,
                                                  op0=ALU.is_ge, op1=ALU.add), reads=[sc, top8], writes=[selb])
            S.op("pe", lambda e: e.transpose(misc.ap[0:33, 0:128], selb.ap[:, 0:33], identf.ap[:]),
                 reads=[selb, identf], writes=[misc])
            for n in range(3):
                S.op("dve", lambda e, n=n: e.tensor_copy(out=selT3.ap[:, n * 128:(n + 1) * 128], in_=misc.ap[0:33, 0:128]),
                     reads=[misc], writes=[selT3])
            for kb in range(qb + 1):
                masks = [(Eh, Eh.ap[:, kb * 128:(kb + 1) * 128], selT3, selT3.ap[:, :])]
                if kb == qb:
                    masks.append((ident, ident.ap[:], wm, wm.ap[:, 0, :]))
                attn_block(S, C, qC3, q_ap, 384, kslc, kslc.ap[:, g, kb * 128:(kb + 1) * 128], masks,
                           [(vslc, vslc.ap[:, kb, g, :])] * 3, accs, 129, kb == 0, kb == qb, C_SCALE)
            attn_flush(S, C)
            for n in range(3):
                acc = accs[n]
                gi1 = (g * 3 + n) * 3 + 1
                S.op("dve", lambda e, acc=acc, n=n: e.reciprocal(out=rl.ap[:, n:n + 1], in_=acc.ap[:, 128:129]), reads=[acc], writes=[rl])
                S.op("dve", lambda e, n=n, gi1=gi1, qt=qt: e.tensor_tensor(
                    out=sg_.ap[:, n:n + 1], in0=rl.ap[:, n:n + 1], in1=gcf.ap[:, qt, gi1:gi1 + 1], op=ALU.mult),
                    reads=[rl, gcf], writes=[sg_])
                S.op("dve", lambda e, acc=acc, n=n, ocur=ocur: e.scalar_tensor_tensor(
                    out=ocur.ap[:, n * 128:(n + 1) * 128], in0=acc.ap[:, 0:128], scalar=sg_.ap[:, n:n + 1],
                    in1=ocur.ap[:, n * 128:(n + 1) * 128], op0=ALU.mult, op1=ALU.add), reads=[acc, sg_, ocur], writes=[ocur])
            for kb in range(qb - 4, qb + 1):
                masks = []
                if kb == qb:
                    masks.append((ident, ident.ap[:], wm, wm.ap[:, 0, :]))
                if kb == qb - 4:
                    masks.append((ident, ident.ap[:], wm, wm.ap[:, 1, :]))
                if kb < 8:
                    masks.append((ident, ident.ap[:], pk, pk.ap[:, :]))
                attn_block(S, C, qC3, q_ap, 384, kwin, kwin.ap[:, g, kb * 128:(kb + 1) * 128], masks,
                           [(vwin, vwin.ap[:, kb, g, :])] * 3, accs, 129, kb == qb - 4, kb == qb, C_SCALE)
            attn_flush(S, C)
            for n in range(3):
                acc = accs[n]
                gi2 = (g * 3 + n) * 3 + 2
                S.op("dve", lambda e, acc=acc, n=n: e.reciprocal(out=rl.ap[:, n:n + 1], in_=acc.ap[:, 128:129]), reads=[acc], writes=[rl])
                S.op("dve", lambda e, n=n, gi2=gi2, qt=qt: e.tensor_tensor(
                    out=sg_.ap[:, n:n + 1], in0=rl.ap[:, n:n + 1], in1=gcf.ap[:, qt, gi2:gi2 + 1], op=ALU.mult),
                    reads=[rl, gcf], writes=[sg_])
                S.op("dve", lambda e, acc=acc, n=n, ocur=ocur: e.scalar_tensor_tensor(
                    out=ocur.ap[:, n * 128:(n + 1) * 128], in0=acc.ap[:, 0:128], scalar=sg_.ap[:, n:n + 1],
                    in1=ocur.ap[:, n * 128:(n + 1) * 128], op0=ALU.mult, op1=ALU.add), reads=[acc, sg_, ocur], writes=[ocur])
            for n in range(3):
                emit_T(ocur, ocur.ap[:, n * 128:(n + 1) * 128], 6 + g * 3 + n, qt)


MIX_INPUTS = {
    "ident": ([128, 128], BF16), "identf": ([128, 128], F32), "amask": ([128, 7, 128], BF16),
    "pkill": ([128, 384], BF16), "wmask": ([128, 2, 384], BF16),
    "kA": ([64, 12, 2 * NT], BF16), "qA": ([64, 12, NT], BF16), "vA": ([128, NKB, 12, 64], BF16),
    "xb": ([128, 4, 16 + NT], BF16), "invcnt": ([128, 4, NT], F32),
    "pool_w": ([4, 128, 128], F32), "pool_scale": ([512], F32),
    "qC3": ([128, 2, NQT, 384], BF16), "kcmp": ([128, 2, 2 * NT], BF16), "vcmp": ([128, 2, 2 * NT], BF16),
    "kslc": ([128, 2, 2 * NT], BF16), "kwin": ([128, 2, 2 * NT], BF16),
    "vslc": ([128, NKB, 2, 128], BF16), "vwin": ([128, NKB, 2, 128], BF16),
    "gc": ([128, NQT, 18], BF16), "cmask": ([128, NQT, 384], BF16), "Eh": ([33, 2 * NT], BF16),
    "validblk": ([128, NQT, 32], F32), "constb": ([128, NQT, 32], F32), "overlap": ([127, 32], BF16),
    "w1_k": ([4096, 128], F32), "w1_v": ([4096, 128], F32), "w2_k": ([128, 128], F32), "w2_v": ([128, 128], F32),
    "pe_k": ([32, 128], F32), "pe_v": ([32, 128], F32),
}


def build_mix_prog():
    import contextlib
    nc = bass.Bass("TRN2", target_bir_lowering=False)
    I = {k: nc.dram_tensor(k, shp, dt, kind="ExternalInput").ap() for k, (shp, dt) in MIX_INPUTS.items()}
    oT = nc.dram_tensor("oT", [12 * 128, NT], F32, kind="ExternalOutput").ap()
    S = Sched(nc)
    with contextlib.ExitStack() as stack:
        C = Ctx()

        def sb(name, shape, dt=F32):
            return Tile("s_" + name, stack.enter_context(nc.sbuf_tensor("s_" + name, shape, dt)))
        C.sb = sb
        C.psum = [Tile(f"ps{i}", stack.enter_context(nc.psum_tensor(f"ps{i}", [128, 512], F32))) for i in range(8)]
        phase_mix(S, C, I, dram_chunks("oT", oT, 12))
        S.emit()
    return nc


def phase_out(S, C, oT_chunks, sg_chunks, xin_chunks, xout_chunks, pa, pb, pc, w_out):
    sb = C.sb
    big = C.big
    ot = sb("ot", [128, 12, NT], F32R)
    for k in range(12):
        S.dma(C.wq, ot.ap[:, k, :], r32(oT_chunks[k].ap), sb=ot, dram=oT_chunks[k], load=True)
    pslots = [sb(f"pslot{i}", [128, 12, 128], F32R) for i in range(2)]
    sgt = [sb(f"sgt{i}", [128, 3, NT], BF16) for i in range(2)]
    tm = [sb(f"otm{i}", [128, 512], F32) for i in range(2)]
    pa_v = pa.rearrange("(k p) n -> p k n", p=128)
    pb_v = pb.rearrange("(k p) n -> p k n", p=128)
    pc_v = pc.rearrange("(k p) n -> p k n", p=128)
    for f in range(KC):
        slot = pslots[f % 2]
        fs = slice(f * 128, (f + 1) * 128)
        S.dma(C.wq, slot.ap[:, 0:2, :], pa_v[:, :, fs], sb=slot)
        S.dma(C.wq, slot.ap[:, 2:6, :], pb_v[:, :, fs], sb=slot)
        S.dma(C.wq, slot.ap[:, 6:12, :], pc_v[:, :, fs], sb=slot)
        sg = sgt[f % 2]
        for b in range(3):
            S.dma("sp", sg.ap[:, b, :], sg_chunks[b * 16 + f].ap, sb=sg, dram=sg_chunks[b * 16 + f])
        P = C.psum[0:6]
        for b, (k0, k1) in enumerate(((0, 2), (2, 6), (6, 12))):
            for k in range(k0, k1):
                for h in range(2):
                    p = P[b * 2 + h]
                    mm(S, p, p.ap[:], slot, slot.ap[:, k, :], ot, ot.ap[:, k, h * 512:(h + 1) * 512], k == k0, k == k1 - 1)
        for h in range(2):
            hs = slice(h * 512, (h + 1) * 512)
            t0, t1 = tm
            S.op("dve", lambda e, h=h, hs=hs, sg=sg, t0=t0, P=P: e.tensor_tensor(out=t0.ap[:], in0=P[0 + h].ap[:], in1=sg.ap[:, 0, hs], op=ALU.mult),
                 reads=[P[0 + h], sg], writes=[t0])
            S.op("dve", lambda e, h=h, hs=hs, sg=sg, t1=t1, P=P: e.tensor_tensor(out=t1.ap[:], in0=P[2 + h].ap[:], in1=sg.ap[:, 1, hs], op=ALU.mult),
                 reads=[P[2 + h], sg], writes=[t1])
            S.op("dve", lambda e, t0=t0, t1=t1: e.tensor_tensor(out=t0.ap[:], in0=t0.ap[:], in1=t1.ap[:], op=ALU.add),
                 reads=[t0, t1], writes=[t0])
            S.op("dve", lambda e, h=h, hs=hs, sg=sg, t1=t1, P=P: e.tensor_tensor(out=t1.ap[:], in0=P[4 + h].ap[:], in1=sg.ap[:, 2, hs], op=ALU.mult),
                 reads=[P[4 + h], sg], writes=[t1])
            S.op("dve", lambda e, f=f, hs=hs, t0=t0, t1=t1: e.tensor_tensor(out=big.ap[:, f, hs], in0=t0.ap[:], in1=t1.ap[:], op=ALU.add),
                 reads=[t0, t1], writes=[big])
    wo_v = w_out.rearrange("(k p) n -> p k n", p=128)
    for fb in range(D // 256):
        slot = C.wslots[fb % 2]
        wv = slot.ap[:, 0:KC * 256].rearrange("p (k n) -> p k n", k=KC)
        S.dma(C.wq, wv, wo_v[:, :, fb * 256:(fb + 1) * 256], sb=slot)
        P = C.psum[(fb % 2) * 4:(fb % 2) * 4 + 4]
        for fs in range(2):
            for k in range(KC):
                for h in range(2):
                    p = P[fs * 2 + h]
                    mm(S, p, p.ap[:], slot, wv[:, k, fs * 128:(fs + 1) * 128], big, big.ap[:, k, h * 512:(h + 1) * 512],
                       k == 0, k == KC - 1)
        for fs in range(2):
            fc = fb * 2 + fs
            xi = C.xin[fc % 2]
            xo = C.xout[fc % 2]
            S.dma("sp", xi.ap[:], f32(xin_chunks[fc].ap), sb=xi, dram=xin_chunks[fc], load=True)
            for h in range(2):
                p = P[fs * 2 + h]
                S.op("dve", lambda e, h=h, p=p, xi=xi, xo=xo: e.tensor_tensor(
                    out=xo.ap[:, h * 512:(h + 1) * 512], in0=p.ap[:], in1=xi.ap[:, h * 512:(h + 1) * 512], op=ALU.add),
                    reads=[p, xi], writes=[xo])
            S.dma("sp", f32(xout_chunks[fc].ap), xo.ap[:], sb=xo, dram=xout_chunks[fc], load=False)


def build_out_prog():
    import contextlib
    nc = bass.Bass("TRN2", target_bir_lowering=False)
    oT = nc.dram_tensor("oT", [12 * 128, NT], F32R, kind="ExternalInput").ap()
    sg = nc.dram_tensor("sg", [48 * 128, NT], BF16, kind="ExternalInput").ap()
    xT = nc.dram_tensor("xT", [D, NT], F32R, kind="ExternalInput").ap()
    pa = nc.dram_tensor("proj_a", [256, D], F32R, kind="ExternalInput").ap()
    pb = nc.dram_tensor("proj_b", [512, D], F32R, kind="ExternalInput").ap()
    pc = nc.dram_tensor("proj_c", [768, D], F32R, kind="ExternalInput").ap()
    wo = nc.dram_tensor("w_out", [D, D], F32R, kind="ExternalInput").ap()
    yT = nc.dram_tensor("yT", [D, NT], F32, kind="ExternalOutput").ap()
    S = Sched(nc)
    with contextlib.ExitStack() as stack:
        C = Ctx()

        def sb(name, shape, dt=F32):
            return Tile("s_" + name, stack.enter_context(nc.sbuf_tensor("s_" + name, shape, dt)))
        C.sb = sb
        C.wq = WQ
        C.psum = [Tile(f"ps{i}", stack.enter_context(nc.psum_tensor(f"ps{i}", [128, 512], F32))) for i in range(8)]
        C.big = sb("big", [128, KC, NT], F32R)
        C.wslots = [sb(f"wslot{i}", [128, 4096], F32R) for i in range(2)]
        C.xin = [sb(f"xin{i}", [128, NT]) for i in range(2)]
        C.xout = [sb(f"xout{i}", [128, NT]) for i in range(2)]
        phase_out(S, C, dram_chunks("oT", oT, 12), dram_chunks("sg", sg, 48), dram_chunks("xT", xT),
                  dram_chunks("yT", yT), pa, pb, pc, wo)
        S.emit()
    return nc


import ml_dtypes
BF = ml_dtypes.bfloat16
R_QA, R_KA, R_VA, R_XB, R_QC = (0, 768), (768, 1536), (1536, 2304), (2304, 2816), (2816, 3584)
R_KCMP, R_VCMP, R_KSLC, R_VSLC, R_KWIN, R_VWIN = (3584, 3840), (3840, 4096), (4096, 4352), (4352, 4608), (4608, 4864), (4864, 5120)
R_GC, R_GM = (5120, 5138), (5138, 11282)


def mix_consts(h):
    p = np.arange(128)[:, None]
    f = np.arange(128)[None, :]
    nb = lambda valid: np.where(valid, 0.0, -BIG).astype(BF)
    c = {}
    c["ident"] = np.eye(128, dtype=np.float32).astype(BF)
    c["identf"] = np.eye(128, dtype=np.float32)
    am = np.stack([p <= f, f <= p,
                   (p <= f) & ((f - p) % 4 == 0), (f - p) % 4 == 0, (f <= p) & ((f - p) % 4 == 0),
                   (p <= f) & ((f - p) % 16 == 0), (f - p) % 16 == 0], axis=1)
    c["amask"] = nb(am)
    c["pkill"] = np.full((128, 384), 0.0 if h == 1 else -BIG, dtype=np.float32).astype(BF)
    c["wmask"] = nb(np.stack([np.tile(p <= f, (1, 3)), np.tile(f < p, (1, 3))], axis=1))
    cl = np.arange(128)[:, None, None]
    cg = cl - (0 if h == 1 else 64)
    tg = h * 1024 + np.arange(8)[None, :, None] * 128 + (np.arange(384) % 128)[None, None, :]
    c["cmask"] = nb((cg >= 0) & (cl < 127) & (16 * cg + 31 <= tg))
    cgl = np.arange(127)[:, None] - (0 if h == 1 else 64)
    j = np.arange(32)[None, :]
    c["overlap"] = ((cgl >= 0) & (16 * cgl < 64 * j + 64) & (16 * cgl + 32 > 64 * j)).astype(np.float32).astype(BF)
    kg = np.arange(2048)[None, :] - (0 if h == 1 else 1024)
    E = np.zeros((33, 2048), np.float32)
    E[:32] = np.where((kg >= 0) & (kg // 64 == np.arange(32)[:, None]), BIG, 0.0)
    E[32] = np.where(kg[0] < 0, BIG, 0.0)
    c["Eh"] = E.astype(BF)
    t = h * 1024 + np.arange(8)[None, :, None] * 128 + np.arange(128)[:, None, None]
    tb = t // 64
    jj = np.arange(32)[None, None, :]
    c["validblk"] = (jj <= tb).astype(np.float32)
    forced = (jj == 0) | (jj == tb) | (jj == tb - 1)
    c["constb"] = np.where(jj > tb, -1.0, np.where(forced, 100.0, 0.0)).astype(np.float32)
    tt = h * 1024 + np.arange(1024)[None, None, :]
    w = np.array([2, 4, 8, 16])[None, :, None]
    c["invcnt"] = np.broadcast_to(1.0 / np.minimum(tt + 1, w), (128, 4, 1024)).astype(np.float32).copy()
    return c


def prep_mix_inputs(own, prev, h):
    def full(r):
        o = own[r[0]:r[1]]
        pv = prev[r[0]:r[1]] if h == 1 else np.zeros_like(o)
        return np.concatenate([pv, o], axis=1)
    d = {}
    d["qA"] = own[R_QA[0]:R_QA[1]].reshape(12, 64, NT).transpose(1, 0, 2)
    d["kA"] = full(R_KA).reshape(12, 64, 2 * NT).transpose(1, 0, 2)
    d["vA"] = full(R_VA).T.reshape(16, 128, 12, 64).transpose(1, 0, 2, 3)
    d["xb"] = full(R_XB)[:, NT - 16:].reshape(4, 128, 16 + NT).transpose(1, 0, 2)
    d["qC3"] = own[R_QC[0]:R_QC[1]].reshape(2, 3, 128, 8, 128).transpose(2, 0, 3, 1, 4).reshape(128, 2, 8, 384)
    for nm, r in (("kcmp", R_KCMP), ("vcmp", R_VCMP), ("kslc", R_KSLC), ("kwin", R_KWIN)):
        d[nm] = full(r).reshape(2, 128, 2 * NT).transpose(1, 0, 2)
    for nm, r in (("vslc", R_VSLC), ("vwin", R_VWIN)):
        d[nm] = full(r).T.reshape(16, 128, 2, 128).transpose(1, 0, 2, 3)
    d["gc"] = own[R_GC[0]:R_GC[1]].T.reshape(8, 128, 18).transpose(1, 0, 2)
    return {k: np.ascontiguousarray(v) for k, v in d.items()}


_PROGS = {}


def _prog(name):
    if name not in _PROGS:
        _PROGS[name] = {"ffn": lambda: build_ffn_prog(False), "ffn_final": lambda: build_ffn_prog(True),
                        "proj": build_proj_prog, "mix": build_mix_prog, "out": build_out_prog}[name]()
    return _PROGS[name]


def _run(name, in_maps):
    res = run_bass_kernel_spmd(_prog(name), in_maps, core_ids=list(range(8)))
    return res.results


def kernel_unfused(x, ffn1_norm, ffn1_wi, ffn1_wo, mix_norm, w_in, pool_w, pool_scale, cmp_pe_k, cmp_w1_k, cmp_w2_k,
           cmp_pe_v, cmp_w1_v, cmp_w2_v, proj_a, proj_b, proj_c, w_out, ffn2_norm, ffn2_wi, ffn2_wo, final_norm):
    f = lambda a: np.ascontiguousarray(np.asarray(a, dtype=np.float32))
    x = f(x)
    xT = [np.ascontiguousarray(x[c // 2, (c % 2) * NT:(c % 2 + 1) * NT, :].T) for c in range(8)]
    consts = [mix_consts(h) for h in range(2)]
    depth = ffn1_norm.shape[0]
    for l in range(depth):
        r = _run("ffn", [{"xT": xT[c], "g": f(ffn1_norm[l]), "wi": f(ffn1_wi[l]), "wo": f(ffn1_wo[l])} for c in range(8)])
        xT = [r[c]["yT"] for c in range(8)]
        r = _run("proj", [{"xT": xT[c], "g": f(mix_norm[l]), "w_in": f(w_in[l])} for c in range(8)])
        cols = [r[c]["colsT"] for c in range(8)]
        wts = dict(pool_w=f(pool_w[l]), pool_scale=f(pool_scale[l]), w1_k=f(cmp_w1_k[l]), w1_v=f(cmp_w1_v[l]),
                   w2_k=f(cmp_w2_k[l]), w2_v=f(cmp_w2_v[l]), pe_k=f(cmp_pe_k[l]), pe_v=f(cmp_pe_v[l]))
        ims = []
        for c in range(8):
            h = c % 2
            d = prep_mix_inputs(cols[c], cols[c - 1] if h == 1 else None, h)
            d.update(consts[h])
            d.update(wts)
            ims.append(d)
        r = _run("mix", ims)
        oT = [r[c]["oT"] for c in range(8)]
        r = _run("out", [{"oT": oT[c], "sg": np.ascontiguousarray(cols[c][R_GM[0]:R_GM[1]]), "xT": xT[c],
                          "proj_a": f(proj_a[l]), "proj_b": f(proj_b[l]), "proj_c": f(proj_c[l]), "w_out": f(w_out[l])}
                         for c in range(8)])
        xT = [r[c]["yT"] for c in range(8)]
        if l < depth - 1:
            r = _run("ffn", [{"xT": xT[c], "g": f(ffn2_norm[l]), "wi": f(ffn2_wi[l]), "wo": f(ffn2_wo[l])} for c in range(8)])
            xT = [r[c]["yT"] for c in range(8)]
        else:
            r = _run("ffn_final", [{"xT": xT[c], "g": f(ffn2_norm[l]), "wi": f(ffn2_wi[l]), "wo": f(ffn2_wo[l]),
                                    "gf": f(final_norm)} for c in range(8)])
            xT = [r[c]["zT"] for c in range(8)]
    out = np.empty((4, 2 * NT, D), np.float32)
    for c in range(8):
        out[c // 2, (c % 2) * NT:(c % 2 + 1) * NT, :] = xT[c].T
    return out


I32 = mybir.dt.int32
CONST_INPUTS = ["ident", "identf", "amask", "pkill", "wmask", "invcnt", "cmask", "Eh", "validblk", "constb", "overlap"]
WEIGHT_SPECS = {
    "ffn1_norm": ([4, D], F32), "ffn1_wi": ([4, D, 2 * DFF], F32R), "ffn1_wo": ([4, DFF, D], F32R),
    "mix_norm": ([4, D], F32), "w_in": ([4, D, IN_COLS], F32R), "pool_w": ([4, 4, 128, 128], F32),
    "pool_scale": ([4, 512], F32), "cmp_pe_k": ([4, 32, 128], F32), "cmp_w1_k": ([4, 4096, 128], F32),
    "cmp_w2_k": ([4, 128, 128], F32), "cmp_pe_v": ([4, 32, 128], F32), "cmp_w1_v": ([4, 4096, 128], F32),
    "cmp_w2_v": ([4, 128, 128], F32), "proj_a": ([4, 256, D], F32R), "proj_b": ([4, 512, D], F32R),
    "proj_c": ([4, 768, D], F32R), "w_out": ([4, D, D], F32R), "ffn2_norm": ([4, D], F32),
    "ffn2_wi": ([4, D, 2 * DFF], F32R), "ffn2_wo": ([4, DFF, D], F32R), "final_norm": ([D], F32),
}


def build_fused(depth=4, upto=None, skip=()):
    import contextlib
    nc = bass.Bass("TRN2", target_bir_lowering=False)
    E = {k: nc.dram_tensor(k, ([depth] + shp[1:]) if len(shp) > 1 else shp, dt, kind="ExternalInput").ap()
         for k, (shp, dt) in WEIGHT_SPECS.items()}
    for k in CONST_INPUTS:
        shp, dt = MIX_INPUTS[k]
        E[k] = nc.dram_tensor(k, shp, dt, kind="ExternalInput").ap()
    E["hflag"] = nc.dram_tensor("hflag", [128, 1], F32, kind="ExternalInput").ap()
    nonce = nc.dram_tensor("nonce", [1, 16], I32, kind="ExternalInput").ap()
    xT = nc.dram_tensor("xT", [D, NT], F32R, kind="ExternalInput").ap()
    yT = nc.dram_tensor("yT", [D, NT], F32, kind="ExternalOutput").ap()
    xs = [nc.dram_tensor(f"xs{i}", [D, NT], F32R).ap() for i in range(2)]
    cols = nc.dram_tensor("cols_s", [NCH * 128, NT], BF16).ap()
    vtm = nc.dram_tensor("vtm_s", [NT, 1408], BF16).ap()
    oTs = nc.dram_tensor("oT_s", [12 * 128, NT], F32R).ap()
    shk = nc.dram_tensor("shk_s", [2 * 4 * SHK_ROWS, NT], BF16, addr_space="Shared").ap()
    shv = nc.dram_tensor("shv_s", [2 * 4 * NT, SHV_COLS], BF16, addr_space="Shared").ap()
    flags = nc.dram_tensor("flags_s", [8, 16], I32, addr_space="Shared").ap()
    S = Sched(nc)

    def par(eng):
        key = ("par", id(eng))
        if key not in S.eng_ctx:
            S.eng_ctx[key] = eng.partition_id() % 2
        return S.eng_ctx[key]

    with contextlib.ExitStack() as top:
        psum = [Tile(f"ps{i}", top.enter_context(nc.psum_tensor(f"ps{i}", [128, 512], F32))) for i in range(8)]
        phase_no = [0]

        def new_ctx(stack, norm=True):
            C = Ctx()
            pn = phase_no[0]
            phase_no[0] += 1

            def sb(name, shape, dt=F32):
                nm = f"p{pn}_{name}"
                return Tile(nm, stack.enter_context(nc.sbuf_tensor(nm, shape, dt)))
            C.sb = sb
            C.wq = WQ
            C.psum = psum
            if norm:
                C.big = sb("big", [128, KC, NT], F32R)
                C.ones = sb("ones", [128, 128], F32R)
                C.ones32 = sb("ones32", [128, 128], F32)
                S.op("dve", lambda e: e.memset(C.ones32.ap[:], 1.0), writes=[C.ones32])
                S.op("dve", lambda e: e.tensor_copy(out=C.ones.ap[:], in_=C.ones32.ap[:]), reads=[C.ones32], writes=[C.ones])
                C.rstd = sb("rstd", [128, NT])
                C.sq = [sb(f"sq{i}", [128, NT], F32R) for i in range(2)]
                C.gv = sb("gv", [128, KC])
                C.epsb = sb("epsb", [128, 1])
                S.op("dve", lambda e: e.memset(C.epsb.ap[:], EPS), writes=[C.epsb])
            return C

        def ffn_tiles(C, JP=None):
            JP = JP or JPASS
            C.wslots = [C.sb(f"wslot{i}", [128, 4096], F32R) for i in range(2)]
            C.gbuf = C.sb("gbuf", [128, JP, NT], F32R)
            C.stmp = [C.sb(f"stmp{i}", [128, 512]) for i in range(2)]
            C.xin = [C.sb(f"xin{i}", [128, NT]) for i in range(2)]
            C.xout = [C.sb(f"xout{i}", [128, NT]) for i in range(2)]

        cur = dram_chunks("xT", xT)
        bufs = [dram_chunks("xs0_", xs[0]), dram_chunks("xs1_", xs[1])]
        bi = 0
        for l in range(depth):
            if "ffn1" not in skip:
                with contextlib.ExitStack() as ph:
                    C = new_ctx(ph)
                    ffn_tiles(C)
                    phase_ffn(S, C, cur, bufs[bi], E["ffn1_norm"][l], E["ffn1_wi"][l], E["ffn1_wo"][l])
                    S.end_phase()
                cur = bufs[bi]
                bi ^= 1
            if "ffn_twice" in skip:
                with contextlib.ExitStack() as ph:
                    C = new_ctx(ph)
                    ffn_tiles(C)
                    phase_ffn(S, C, cur, bufs[bi], E["ffn1_norm"][l], E["ffn1_wi"][l], E["ffn1_wo"][l])
                    S.end_phase()
                cur = bufs[bi]
                bi ^= 1
            if upto == "ffn1":
                break
            t_shared, t_vtm, t_cols, t_prev = Tile("t_shared"), Tile("t_vtm"), Tile("t_cols"), Tile("t_prev")
            DEV = {
                "cols": cols, "vtm": vtm, "t_shared": t_shared, "t_vtm": t_vtm, "t_cols": t_cols, "t_prev": t_prev,
                "nonce": nonce,
                "shk": lambda p, l=l: shk[(p * 4 + l) * SHK_ROWS:(p * 4 + l + 1) * SHK_ROWS, :],
                "shv": lambda p, l=l: shv[(p * 4 + l) * NT:(p * 4 + l + 1) * NT, :],
                "flag": lambda p, l=l: flags[p * 4 + l:p * 4 + l + 1, :],
            }

            def poll(e, l=l):
                pv = S.par_of(e)
                for p in (0, 1):
                    with (e.If(pv == 0) if p == 0 else e.Else()):
                        with e.register(f"pr{l}_{p}") as r, e.register(f"pn{l}_{p}") as nv, e.register(f"pf{l}_{p}") as fv:
                            e.load(nv, nonce[0:1, 0:1])
                            e.reg_mov(r, 1)
                            with e.While(r):
                                e.load(fv, flags[(1 - p) * 4 + l:(1 - p) * 4 + l + 1, 0:1])
                                e.reg_sub(r, fv, nv)
            DEV["poll"] = poll
            with contextlib.ExitStack() as ph:
                C = new_ctx(ph)
                C.wslots = [C.sb(f"wslot{i}", [128, 4096], F32R) for i in range(2)]
                C.cout = [C.sb(f"cout{i}", [128, NT], BF16) for i in range(2)]
                phase_proj(S, C, cur, E["mix_norm"][l], E["w_in"][l], dram_chunks("cols", cols, NCH), DEV=(None if "nodev" in skip else DEV))
                S.end_phase()
            if "proj_twice" in skip:
                with contextlib.ExitStack() as ph:
                    C = new_ctx(ph)
                    C.wslots = [C.sb(f"wslot{i}", [128, 4096], F32R) for i in range(2)]
                    C.cout = [C.sb(f"cout{i}", [128, NT], BF16) for i in range(2)]
                    phase_proj(S, C, cur, E["mix_norm"][l], E["w_in"][l], dram_chunks("cols", cols, NCH), DEV=None)
                    S.end_phase()
            if upto == "proj":
                break
            with contextlib.ExitStack() as ph:
                C = new_ctx(ph, norm=False)
                I = {k: E[k] for k in CONST_INPUTS}
                I["hflag"] = E["hflag"]
                I.update(pool_w=E["pool_w"][l], pool_scale=E["pool_scale"][l], w1_k=E["cmp_w1_k"][l], w1_v=E["cmp_w1_v"][l],
                         w2_k=E["cmp_w2_k"][l], w2_v=E["cmp_w2_v"][l], pe_k=E["cmp_pe_k"][l], pe_v=E["cmp_pe_v"][l])
                I["_dev"] = DEV
                oT_ch = dram_chunks("oTs", oTs, 12)
                phase_mix(S, C, I, oT_ch)
                S.end_phase()
            if upto == "mix":
                break
            with contextlib.ExitStack() as ph:
                C = new_ctx(ph, norm=False)
                C.big = C.sb("big", [128, KC, NT], F32R)
                C.wslots = [C.sb(f"wslot{i}", [128, 4096], F32R) for i in range(2)]
                C.xin = [C.sb(f"xin{i}", [128, NT]) for i in range(2)]
                C.xout = [C.sb(f"xout{i}", [128, NT]) for i in range(2)]
                sg_ch = [Tile(f"sg{i}", cols[R_GM[0] + i * 128:R_GM[0] + (i + 1) * 128, :]) for i in range(48)]
                phase_out(S, C, dram_chunks("oTs", oTs, 12), sg_ch, cur, bufs[bi], E["proj_a"][l], E["proj_b"][l],
                          E["proj_c"][l], E["w_out"][l])
                S.end_phase()
            cur = bufs[bi]
            bi ^= 1
            with contextlib.ExitStack() as ph:
                C = new_ctx(ph)
                ffn_tiles(C)
                phase_ffn(S, C, cur, bufs[bi], E["ffn2_norm"][l], E["ffn2_wi"][l], E["ffn2_wo"][l])
                S.end_phase()
            cur = bufs[bi]
            bi ^= 1
        with contextlib.ExitStack() as ph:
            C = new_ctx(ph)
            phase_final_norm(S, C, cur, dram_chunks("yT", yT), E["final_norm"])
            S.end_phase()
    return nc


_FUSED = {}


def kernel_fused(depth=4, upto=None, ncores=8, skip=(), **inp):
    f = lambda a: np.ascontiguousarray(np.asarray(a, dtype=np.float32))
    x = f(inp["x"])
    if (depth, upto) not in _FUSED:
        _FUSED[(depth, upto)] = build_fused(depth, upto, skip)
    nc = _FUSED[(depth, upto)]
    shared = {k: (f(inp[k][:depth]) if len(WEIGHT_SPECS[k][0]) > 1 else f(inp[k])) for k in WEIGHT_SPECS}
    consts = [mix_consts(h) for h in range(2)]
    nonce = np.full((1, 16), np.random.randint(1, 2 ** 30), np.int32)
    ims = []
    for c in range(ncores):
        h = c % 2
        d = dict(shared)
        d.update({k: consts[h][k] for k in CONST_INPUTS})
        d["hflag"] = np.full((128, 1), float(h), np.float32)
        d["nonce"] = nonce
        d["xT"] = np.ascontiguousarray(x[c // 2, h * NT:(h + 1) * NT, :].T)
        ims.append(d)
    res = run_bass_kernel_spmd(nc, ims, core_ids=list(range(ncores)))
    out = np.zeros((4, 2 * NT, D), np.float32)
    for c in range(ncores):
        out[c // 2, (c % 2) * NT:(c % 2 + 1) * NT, :] = res.results[c]["yT"].T
    return out


def kernel(**inputs):
    return kernel_fused(depth=4, **inputs)
```

```python
import numpy as np
import concourse.bass as bass
import concourse.mybir as mybir
from concourse.bass_utils import run_bass_kernel_spmd

F32 = mybir.dt.float32
F32R = mybir.dt.float32r
BF16 = mybir.dt.bfloat16
AF = mybir.ActivationFunctionType
ALU = mybir.AluOpType

D = 2048
NT = 1024
DFF = 5632
NJ = DFF // 128
KC = D // 128
EPS = 1e-6
WQ = "pool"
JPASS = 15
WQ2 = "sp"


class Tile:
    def __init__(self, name, ap=None):
        self.name = name
        self.ap = ap
        self.w = {}
        self.r = {}
        self.dsem = None
        self.dcount = 0

    def __getitem__(self, idx):
        return self.ap[idx]


class Par:
    def __init__(self, fn):
        self.fn = fn


class Sched:
    CE = ("pe", "act", "dve", "pool")

    def __init__(self, nc):
        self.nc = nc
        self.sem_pool = []
        self.csem = None
        self.tiles = {}
        self._reset()

    def _reset(self):
        self.ops = {e: [] for e in ("pe", "act", "dve", "pool", "sp")}
        self.cnt = {e: 0 for e in self.CE}
        self.waited = {}
        self.sems = {}
        self.dsems = []
        self.needed = {e: set() for e in self.CE}

    def dma_sem(self, tile):
        if tile.dsem is None:
            if self.sem_pool:
                tile.dsem, tile.dcount = self.sem_pool.pop()
            else:
                self.nsem = getattr(self, "nsem", 0) + 1
                tile.dsem, tile.dcount = self.nc.alloc_semaphore(name=f"dq{self.nsem}"), 0
            self.dsems.append(tile)
        return tile.dsem

    def _collect(self, cur, reads, writes):
        for t in list(reads) + list(writes):
            self.tiles[id(t)] = t
        deps = {}
        raw_self = 0
        for t in reads:
            for k, v in t.w.items():
                if k == cur:
                    raw_self = max(raw_self, v)
                elif deps.get(k, 0) < v:
                    deps[k] = v
        for t in writes:
            for k, v in t.w.items():
                if k != cur and deps.get(k, 0) < v:
                    deps[k] = v
            for k, v in t.r.items():
                if k != cur and deps.get(k, 0) < v:
                    deps[k] = v
        if raw_self and cur in ("act", "dve", "pool"):
            deps[cur] = raw_self
        waits = []
        for k, v in deps.items():
            if self.waited.get((cur, k), 0) >= v:
                continue
            self.waited[(cur, k)] = v
            waits.append((k, v))
            if k in self.CE:
                self.needed[k].add(v)
        return waits

    def op(self, eng, fn, reads=(), writes=()):
        waits = self._collect(eng, reads, writes)
        self.cnt[eng] += 1
        idx = self.cnt[eng]
        for t in reads:
            t.r[eng] = idx
        for t in writes:
            t.w[eng] = idx
        self.ops[eng].append(("op", waits, fn, idx))

    def dma(self, q, out, in_, sb, dram=None, load=True, extra_reads=(), **kw):
        sem = self.dma_sem(sb)
        key = ("d", sb.name)
        if load:
            reads, writes = ([dram] if dram is not None else []), [sb]
        else:
            reads, writes = [sb], ([dram] if dram is not None else [])
        reads = list(reads) + list(extra_reads)
        waits = self._collect(q, reads, writes)
        sb.dcount += 16
        val = sb.dcount
        for t in reads:
            t.r[key] = val
        for t in writes:
            t.w[key] = val
        self.sems[key] = sem
        if q in self.CE:
            self.cnt[q] += 0
        self.ops[q].append(("dma", waits, (out, in_, sem, kw), None))

    def par_of(self, eng):
        key = ("par", id(eng))
        if key not in self.eng_ctx:
            self.eng_ctx[key] = eng.partition_id() % 2
        return self.eng_ctx[key]

    def raw(self, eng, fn):
        self.ops[eng].append(("raw", [], fn, None))

    def end_phase(self):
        nc = self.nc
        if self.csem is None:
            self.csem = {e: nc.alloc_semaphore(name="c_" + e) for e in self.CE}
        self.emit()
        for t in self.dsems:
            self.sem_pool.append((t.dsem, t.dcount))
            t.dsem = None
            t.dcount = 0
        for t in self.tiles.values():
            t.w.clear()
            t.r.clear()
        self.tiles = {}
        self._reset()

    def emit(self):
        nc = self.nc
        for e in self.CE:
            if self.csem is not None:
                self.sems[e] = self.csem[e]
            else:
                self.sems[e] = nc.alloc_semaphore(name="c_" + e)
        sig = {}
        for e in self.CE:
            m = {}
            base = getattr(self, "sig_base", {})
            n = base.get(e, 0)
            for i in sorted(self.needed[e]):
                n += 1
                m[i] = n
            sig[e] = m
            base[e] = n
            self.sig_base = base
        engs = {"pe": nc.tensor, "act": nc.scalar, "dve": nc.vector, "pool": nc.gpsimd, "sp": nc.sync}
        with nc.Block() as block:
            self.eng_ctx = {}

            def body(ename):
                eng = engs[ename]
                for kind, waits, payload, idx in self.ops[ename]:
                    for k, v in waits:
                        if k in self.CE:
                            eng.wait_ge(self.sems[k], sig[k][v])
                        else:
                            eng.wait_ge(self.sems[k], v)
                    if kind == "raw":
                        payload(eng)
                    elif kind == "op":
                        ins = payload(eng)
                        if idx in sig[ename]:
                            ins.then_inc(self.sems[ename], 1)
                    else:
                        out, in_, sem, kw = payload
                        if isinstance(out, Par) or isinstance(in_, Par):
                            pv = self.par_of(eng)
                            for p in (0, 1):
                                o = out.fn(p) if isinstance(out, Par) else out
                                i = in_.fn(p) if isinstance(in_, Par) else in_
                                with (eng.If(pv == 0) if p == 0 else eng.Else()):
                                    eng.dma_start(out=o, in_=i, **kw).then_inc(sem, 16)
                        else:
                            eng.dma_start(out=out, in_=in_, **kw).then_inc(sem, 16)
                if ename == "sp":
                    for t in self.dsems:
                        eng.wait_ge(t.dsem, t.dcount)

            @block.tensor
            def _(e):
                body("pe")

            @block.scalar
            def _(e):
                body("act")

            @block.vector
            def _(e):
                body("dve")

            @block.gpsimd
            def _(e):
                body("pool")

            @block.sync
            def _(e):
                body("sp")


class Ctx:
    pass


def alloc_common(nc, S, stack):
    C = Ctx()

    def sb(name, shape, dt=F32):
        t = stack.enter_context(nc.sbuf_tensor(name, shape, dt))
        return Tile(name, t)

    def ps(name, shape, dt=F32):
        t = stack.enter_context(nc.psum_tensor(name, shape, dt))
        return Tile(name, t)

    C.sb = sb
    C.wq = WQ
    C.ps = ps
    C.big = sb("big", [128, KC, NT], F32R)
    C.psum = [ps(f"ps{i}", [128, 512]) for i in range(8)]
    C.ones = sb("ones", [128, 128], F32R)
    C.ones32 = sb("ones32", [128, 128], F32)
    S.op("dve", lambda e: e.memset(C.ones32.ap[:], 1.0), writes=[C.ones32])
    S.op("dve", lambda e: e.tensor_copy(out=C.ones.ap[:], in_=C.ones32.ap[:]), reads=[C.ones32], writes=[C.ones])
    C.rstd = sb("rstd", [128, NT])
    C.sq = [sb(f"sq{i}", [128, NT], F32R) for i in range(2)]
    C.gv = sb("gv", [128, KC])
    return C


def r32(ap):
    return ap if ap.dtype == F32R else ap.bitcast(F32R)


def f32(ap):
    return ap if ap.dtype == F32 else ap.bitcast(F32)


def rmsnorm_fm(S, C, xdram_chunks, gdram):
    big = C.big
    for k in range(KC):
        S.dma(C.wq, big.ap[:, k, :], r32(xdram_chunks[k].ap), sb=big, dram=xdram_chunks[k], load=True)
    S.dma("sp", C.gv.ap[:], gdram.rearrange("(c p) -> p c", p=128), sb=C.gv, load=True,
          allow_slow_non_contiguous=True)
    pa, pb = C.psum[0], C.psum[1]
    for k in range(KC):
        sq = C.sq[k % 2]
        S.op("act", lambda e, k=k, sq=sq: e.activation(out=r32(sq.ap[:]), in_=f32(big.ap[:, k, :]), func=AF.Square),
             reads=[big], writes=[sq])
        for h, p in enumerate((pa, pb)):
            S.op("pe", lambda e, k=k, sq=sq, h=h, p=p: e.matmul(
                p.ap[:], r32(C.ones.ap[:]), r32(sq.ap[:, h * 512:(h + 1) * 512]),
                start=(k == 0), stop=(k == KC - 1)), reads=[C.ones, sq], writes=[p])
    for h, p in enumerate((pa, pb)):
        S.op("act", lambda e, h=h, p=p: e.activation(
            out=C.rstd.ap[:, h * 512:(h + 1) * 512], in_=p.ap[:], func=AF.Sqrt,
            scale=1.0 / D, bias=C.epsb.ap[:]), reads=[p, C.epsb], writes=[C.rstd])
    S.op("dve", lambda e: e.reciprocal(out=C.rstd.ap[:], in_=C.rstd.ap[:]), reads=[C.rstd], writes=[C.rstd])
    for k in range(KC):
        S.op("dve", lambda e, k=k: e.scalar_tensor_tensor(
            out=big.ap[:, k, :], in0=f32(big.ap[:, k, :]), scalar=C.gv.ap[:, k:k + 1], in1=C.rstd.ap[:],
            op0=ALU.mult, op1=ALU.mult), reads=[big, C.gv, C.rstd], writes=[big])


def phase_ffn(S, C, xin_chunks, xout_chunks, gdram, wi, wo, JP=None):
    JP = JP or JPASS
    rmsnorm_fm(S, C, xin_chunks, gdram)
    big = C.big
    wslots = C.wslots
    gbuf = C.gbuf
    ws_i = [0]

    def next_slot():
        s = wslots[ws_i[0] % len(wslots)]
        ws_i[0] += 1
        return s

    wi_v = wi.rearrange("(k p) n -> p k n", p=128)
    wo_v = wo.rearrange("(j p) n -> p j n", p=128)
    npass = (NJ + JP - 1) // JP
    pbank = [0]
    for ps_ in range(npass):
        j0 = ps_ * JP
        nj = min(JP, NJ - j0)
        for jj in range(nj):
            j = j0 + jj
            slot = next_slot()
            wv = slot.ap[:, 0:2 * KC * 128].rearrange("p (a k n) -> p a k n", a=2, k=KC)
            S.dma(C.wq, wv[:, 0], wi_v[:, :, j * 128:(j + 1) * 128], sb=slot, load=True)
            S.dma(C.wq, wv[:, 1], wi_v[:, :, DFF + j * 128:DFF + (j + 1) * 128], sb=slot, load=True)
            base = (jj % 2) * 4
            P = C.psum[base:base + 4]
            for k in range(KC):
                for ab in range(2):
                    for h in range(2):
                        p = P[ab * 2 + h]
                        S.op("pe", lambda e, k=k, ab=ab, h=h, p=p, wv=wv: e.matmul(
                            p.ap[:], r32(wv[:, ab, k, :]), r32(big.ap[:, k, h * 512:(h + 1) * 512]),
                            start=(k == 0), stop=(k == KC - 1)), reads=[slot, big], writes=[p])
            for h in range(2):
                st = C.stmp[h]
                S.op("act", lambda e, h=h, st=st, P=P: e.activation(out=st.ap[:], in_=P[h].ap[:], func=AF.Silu),
                     reads=[P[h]], writes=[st])
                S.op("dve", lambda e, h=h, st=st, P=P, jj=jj: e.tensor_tensor(
                    out=r32(gbuf.ap[:, jj, h * 512:(h + 1) * 512]), in0=st.ap[:], in1=P[2 + h].ap[:], op=ALU.mult),
                    reads=[st, P[2 + h]], writes=[gbuf])
        if ps_ == 0 and getattr(C, "dbg", None) is not None:
            S.dma(C.wq, C.dbg, gbuf.ap[:], sb=gbuf, load=False)
        for fb in range(D // 256):
            slot = next_slot()
            wv = slot.ap[:, 0:nj * 256].rearrange("p (j n) -> p j n", j=nj)
            S.dma(C.wq, wv, wo_v[:, j0:j0 + nj, fb * 256:(fb + 1) * 256], sb=slot, load=True)
            base = (pbank[0] % 2) * 4
            pbank[0] += 1
            P = C.psum[base:base + 4]
            for fs in range(2):
                for jj in range(nj):
                    for h in range(2):
                        p = P[fs * 2 + h]
                        S.op("pe", lambda e, fs=fs, jj=jj, h=h, p=p, wv=wv: e.matmul(
                            p.ap[:], r32(wv[:, jj, fs * 128:(fs + 1) * 128]),
                            r32(gbuf.ap[:, jj, h * 512:(h + 1) * 512]),
                            start=(jj == 0), stop=(jj == nj - 1)), reads=[slot, gbuf], writes=[p])
            for fs in range(2):
                fc = fb * 2 + fs
                xi = C.xin[fc % 2]
                xo = C.xout[fc % 2]
                src = xin_chunks[fc] if ps_ == 0 else xout_chunks[fc]
                S.dma("sp", xi.ap[:], f32(src.ap), sb=xi, dram=src, load=True)
                for h in range(2):
                    p = P[fs * 2 + h]
                    S.op("dve", lambda e, h=h, p=p, xi=xi, xo=xo: e.scalar_tensor_tensor(
                        out=xo.ap[:, h * 512:(h + 1) * 512], in0=p.ap[:], scalar=0.5,
                        in1=xi.ap[:, h * 512:(h + 1) * 512], op0=ALU.mult, op1=ALU.add),
                        reads=[p, xi], writes=[xo])
                S.dma("sp", f32(xout_chunks[fc].ap), xo.ap[:], sb=xo, dram=xout_chunks[fc], load=False)


def phase_final_norm(S, C, xin_chunks, out_chunks, gdram):
    rmsnorm_fm(S, C, xin_chunks, gdram)
    for k in range(KC):
        S.dma("sp", out_chunks[k].ap, f32(C.big.ap[:, k, :]), sb=C.big, dram=out_chunks[k], load=False)


def dram_chunks(name, ap2d, n=KC):
    return [Tile(f"{name}{i}", ap2d[i * 128:(i + 1) * 128, :]) for i in range(n)]


def alloc_ffn(nc, S, C, JP=None):
    JP = JP or JPASS
    C.wslots = [C.sb(f"wslot{i}", [128, 4096], F32R) for i in range(2)]
    C.gbuf = C.sb("gbuf", [128, JP, NT], F32R)
    C.stmp = [C.sb(f"stmp{i}", [128, 512]) for i in range(2)]
    C.xin = [C.sb(f"xin{i}", [128, NT]) for i in range(2)]
    C.xout = [C.sb(f"xout{i}", [128, NT]) for i in range(2)]
    C.epsb = C.sb("epsb", [128, 1])
    S.op("dve", lambda e: e.memset(C.epsb.ap[:], EPS), writes=[C.epsb])


def build_ffn_prog(final_norm=False):
    import contextlib
    nc = bass.Bass("TRN2", target_bir_lowering=False)
    xT = nc.dram_tensor("xT", [D, NT], F32R, kind="ExternalInput").ap()
    g = nc.dram_tensor("g", [D], F32, kind="ExternalInput").ap()
    wi = nc.dram_tensor("wi", [D, 2 * DFF], F32R, kind="ExternalInput").ap()
    wo = nc.dram_tensor("wo", [DFF, D], F32R, kind="ExternalInput").ap()
    yT = nc.dram_tensor("yT", [D, NT], F32, kind="ExternalOutput").ap()
    if final_norm:
        gf = nc.dram_tensor("gf", [D], F32, kind="ExternalInput").ap()
        zT = nc.dram_tensor("zT", [D, NT], F32, kind="ExternalOutput").ap()
    S = Sched(nc)
    with contextlib.ExitStack() as stack:
        C = alloc_common(nc, S, stack)
        alloc_ffn(nc, S, C)
        xin = dram_chunks("xT", xT)
        xout = dram_chunks("yT", yT)
        phase_ffn(S, C, xin, xout, g, wi, wo)
        if final_norm:
            phase_final_norm(S, C, xout, dram_chunks("zT", zT), gf)
        S.emit()
    return nc


IN_COLS = 11282
NCH = (IN_COLS + 127) // 128
SIG_FROM = 40


SHK_KA, SHK_KCMP, SHK_VCMP, SHK_KSLC, SHK_KWIN, SHK_XB, SHK_ROWS = 0, 768, 1024, 1280, 1536, 1792, 2304
SHV_COLS = 1280
VTM_COLS = 1298


def shk_row_of_chunk(c):
    if 6 <= c < 12:
        return SHK_KA + (c - 6) * 128
    if 18 <= c < 22:
        return SHK_XB + (c - 18) * 128
    for lo, base in ((28, SHK_KCMP), (30, SHK_VCMP), (32, SHK_KSLC), (36, SHK_KWIN)):
        if lo <= c < lo + 2:
            return base + (c - lo) * 128
    return None


def phase_proj(S, C, xin_chunks, gdram, w_in, cols_chunks, DEV=None):
    rmsnorm_fm(S, C, xin_chunks, gdram)
    big = C.big
    w_v = w_in.rearrange("(k p) n -> p k n", p=128)
    si = 0
    for c2 in range(0, NCH, 2):
        if DEV is not None and c2 in (12, 14, 16, 34, 38):
            continue
        slot = C.wslots[si % 2]
        si += 1
        c_lo = c2 * 128
        ncol = min(256, IN_COLS - c_lo)
        wv = slot.ap[:, 0:KC * ncol].rearrange("p (k n) -> p k n", k=KC)
        S.dma(C.wq, wv, w_v[:, :, c_lo:c_lo + ncol], sb=slot, load=True)
        for cs in range(2):
            c = c2 + cs
            if c >= NCH:
                break
            m = min(128, IN_COLS - c * 128)
            base = (c % 4) * 2
            P = C.psum[base:base + 2]
            for k in range(KC):
                for h in range(2):
                    S.op("pe", lambda e, k=k, h=h, P=P, wv=wv, cs=cs, m=m: e.matmul(
                        P[h].ap[0:m, :], r32(wv[:, k, cs * 128:cs * 128 + m]),
                        r32(big.ap[:, k, h * 512:(h + 1) * 512]),
                        start=(k == 0), stop=(k == KC - 1)), reads=[slot, big], writes=[P[h]])
            ob = C.cout[c % 2]
            fn = AF.Sigmoid if c >= SIG_FROM else AF.Copy
            for h in range(2):
                if c < SIG_FROM and h == 1:
                    S.op("dve", lambda e, h=h, P=P, ob=ob, m=m: e.tensor_copy(
                        out=ob.ap[0:m, h * 512:(h + 1) * 512], in_=P[h].ap[0:m, :]), reads=[P[h]], writes=[ob])
                else:
                    S.op("act", lambda e, h=h, P=P, ob=ob, m=m, fn=fn: e.activation(
                        out=ob.ap[0:m, h * 512:(h + 1) * 512], in_=P[h].ap[0:m, :], func=fn),
                        reads=[P[h]], writes=[ob])
            S.dma("sp", cols_chunks[c].ap[0:m, :], ob.ap[0:m, :], sb=ob, dram=cols_chunks[c], load=False)
            if DEV is not None and shk_row_of_chunk(c) is not None:
                r0 = shk_row_of_chunk(c)
                S.dma("sp", Par(lambda p, r0=r0: DEV["shk"](p)[r0:r0 + 128, :]), ob.ap[:, :], sb=ob, dram=DEV["t_shared"], load=False)
    if DEV is None:
        return
    vts = [C.sb(f"vt{i}", [128, 256], BF16) for i in range(2)]
    vi = 0
    blocks = [(R_VA[0], 256, 0), (R_VA[0] + 256, 256, 256), (R_VA[0] + 512, 256, 512),
              (R_VSLC[0], 256, 768), (R_VWIN[0], 256, 1024), (R_GC[0], 18, 1280)]
    for (c_lo, ncol, v0) in blocks:
        slot = C.wslots[si % 2]
        si += 1
        wv = slot.ap[:, 0:KC * ncol].rearrange("p (k n) -> p k n", k=KC)
        S.dma(C.wq, wv, w_v[:, :, c_lo:c_lo + ncol], sb=slot, load=True)
        for tt in range(NT // 128):
            p = C.psum[tt % 4]
            for k in range(KC):
                S.op("pe", lambda e, k=k, tt=tt, p=p, wv=wv, ncol=ncol: e.matmul(
                    p.ap[:, 0:ncol], r32(big.ap[:, k, tt * 128:(tt + 1) * 128]), r32(wv[:, k, :]),
                    start=(k == 0), stop=(k == KC - 1)), reads=[slot, big], writes=[p])
            vt = vts[vi % 2]
            vi += 1
            fn = AF.Sigmoid if ncol == 18 else AF.Copy
            S.op("act", lambda e, p=p, vt=vt, ncol=ncol, fn=fn: e.activation(out=vt.ap[:, 0:ncol], in_=p.ap[:, 0:ncol], func=fn),
                 reads=[p], writes=[vt])
            S.dma("sp", DEV["vtm"][tt * 128:(tt + 1) * 128, v0:v0 + ncol], vt.ap[:, 0:ncol], sb=vt, dram=DEV["t_vtm"], load=False)
            if ncol == 256:
                S.dma("sp", Par(lambda p, tt=tt, v0=v0: DEV["shv"](p)[tt * 128:(tt + 1) * 128, v0:v0 + 256]), vt.ap[:, :],
                      sb=vt, dram=DEV["t_shared"], load=False)
    nt = C.sb("noncet", [1, 16], mybir.dt.int32)
    S.dma("sp", nt.ap[:], DEV["nonce"], sb=nt)
    tf = Tile("flagdram")
    S.dma("sp", Par(lambda p: DEV["flag"](p)), nt.ap[:], sb=nt, dram=tf, load=False, extra_reads=[DEV["t_shared"]])


def build_proj_prog():
    import contextlib
    nc = bass.Bass("TRN2", target_bir_lowering=False)
    xT = nc.dram_tensor("xT", [D, NT], F32R, kind="ExternalInput").ap()
    g = nc.dram_tensor("g", [D], F32, kind="ExternalInput").ap()
    w_in = nc.dram_tensor("w_in", [D, IN_COLS], F32R, kind="ExternalInput").ap()
    colsT = nc.dram_tensor("colsT", [NCH * 128, NT], BF16, kind="ExternalOutput").ap()
    S = Sched(nc)
    with contextlib.ExitStack() as stack:
        C = alloc_common(nc, S, stack)
        C.wslots = [C.sb(f"wslot{i}", [128, 4096], F32R) for i in range(2)]
        C.cout = [C.sb(f"cout{i}", [128, NT], BF16) for i in range(2)]
        C.epsb = C.sb("epsb", [128, 1])
        S.op("dve", lambda e: e.memset(C.epsb.ap[:], EPS), writes=[C.epsb])
        phase_proj(S, C, dram_chunks("xT", xT), g, w_in, dram_chunks("colsT", colsT, NCH))
        S.emit()
    return nc


BIG = 30000.0
A_SCALE = 64 ** -0.5
C_SCALE = 128 ** -0.5
NQT = NT // 128
NKB = 16


def mm(S, out_t, out_ap, lhsT_t, lhsT_ap, rhs_t, rhs_ap, start, stop):
    rd = [t for t in (lhsT_t if isinstance(lhsT_t, (list, tuple)) else [lhsT_t])]
    rd += [t for t in (rhs_t if isinstance(rhs_t, (list, tuple)) else [rhs_t])]
    S.op("pe", lambda e: e.matmul(out_ap, lhsT_ap, rhs_ap, start=start, stop=stop), reads=rd, writes=[out_t])


def attn_block(S, C, q_t, q_ap, nq, k_t, k_ap, masks, v_list, accs, acc_w, first, last, scale):
    i = C.sbi
    C.sbi += 1
    sb = (C.psum[0], C.psum[1], C.psum[6])[i % 3]
    pT = C.pT[i % 3]
    nk = k_ap.shape[-1]
    mm(S, sb, sb.ap[0:nk, 0:nq], k_t, k_ap, q_t, q_ap, True, len(masks) == 0)
    for mi, (lt, la, rt, ra) in enumerate(masks):
        mm(S, sb, sb.ap[0:nk, 0:nq], lt, la, rt, ra, False, mi == len(masks) - 1)
    S.op("act", lambda e: e.activation(out=pT.ap[0:nk, 0:nq], in_=sb.ap[0:nk, 0:nq], func=AF.Exp, scale=scale),
         reads=[sb], writes=[pT])

    def pv():
        for n, (acc, (vt, va)) in enumerate(zip(accs, v_list)):
            mm(S, acc, acc.ap[:, 0:acc_w], pT, pT.ap[0:nk, n * 128:(n + 1) * 128], vt, va, first, last)
    pend = getattr(C, "pending_pv", None) or []
    pend.append(pv)
    if len(pend) > 2:
        pend.pop(0)()
    C.pending_pv = pend


def attn_flush(S, C):
    pend = getattr(C, "pending_pv", None) or []
    C.pending_pv = []
    for f_ in pend:
        f_()


def phase_mix(S, C, I, oT_chunks):
    sb = C.sb
    DEV = I.get("_dev")

    def ld_act(name, tile, out_ap, sel=None):
        if DEV is None:
            src = I[name]
            if name in ("kA", "qA"):
                src = src[:, sel, :]
            elif name == "vA":
                src = src[:, :, sel, :]
            S.dma("pool" if name == "xb" else "sp", out_ap, src, sb=tile)
            return
        cols, vtm = DEV["cols"], DEV["vtm"]
        def prv(o, apfn):
            S.dma("pool", o, Par(lambda p: apfn(DEV["shk"](1 - p), DEV["shv"](1 - p))), sb=tile, dram=DEV["t_prev"])
        if name == "qA":
            S.dma("sp", out_ap, cols[sel * 64:(sel + 1) * 64, :], sb=tile, dram=DEV["t_cols"])
        elif name == "kA":
            S.dma("sp", out_ap[:, NT:2 * NT], cols[R_KA[0] + sel * 64:R_KA[0] + (sel + 1) * 64, :], sb=tile, dram=DEV["t_cols"])
            prv(out_ap[:, 0:NT], lambda shk, shv: shk[SHK_KA + sel * 64:SHK_KA + (sel + 1) * 64, :])
        elif name == "vA":
            S.dma("sp", out_ap[:, 8:16, :], vtm[:, sel * 64:(sel + 1) * 64].rearrange("(kb p) c -> p kb c", p=128),
                  sb=tile, dram=DEV["t_vtm"])
            prv(out_ap[:, 0:8, :], lambda shk, shv: shv[:, sel * 64:(sel + 1) * 64].rearrange("(kb p) c -> p kb c", p=128))
        elif name == "xb":
            S.dma("pool", out_ap[:, :, 16:16 + NT], cols[R_XB[0]:R_XB[1], :].rearrange("(g p) t -> p g t", p=128),
                  sb=tile, dram=DEV["t_cols"])
            prv(out_ap[:, :, 0:16], lambda shk, shv: shk[SHK_XB:SHK_XB + 512, NT - 16:NT].rearrange("(g p) t -> p g t", p=128))
        elif name == "qC3":
            for g in range(2):
                for n in range(3):
                    r0 = R_QC[0] + (g * 3 + n) * 128
                    S.dma("sp", out_ap[:, g, :, n * 128:(n + 1) * 128],
                          cols[r0:r0 + 128, :].rearrange("p (qt q) -> p qt q", q=128), sb=tile, dram=DEV["t_cols"])
        elif name in ("kcmp", "vcmp", "kslc", "kwin"):
            r = {"kcmp": R_KCMP, "vcmp": R_VCMP, "kslc": R_KSLC, "kwin": R_KWIN}[name]
            so = {"kcmp": SHK_KCMP, "vcmp": SHK_VCMP, "kslc": SHK_KSLC, "kwin": SHK_KWIN}[name]
            S.dma("sp", out_ap[:, :, NT:2 * NT], cols[r[0]:r[1], :].rearrange("(g p) t -> p g t", p=128), sb=tile, dram=DEV["t_cols"])
            prv(out_ap[:, :, 0:NT], lambda shk, shv: shk[so:so + 256, :].rearrange("(g p) t -> p g t", p=128))
        elif name in ("vslc", "vwin"):
            c0 = 768 if name == "vslc" else 1024
            for g_ in range(2):
                cg = c0 + g_ * 128
                S.dma("sp", out_ap[:, 8:16, g_, :], vtm[:, cg:cg + 128].rearrange("(kb p) d -> p kb d", p=128), sb=tile, dram=DEV["t_vtm"])
                prv(out_ap[:, 0:8, g_, :], lambda shk, shv, cg=cg: shv[:, cg:cg + 128].rearrange("(kb p) d -> p kb d", p=128))
        elif name == "gc":
            S.dma("sp", out_ap, vtm[:, 1280:1298].rearrange("(qt p) c -> p qt c", p=128), sb=tile, dram=DEV["t_vtm"])
        else:
            raise KeyError(name)

    if DEV is not None:
        S.raw("pool", DEV["poll"])
    ident = sb("ident", [128, 128], BF16)
    identf = sb("identf", [128, 128], F32)
    S.dma("sp", ident.ap[:], I["ident"], sb=ident)
    S.dma("sp", identf.ap[:], I["identf"], sb=identf)
    am = sb("amask", [128, 7, 128], BF16)
    S.dma("sp", am.ap[:], I["amask"], sb=am)
    pk = sb("pkill", [128, 384], BF16)
    S.dma("sp", pk.ap[:], I["pkill"], sb=pk)
    wm = sb("wmask", [128, 2, 384], BF16)
    S.dma("sp", wm.ap[:], I["wmask"], sb=wm)
    C.pT = [sb(f"pT{i}", [128, 384], BF16) for i in range(3)]
    C.sbi = 0
    ostage = [sb(f"ostage{i}", [128, 128], F32) for i in range(2)]
    osi = [0]
    accs = C.psum[2:5]
    misc = C.psum[5]

    def emit_T(src_t, src_ap, chunk, qt):
        st = ostage[osi[0] % 2]
        osi[0] += 1
        S.op("pe", lambda e: e.transpose(misc.ap[:, 0:128], src_ap, identf.ap[:]), reads=[src_t, identf], writes=[misc])
        S.op("dve", lambda e: e.tensor_copy(out=st.ap[:], in_=misc.ap[:, 0:128]), reads=[misc], writes=[st])
        S.dma("sp", f32(oT_chunks[chunk].ap)[:, qt * 128:(qt + 1) * 128], st.ap[:], sb=st, dram=oT_chunks[chunk], load=False)

    kA3 = sb("kA3", [64, 3, 2 * NT], BF16)
    qA3 = sb("qA3", [64, 3, NT], BF16)
    vA3 = sb("vA3", [128, NKB, 3, 65], BF16)
    oA = [sb(f"oA{i}", [128, 256], F32) for i in range(NQT)]
    rl = sb("rl", [128, 8], F32)
    S.op("pool", lambda e: e.memset(vA3.ap[:, :, :, 64:65], 1.0), writes=[vA3])
    def a_mask(g, delta):
        if g == 0:
            return 0 if delta == 0 else 1
        if g == 1:
            return 2 if delta == 0 else (4 if delta == 4 else 3)
        return 5 if delta == 0 else 6
    maxd = (1, 4, 15)
    for h in range(4):
        for g in range(3):
            hh = g * 4 + h
            ld_act("kA", kA3, kA3.ap[:, g, :], hh)
            ld_act("qA", qA3, qA3.ap[:, g, :], hh)
            ld_act("vA", vA3, vA3.ap[:, :, g, 0:64], hh)
        for qt in range(NQT):
            qb = 8 + qt
            blocks = [(g, kb) for g in range(3) for kb in range(max(0, qb - maxd[g]), qb + 1)]
            acc = accs[qt % 3]
            for bi, (g, kb) in enumerate(blocks):
                masks = [(ident, ident.ap[:], am, am.ap[:, a_mask(g, qb - kb), :])]
                if kb < 8:
                    masks.append((ident, ident.ap[:], pk, pk.ap[:, 0:128]))
                attn_block(S, C, qA3, qA3.ap[:, g, qt * 128:(qt + 1) * 128], 128,
                           kA3, kA3.ap[:, g, kb * 128:(kb + 1) * 128], masks,
                           [(vA3, vA3.ap[:, kb, g, :])], [acc], 65, bi == 0, bi == len(blocks) - 1, A_SCALE)
            attn_flush(S, C)
            S.op("dve", lambda e, acc=acc: e.reciprocal(out=rl.ap[:, 0:1], in_=acc.ap[:, 64:65]), reads=[acc], writes=[rl])
            S.op("dve", lambda e, acc=acc, qt=qt, h=h: e.tensor_scalar(
                out=oA[qt].ap[:, h * 64:(h + 1) * 64], in0=acc.ap[:, 0:64], scalar1=rl.ap[:, 0:1], scalar2=None,
                op0=ALU.mult), reads=[acc, rl], writes=[oA[qt]])
    for qt in range(NQT):
        for c in range(2):
            emit_T(oA[qt], oA[qt].ap[:, c * 128:(c + 1) * 128], c, qt)

    xb = sb("xb", [128, 4, 16 + NT], F32)
    ld_act("xb", xb, xb.ap[:])
    if DEV is not None:
        hfl = sb("hflag", [128, 1], F32)
        S.dma("sp", hfl.ap[:], I["hflag"], sb=hfl)
        S.op("pool", lambda e: e.tensor_scalar(out=xb.ap[:, :, 0:16], in0=xb.ap[:, :, 0:16], scalar1=hfl.ap[:, 0:1], scalar2=None, op0=ALU.mult), reads=[xb, hfl], writes=[xb])
    icnt = sb("icnt", [128, 4, NT], F32)
    S.dma("sp", icnt.ap[:], I["invcnt"], sb=icnt)
    pw = sb("poolw", [128, 4, 128], F32)
    S.dma("sp", pw.ap[:], I["pool_w"].rearrange("g c d -> c g d"), sb=pw)
    pwb = sb("poolwb", [128, 4, 128], BF16)
    S.op("pool", lambda e: e.tensor_copy(out=pwb.ap[:], in_=pw.ap[:]), reads=[pw], writes=[pwb])
    psc = sb("poolsc", [128, 4], F32)
    S.dma("sp", psc.ap[:], I["pool_scale"].rearrange("(g p) -> p g", p=128), sb=psc, allow_slow_non_contiguous=True)
    t1 = sb("pl1", [128, 16 + NT], F32)
    t2 = sb("pl2", [128, 16 + NT], F32)
    zb = sb("zb", [128, NT], BF16)
    ybs = sb("ybs", [128, NT], F32)
    S.op("pool", lambda e: e.memset(t1.ap[:, 0:16], 0.0), writes=[t1])
    S.op("pool", lambda e: e.memset(t2.ap[:, 0:16], 0.0), writes=[t2])
    L = 16 + NT
    for gi in range(4):
        src_t, src = xb, xb.ap[:, gi, :]
        tt = [t1, t2]
        a0 = 0
        for s_ in range(gi + 1):
            sh = 1 << s_
            dst = tt[s_ % 2]
            a0 += sh
            S.op("pool", lambda e, dst=dst, src=src, sh=sh, a0=a0: e.tensor_tensor(
                out=dst.ap[:, a0:L], in0=src[:, a0:L], in1=src[:, a0 - sh:L - sh], op=ALU.add),
                reads=[src_t], writes=[dst])
            src_t, src = dst, dst.ap[:, :]
        S.op("dve", lambda e, src=src, gi=gi: e.tensor_tensor(out=ybs.ap[:], in0=src[:, 16:L], in1=icnt.ap[:, gi, :], op=ALU.mult),
             reads=[src_t, icnt], writes=[ybs])
        S.op("dve", lambda e, gi=gi: e.tensor_tensor(out=zb.ap[:], in0=ybs.ap[:], in1=xb.ap[:, gi, 16:L], op=ALU.subtract),
             reads=[ybs, xb], writes=[zb])
        for hf in range(2):
            P = C.psum[6 + hf]
            mm(S, P, P.ap[:, :], pwb, pwb.ap[:, gi, :], zb, zb.ap[:, hf * 512:(hf + 1) * 512], True, True)
            S.op("dve", lambda e, P=P, hf=hf, gi=gi: e.tensor_scalar(
                out=ybs.ap[:, hf * 512:(hf + 1) * 512], in0=P.ap[:, :], scalar1=psc.ap[:, gi:gi + 1], scalar2=None,
                op0=ALU.mult), reads=[P, psc], writes=[ybs])
        S.dma("sp", f32(oT_chunks[2 + gi].ap)[:, :], ybs.ap[:], sb=ybs, dram=oT_chunks[2 + gi], load=False)

    qC3 = sb("qC3", [128, 2, NQT, 384], BF16)
    ld_act("qC3", qC3, qC3.ap[:])
    kcmp = sb("kcmp", [128, 2, 2, 2 * NT], BF16)
    ld_act("kcmp", kcmp, kcmp.ap[:, 0])
    ld_act("vcmp", kcmp, kcmp.ap[:, 1])
    kslc = sb("kslc", [128, 2, 2 * NT], BF16)
    ld_act("kslc", kslc, kslc.ap[:])
    kwin = sb("kwin", [128, 2, 2 * NT], BF16)
    ld_act("kwin", kwin, kwin.ap[:])
    vslc = sb("vslc", [128, NKB, 2, 129], BF16)
    vwin = sb("vwin", [128, NKB, 2, 129], BF16)
    S.op("pool", lambda e: e.memset(vslc.ap[:, :, :, 128:129], 1.0), writes=[vslc])
    S.op("pool", lambda e: e.memset(vwin.ap[:, :, :, 128:129], 1.0), writes=[vwin])
    ld_act("vslc", vslc, vslc.ap[:, :, :, 0:128])
    ld_act("vwin", vwin, vwin.ap[:, :, :, 0:128])
    gc = sb("gc", [128, NQT, 18], BF16)
    ld_act("gc", gc, gc.ap[:])
    gcf = sb("gcf", [128, NQT, 18], F32)
    S.op("pool", lambda e: e.tensor_copy(out=gcf.ap[:], in_=gc.ap[:]), reads=[gc], writes=[gcf])
    cm = sb("cmask", [128, NQT, 384], BF16)
    S.dma("sp", cm.ap[:], I["cmask"], sb=cm)
    Eh = sb("Eh", [33, 2 * NT], BF16)
    S.dma("sp", Eh.ap[:], I["Eh"], sb=Eh)
    vblk = sb("vblk", [128, NQT, 32], F32)
    S.dma("sp", vblk.ap[:], I["validblk"], sb=vblk)
    cstb = sb("cstb", [128, NQT, 32], F32)
    S.dma("sp", cstb.ap[:], I["constb"], sb=cstb)
    w1b = sb("w1b", [128, 2, 32, 128], BF16)
    S.dma("pool", w1b.ap[:, 0], I["w1_k"].rearrange("(j p) n -> p j n", p=128), sb=w1b)
    S.dma("pool", w1b.ap[:, 1], I["w1_v"].rearrange("(j p) n -> p j n", p=128), sb=w1b)
    w2 = sb("w2", [128, 2, 128], F32)
    S.dma("sp", w2.ap[:, 0], I["w2_k"], sb=w2)
    S.dma("sp", w2.ap[:, 1], I["w2_v"], sb=w2)
    w2b = sb("w2b", [128, 2, 128], BF16)
    S.op("pool", lambda e: e.tensor_copy(out=w2b.ap[:], in_=w2.ap[:]), reads=[w2], writes=[w2b])
    peT = sb("peT", [128, 2, 32], F32)
    S.dma("sp", peT.ap[:, 0], I["pe_k"].rearrange("j d -> d j"), sb=peT, allow_slow_non_contiguous=True)
    S.dma("sp", peT.ap[:, 1], I["pe_v"].rearrange("j d -> d j"), sb=peT, allow_slow_non_contiguous=True)
    peTb = sb("peTb", [128, 2, 32], BF16)
    S.op("pool", lambda e: e.tensor_copy(out=peTb.ap[:], in_=peT.ap[:]), reads=[peT], writes=[peTb])
    onesb = sb("onesb", [1, 128], BF16)
    S.op("pool", lambda e: e.memset(onesb.ap[:], 1.0), writes=[onesb])
    brow = sb("brow", [1, 2, 128], BF16)
    kcT = sb("kcT", [128, 2, 128], BF16)
    vca = sb("vca", [128, 2, 161], BF16)
    S.op("pool", lambda e: e.memset(vca.ap[:, :, 128:129], 1.0), writes=[vca])
    for g in range(2):
        S.dma("sp", vca.ap[0:127, g, 129:161], I["overlap"], sb=vca)
    hx = sb("hx", [128, 128], F32)
    hy = sb("hy", [128, 128], F32)
    hz = sb("hz", [128, 128], F32)
    hg = sb("hg", [128, 128], BF16)
    for kv in range(2):
        for j in range(32):
            mm(S, misc, misc.ap[0:1, 0:128], peTb, peTb.ap[:, kv, j:j + 1], w1b, w1b.ap[:, kv, j, :], j == 0, j == 31)
        S.op("dve", lambda e, kv=kv: e.tensor_copy(out=brow.ap[:, kv, :], in_=misc.ap[0:1, 0:128]), reads=[misc], writes=[brow])
        for g in range(2):
            P = C.psum[6]
            for j in range(32):
                mm(S, P, P.ap[:, 0:127], w1b, w1b.ap[:, kv, j, :], kcmp,
                   kcmp.ap[:, kv, g, j:j + 16 * 126 + 1:16], j == 0, False)
            mm(S, P, P.ap[:, 0:127], brow, brow.ap[:, kv, :], onesb, onesb.ap[:, 0:127], False, True)
            S.op("dve", lambda e, P=P: e.tensor_copy(out=hx.ap[:, 0:127], in_=P.ap[:, 0:127]), reads=[P], writes=[hx])
            S.op("dve", lambda e: e.tensor_tensor(out=hy.ap[:, 0:127], in0=hx.ap[:, 0:127], in1=hx.ap[:, 0:127], op=ALU.mult),
                 reads=[hx], writes=[hy])
            S.op("dve", lambda e: e.tensor_scalar(out=hy.ap[:, 0:127], in0=hy.ap[:, 0:127], scalar1=0.044715, scalar2=1.0,
                                                  op0=ALU.mult, op1=ALU.add), reads=[hy], writes=[hy])
            S.op("dve", lambda e: e.tensor_tensor(out=hz.ap[:, 0:127], in0=hy.ap[:, 0:127], in1=hx.ap[:, 0:127], op=ALU.mult),
                 reads=[hy, hx], writes=[hz])
            S.op("act", lambda e: e.activation(out=hy.ap[:, 0:127], in_=hz.ap[:, 0:127], func=AF.Sigmoid, scale=1.5957691216),
                 reads=[hz], writes=[hy])
            S.op("dve", lambda e: e.tensor_tensor(out=hg.ap[:, 0:127], in0=hy.ap[:, 0:127], in1=hx.ap[:, 0:127], op=ALU.mult),
                 reads=[hy, hx], writes=[hg])
            P2 = C.psum[7]
            if kv == 0:
                mm(S, P2, P2.ap[:, 0:127], w2b, w2b.ap[:, 0, :], hg, hg.ap[:, 0:127], True, True)
                S.op("dve", lambda e, g=g, P2=P2: e.tensor_copy(out=kcT.ap[:, g, 0:127], in_=P2.ap[:, 0:127]), reads=[P2], writes=[kcT])
            else:
                mm(S, P2, P2.ap[0:127, 0:128], hg, hg.ap[:, 0:127], w2b, w2b.ap[:, 1, :], True, True)
                S.op("dve", lambda e, g=g, P2=P2: e.tensor_copy(out=vca.ap[0:127, g, 0:128], in_=P2.ap[0:127, 0:128]), reads=[P2], writes=[vca])

    imp = sb("imp", [128, 32], F32)
    sc = sb("score", [128, 32], F32)
    top8 = sb("top8", [128, 8], F32)
    selb = sb("selb", [128, 33], F32)
    S.op("pool", lambda e: e.memset(selb.ap[:, 32:33], -1.0), writes=[selb])
    selT3 = sb("selT3", [33, 384], BF16)
    sg_ = sb("sgate", [128, 16], F32)
    oC = [sb(f"oC{i}", [128, 384], F32) for i in range(2)]
    oci = 0
    for g in range(2):
        for qt in range(NQT):
            qb = 8 + qt
            q_ap = qC3.ap[:, g, qt, :]
            ocur = oC[oci % 2]
            oci += 1
            attn_block(S, C, qC3, q_ap, 384, kcT, kcT.ap[:, g, 0:127],
                       [(ident, ident.ap[0:127, 0:127], cm, cm.ap[0:127, qt, :])],
                       [(vca, vca.ap[0:127, g, :])] * 3, accs, 161, True, True, C_SCALE)
            attn_flush(S, C)
            for n in range(3):
                acc = accs[n]
                gi0 = (g * 3 + n) * 3
                S.op("dve", lambda e, acc=acc, n=n: e.tensor_scalar(
                    out=rl.ap[:, n:n + 1], in0=acc.ap[:, 128:129], scalar1=1e-30, scalar2=None, op0=ALU.max),
                    reads=[acc], writes=[rl])
                S.op("dve", lambda e, n=n: e.reciprocal(out=rl.ap[:, n:n + 1], in_=rl.ap[:, n:n + 1]), reads=[rl], writes=[rl])
                if n == 0:
                    S.op("dve", lambda e, acc=acc, n=n: e.tensor_scalar(
                        out=imp.ap[:], in0=acc.ap[:, 129:161], scalar1=rl.ap[:, n:n + 1], scalar2=None, op0=ALU.mult),
                        reads=[acc, rl], writes=[imp])
                else:
                    S.op("dve", lambda e, acc=acc, n=n: e.scalar_tensor_tensor(
                        out=imp.ap[:], in0=acc.ap[:, 129:161], scalar=rl.ap[:, n:n + 1], in1=imp.ap[:],
                        op0=ALU.mult, op1=ALU.add), reads=[acc, rl, imp], writes=[imp])
                S.op("dve", lambda e, n=n, gi0=gi0, qt=qt: e.tensor_tensor(
                    out=sg_.ap[:, n:n + 1], in0=rl.ap[:, n:n + 1], in1=gcf.ap[:, qt, gi0:gi0 + 1], op=ALU.mult),
                    reads=[rl, gcf], writes=[sg_])
                S.op("dve", lambda e, acc=acc, n=n, ocur=ocur: e.tensor_scalar(
                    out=ocur.ap[:, n * 128:(n + 1) * 128], in0=acc.ap[:, 0:128], scalar1=sg_.ap[:, n:n + 1], scalar2=None,
                    op0=ALU.mult), reads=[acc, sg_], writes=[ocur])
            S.op("dve", lambda e, qt=qt: e.tensor_tensor(out=sc.ap[:], in0=imp.ap[:], in1=vblk.ap[:, qt, :], op=ALU.mult),
                 reads=[imp, vblk], writes=[sc])
            S.op("dve", lambda e, qt=qt: e.tensor_tensor(out=sc.ap[:], in0=sc.ap[:], in1=cstb.ap[:, qt, :], op=ALU.add),
                 reads=[sc, cstb], writes=[sc])
            S.op("dve", lambda e: e.max(out=top8.ap[:], in_=sc.ap[:]), reads=[sc], writes=[top8])
            S.op("dve", lambda e: e.tensor_scalar(out=selb.ap[:, 0:32], in0=sc.ap[:], scalar1=top8.ap[:, 7:8], scalar2=-1.0,
                                                  op0=ALU.is_ge, op1=ALU.add), reads=[sc, top8], writes=[selb])
            for kb in range(qb - 4, qb + 1):
                masks = []
                if kb == qb:
                    masks.append((ident, ident.ap[:], wm, wm.ap[:, 0, :]))
                if kb == qb - 4:
                    masks.append((ident, ident.ap[:], wm, wm.ap[:, 1, :]))
                if kb < 8:
                    masks.append((ident, ident.ap[:], pk, pk.ap[:, :]))
                attn_block(S, C, qC3, q_ap, 384, kwin, kwin.ap[:, g, kb * 128:(kb + 1) * 128], masks,
                           [(vwin, vwin.ap[:, kb, g, :])] * 3, accs, 129, kb == qb - 4, kb == qb, C_SCALE)
            attn_flush(S, C)
            for n in range(3):
                acc = accs[n]
                gi2 = (g * 3 + n) * 3 + 2
                S.op("dve", lambda e, acc=acc, n=n: e.reciprocal(out=rl.ap[:, n:n + 1], in_=acc.ap[:, 128:129]), reads=[acc], writes=[rl])
                S.op("dve", lambda e, n=n, gi2=gi2, qt=qt: e.tensor_tensor(
                    out=sg_.ap[:, n:n + 1], in0=rl.ap[:, n:n + 1], in1=gcf.ap[:, qt, gi2:gi2 + 1], op=ALU.mult),
                    reads=[rl, gcf], writes=[sg_])
                S.op("dve", lambda e, acc=acc, n=n, ocur=ocur: e.scalar_tensor_tensor(
                    out=ocur.ap[:, n * 128:(n + 1) * 128], in0=acc.ap[:, 0:128], scalar=sg_.ap[:, n:n + 1],
                    in1=ocur.ap[:, n * 128:(n + 1) * 128], op0=ALU.mult, op1=ALU.add), reads=[acc, sg_, ocur], writes=[ocur])
            S.op("pe", lambda e: e.transpose(misc.ap[0:33, 0:128], selb.ap[:, 0:33], identf.ap[:]),
                 reads=[selb, identf], writes=[misc])
            for n in range(3):
                S.op("dve", lambda e, n=n: e.tensor_copy(out=selT3.ap[:, n * 128:(n + 1) * 128], in_=misc.ap[0:33, 0:128]),
                     reads=[misc], writes=[selT3])
            for kb in range(qb + 1):
                masks = [(Eh, Eh.ap[:, kb * 128:(kb + 1) * 128], selT3, selT3.ap[:, :])]
                if kb == qb:
                    masks.append((ident, ident.ap[:], wm, wm.ap[:, 0, :]))
                attn_block(S, C, qC3, q_ap, 384, kslc, kslc.ap[:, g, kb * 128:(kb + 1) * 128], masks,
                           [(vslc, vslc.ap[:, kb, g, :])] * 3, accs, 129, kb == 0, kb == qb, C_SCALE)
            attn_flush(S, C)
            for n in range(3):
                acc = accs[n]
                gi1 = (g * 3 + n) * 3 + 1
                S.op("dve", lambda e, acc=acc, n=n: e.reciprocal(out=rl.ap[:, n:n + 1], in_=acc.ap[:, 128:129]), reads=[acc], writes=[rl])
                S.op("dve", lambda e, n=n, gi1=gi1, qt=qt: e.tensor_tensor(
                    out=sg_.ap[:, n:n + 1], in0=rl.ap[:, n:n + 1], in1=gcf.ap[:, qt, gi1:gi1 + 1], op=ALU.mult),
                    reads=[rl, gcf], writes=[sg_])
                S.op("dve", lambda e, acc=acc, n=n, ocur=ocur: e.scalar_tensor_tensor(
                    out=ocur.ap[:, n * 128:(n + 1) * 128], in0=acc.ap[:, 0:128], scalar=sg_.ap[:, n:n + 1],
                    in1=ocur.ap[:, n * 128:(n + 1) * 128], op0=ALU.mult, op1=ALU.add), reads=[acc, sg_, ocur], writes=[ocur])
            for n in range(3):
                emit_T(ocur, ocur.ap[:, n * 128:(n + 1) * 128], 6 + g * 3 + n, qt)


MIX_INPUTS = {
    "ident": ([128, 128], BF16), "identf": ([128, 128], F32), "amask": ([128, 7, 128], BF16),
    "pkill": ([128, 384], BF16), "wmask": ([128, 2, 384], BF16),
    "kA": ([64, 12, 2 * NT], BF16), "qA": ([64, 12, NT], BF16), "vA": ([128, NKB, 12, 64], BF16),
    "xb": ([128, 4, 16 + NT], BF16), "invcnt": ([128, 4, NT], F32),
    "pool_w": ([4, 128, 128], F32), "pool_scale": ([512], F32),
    "qC3": ([128, 2, NQT, 384], BF16), "kcmp": ([128, 2, 2 * NT], BF16), "vcmp": ([128, 2, 2 * NT], BF16),
    "kslc": ([128, 2, 2 * NT], BF16), "kwin": ([128, 2, 2 * NT], BF16),
    "vslc": ([128, NKB, 2, 128], BF16), "vwin": ([128, NKB, 2, 128], BF16),
    "gc": ([128, NQT, 18], BF16), "cmask": ([128, NQT, 384], BF16), "Eh": ([33, 2 * NT], BF16),
    "validblk": ([128, NQT, 32], F32), "constb": ([128, NQT, 32], F32), "overlap": ([127, 32], BF16),
    "w1_k": ([4096, 128], F32), "w1_v": ([4096, 128], F32), "w2_k": ([128, 128], F32), "w2_v": ([128, 128], F32),
    "pe_k": ([32, 128], F32), "pe_v": ([32, 128], F32),
}


def build_mix_prog():
    import contextlib
    nc = bass.Bass("TRN2", target_bir_lowering=False)
    I = {k: nc.dram_tensor(k, shp, dt, kind="ExternalInput").ap() for k, (shp, dt) in MIX_INPUTS.items()}
    oT = nc.dram_tensor("oT", [12 * 128, NT], F32, kind="ExternalOutput").ap()
    S = Sched(nc)
    with contextlib.ExitStack() as stack:
        C = Ctx()

        def sb(name, shape, dt=F32):
            return Tile("s_" + name, stack.enter_context(nc.sbuf_tensor("s_" + name, shape, dt)))
        C.sb = sb
        C.psum = [Tile(f"ps{i}", stack.enter_context(nc.psum_tensor(f"ps{i}", [128, 512], F32))) for i in range(8)]
        phase_mix(S, C, I, dram_chunks("oT", oT, 12))
        S.emit()
    return nc


def phase_out(S, C, oT_chunks, sg_chunks, xin_chunks, xout_chunks, pa, pb, pc, w_out):
    sb = C.sb
    big = C.big
    ot = sb("ot", [128, 12, NT], F32R)
    for k in range(12):
        S.dma(C.wq, ot.ap[:, k, :], r32(oT_chunks[k].ap), sb=ot, dram=oT_chunks[k], load=True)
    pslots = [sb(f"pslot{i}", [128, 12, 128], F32R) for i in range(2)]
    sgt = [sb(f"sgt{i}", [128, 3, NT], BF16) for i in range(2)]
    tm = [sb(f"otm{i}", [128, 512], F32) for i in range(2)]
    pa_v = pa.rearrange("(k p) n -> p k n", p=128)
    pb_v = pb.rearrange("(k p) n -> p k n", p=128)
    pc_v = pc.rearrange("(k p) n -> p k n", p=128)
    for f in range(KC):
        slot = pslots[f % 2]
        fs = slice(f * 128, (f + 1) * 128)
        S.dma(C.wq, slot.ap[:, 0:2, :], pa_v[:, :, fs], sb=slot)
        S.dma(C.wq, slot.ap[:, 2:6, :], pb_v[:, :, fs], sb=slot)
        S.dma(C.wq, slot.ap[:, 6:12, :], pc_v[:, :, fs], sb=slot)
        sg = sgt[f % 2]
        for b in range(3):
            S.dma("sp", sg.ap[:, b, :], sg_chunks[b * 16 + f].ap, sb=sg, dram=sg_chunks[b * 16 + f])
        P = C.psum[0:6]
        for b, (k0, k1) in enumerate(((0, 2), (2, 6), (6, 12))):
            for k in range(k0, k1):
                for h in range(2):
                    p = P[b * 2 + h]
                    mm(S, p, p.ap[:], slot, slot.ap[:, k, :], ot, ot.ap[:, k, h * 512:(h + 1) * 512], k == k0, k == k1 - 1)
        for h in range(2):
            hs = slice(h * 512, (h + 1) * 512)
            t0, t1 = tm
            S.op("dve", lambda e, h=h, hs=hs, sg=sg, t0=t0, P=P: e.tensor_tensor(out=t0.ap[:], in0=P[0 + h].ap[:], in1=sg.ap[:, 0, hs], op=ALU.mult),
                 reads=[P[0 + h], sg], writes=[t0])
            S.op("dve", lambda e, h=h, hs=hs, sg=sg, t1=t1, P=P: e.tensor_tensor(out=t1.ap[:], in0=P[2 + h].ap[:], in1=sg.ap[:, 1, hs], op=ALU.mult),
                 reads=[P[2 + h], sg], writes=[t1])
            S.op("dve", lambda e, t0=t0, t1=t1: e.tensor_tensor(out=t0.ap[:], in0=t0.ap[:], in1=t1.ap[:], op=ALU.add),
                 reads=[t0, t1], writes=[t0])
            S.op("dve", lambda e, h=h, hs=hs, sg=sg, t1=t1, P=P: e.tensor_tensor(out=t1.ap[:], in0=P[4 + h].ap[:], in1=sg.ap[:, 2, hs], op=ALU.mult),
                 reads=[P[4 + h], sg], writes=[t1])
            S.op("dve", lambda e, f=f, hs=hs, t0=t0, t1=t1: e.tensor_tensor(out=big.ap[:, f, hs], in0=t0.ap[:], in1=t1.ap[:], op=ALU.add),
                 reads=[t0, t1], writes=[big])
    wo_v = w_out.rearrange("(k p) n -> p k n", p=128)
    for fb in range(D // 256):
        slot = C.wslots[fb % 2]
        wv = slot.ap[:, 0:KC * 256].rearrange("p (k n) -> p k n", k=KC)
        S.dma(C.wq, wv, wo_v[:, :, fb * 256:(fb + 1) * 256], sb=slot)
        P = C.psum[(fb % 2) * 4:(fb % 2) * 4 + 4]
        for fs in range(2):
            for k in range(KC):
                for h in range(2):
                    p = P[fs * 2 + h]
                    mm(S, p, p.ap[:], slot, wv[:, k, fs * 128:(fs + 1) * 128], big, big.ap[:, k, h * 512:(h + 1) * 512],
                       k == 0, k == KC - 1)
        for fs in range(2):
            fc = fb * 2 + fs
            xi = C.xin[fc % 2]
            xo = C.xout[fc % 2]
            S.dma("sp", xi.ap[:], f32(xin_chunks[fc].ap), sb=xi, dram=xin_chunks[fc], load=True)
            for h in range(2):
                p = P[fs * 2 + h]
                S.op("dve", lambda e, h=h, p=p, xi=xi, xo=xo: e.tensor_tensor(
                    out=xo.ap[:, h * 512:(h + 1) * 512], in0=p.ap[:], in1=xi.ap[:, h * 512:(h + 1) * 512], op=ALU.add),
                    reads=[p, xi], writes=[xo])
            S.dma("sp", f32(xout_chunks[fc].ap), xo.ap[:], sb=xo, dram=xout_chunks[fc], load=False)


def build_out_prog():
    import contextlib
    nc = bass.Bass("TRN2", target_bir_lowering=False)
    oT = nc.dram_tensor("oT", [12 * 128, NT], F32R, kind="ExternalInput").ap()
    sg = nc.dram_tensor("sg", [48 * 128, NT], BF16, kind="ExternalInput").ap()
    xT = nc.dram_tensor("xT", [D, NT], F32R, kind="ExternalInput").ap()
    pa = nc.dram_tensor("proj_a", [256, D], F32R, kind="ExternalInput").ap()
    pb = nc.dram_tensor("proj_b", [512, D], F32R, kind="ExternalInput").ap()
    pc = nc.dram_tensor("proj_c", [768, D], F32R, kind="ExternalInput").ap()
    wo = nc.dram_tensor("w_out", [D, D], F32R, kind="ExternalInput").ap()
    yT = nc.dram_tensor("yT", [D, NT], F32, kind="ExternalOutput").ap()
    S = Sched(nc)
    with contextlib.ExitStack() as stack:
        C = Ctx()

        def sb(name, shape, dt=F32):
            return Tile("s_" + name, stack.enter_context(nc.sbuf_tensor("s_" + name, shape, dt)))
        C.sb = sb
        C.wq = WQ
        C.psum = [Tile(f"ps{i}", stack.enter_context(nc.psum_tensor(f"ps{i}", [128, 512], F32))) for i in range(8)]
        C.big = sb("big", [128, KC, NT], F32R)
        C.wslots = [sb(f"wslot{i}", [128, 4096], F32R) for i in range(2)]
        C.xin = [sb(f"xin{i}", [128, NT]) for i in range(2)]
        C.xout = [sb(f"xout{i}", [128, NT]) for i in range(2)]
        phase_out(S, C, dram_chunks("oT", oT, 12), dram_chunks("sg", sg, 48), dram_chunks("xT", xT),
                  dram_chunks("yT", yT), pa, pb, pc, wo)
        S.emit()
    return nc


import ml_dtypes
BF = ml_dtypes.bfloat16
R_QA, R_KA, R_VA, R_XB, R_QC = (0, 768), (768, 1536), (1536, 2304), (2304, 2816), (2816, 3584)
R_KCMP, R_VCMP, R_KSLC, R_VSLC, R_KWIN, R_VWIN = (3584, 3840), (3840, 4096), (4096, 4352), (4352, 4608), (4608, 4864), (4864, 5120)
R_GC, R_GM = (5120, 5138), (5138, 11282)


def mix_consts(h):
    p = np.arange(128)[:, None]
    f = np.arange(128)[None, :]
    nb = lambda valid: np.where(valid, 0.0, -BIG).astype(BF)
    c = {}
    c["ident"] = np.eye(128, dtype=np.float32).astype(BF)
    c["identf"] = np.eye(128, dtype=np.float32)
    am = np.stack([p <= f, f <= p,
                   (p <= f) & ((f - p) % 4 == 0), (f - p) % 4 == 0, (f <= p) & ((f - p) % 4 == 0),
                   (p <= f) & ((f - p) % 16 == 0), (f - p) % 16 == 0], axis=1)
    c["amask"] = nb(am)
    c["pkill"] = np.full((128, 384), 0.0 if h == 1 else -BIG, dtype=np.float32).astype(BF)
    c["wmask"] = nb(np.stack([np.tile(p <= f, (1, 3)), np.tile(f < p, (1, 3))], axis=1))
    cl = np.arange(128)[:, None, None]
    cg = cl - (0 if h == 1 else 64)
    tg = h * 1024 + np.arange(8)[None, :, None] * 128 + (np.arange(384) % 128)[None, None, :]
    c["cmask"] = nb((cg >= 0) & (cl < 127) & (16 * cg + 31 <= tg))
    cgl = np.arange(127)[:, None] - (0 if h == 1 else 64)
    j = np.arange(32)[None, :]
    c["overlap"] = ((cgl >= 0) & (16 * cgl < 64 * j + 64) & (16 * cgl + 32 > 64 * j)).astype(np.float32).astype(BF)
    kg = np.arange(2048)[None, :] - (0 if h == 1 else 1024)
    E = np.zeros((33, 2048), np.float32)
    E[:32] = np.where((kg >= 0) & (kg // 64 == np.arange(32)[:, None]), BIG, 0.0)
    E[32] = np.where(kg[0] < 0, BIG, 0.0)
    c["Eh"] = E.astype(BF)
    t = h * 1024 + np.arange(8)[None, :, None] * 128 + np.arange(128)[:, None, None]
    tb = t // 64
    jj = np.arange(32)[None, None, :]
    c["validblk"] = (jj <= tb).astype(np.float32)
    forced = (jj == 0) | (jj == tb) | (jj == tb - 1)
    c["constb"] = np.where(jj > tb, -1.0, np.where(forced, 100.0, 0.0)).astype(np.float32)
    tt = h * 1024 + np.arange(1024)[None, None, :]
    w = np.array([2, 4, 8, 16])[None, :, None]
    c["invcnt"] = np.broadcast_to(1.0 / np.minimum(tt + 1, w), (128, 4, 1024)).astype(np.float32).copy()
    return c


def prep_mix_inputs(own, prev, h):
    def full(r):
        o = own[r[0]:r[1]]
        pv = prev[r[0]:r[1]] if h == 1 else np.zeros_like(o)
        return np.concatenate([pv, o], axis=1)
    d = {}
    d["qA"] = own[R_QA[0]:R_QA[1]].reshape(12, 64, NT).transpose(1, 0, 2)
    d["kA"] = full(R_KA).reshape(12, 64, 2 * NT).transpose(1, 0, 2)
    d["vA"] = full(R_VA).T.reshape(16, 128, 12, 64).transpose(1, 0, 2, 3)
    d["xb"] = full(R_XB)[:, NT - 16:].reshape(4, 128, 16 + NT).transpose(1, 0, 2)
    d["qC3"] = own[R_QC[0]:R_QC[1]].reshape(2, 3, 128, 8, 128).transpose(2, 0, 3, 1, 4).reshape(128, 2, 8, 384)
    for nm, r in (("kcmp", R_KCMP), ("vcmp", R_VCMP), ("kslc", R_KSLC), ("kwin", R_KWIN)):
        d[nm] = full(r).reshape(2, 128, 2 * NT).transpose(1, 0, 2)
    for nm, r in (("vslc", R_VSLC), ("vwin", R_VWIN)):
        d[nm] = full(r).T.reshape(16, 128, 2, 128).transpose(1, 0, 2, 3)
    d["gc"] = own[R_GC[0]:R_GC[1]].T.reshape(8, 128, 18).transpose(1, 0, 2)
    return {k: np.ascontiguousarray(v) for k, v in d.items()}


_PROGS = {}


def _prog(name):
    if name not in _PROGS:
        _PROGS[name] = {"ffn": lambda: build_ffn_prog(False), "ffn_final": lambda: build_ffn_prog(True),
                        "proj": build_proj_prog, "mix": build_mix_prog, "out": build_out_prog}[name]()
    return _PROGS[name]


def _run(name, in_maps):
    res = run_bass_kernel_spmd(_prog(name), in_maps, core_ids=list(range(8)))
    return res.results


def kernel_unfused(x, ffn1_norm, ffn1_wi, ffn1_wo, mix_norm, w_in, pool_w, pool_scale, cmp_pe_k, cmp_w1_k, cmp_w2_k,
           cmp_pe_v, cmp_w1_v, cmp_w2_v, proj_a, proj_b, proj_c, w_out, ffn2_norm, ffn2_wi, ffn2_wo, final_norm):
    f = lambda a: np.ascontiguousarray(np.asarray(a, dtype=np.float32))
    x = f(x)
    xT = [np.ascontiguousarray(x[c // 2, (c % 2) * NT:(c % 2 + 1) * NT, :].T) for c in range(8)]
    consts = [mix_consts(h) for h in range(2)]
    depth = ffn1_norm.shape[0]
    for l in range(depth):
        r = _run("ffn", [{"xT": xT[c], "g": f(ffn1_norm[l]), "wi": f(ffn1_wi[l]), "wo": f(ffn1_wo[l])} for c in range(8)])
        xT = [r[c]["yT"] for c in range(8)]
        r = _run("proj", [{"xT": xT[c], "g": f(mix_norm[l]), "w_in": f(w_in[l])} for c in range(8)])
        cols = [r[c]["colsT"] for c in range(8)]
        wts = dict(pool_w=f(pool_w[l]), pool_scale=f(pool_scale[l]), w1_k=f(cmp_w1_k[l]), w1_v=f(cmp_w1_v[l]),
                   w2_k=f(cmp_w2_k[l]), w2_v=f(cmp_w2_v[l]), pe_k=f(cmp_pe_k[l]), pe_v=f(cmp_pe_v[l]))
        ims = []
        for c in range(8):
            h = c % 2
            d = prep_mix_inputs(cols[c], cols[c - 1] if h == 1 else None, h)
            d.update(consts[h])
            d.update(wts)
            ims.append(d)
        r = _run("mix", ims)
        oT = [r[c]["oT"] for c in range(8)]
        r = _run("out", [{"oT": oT[c], "sg": np.ascontiguousarray(cols[c][R_GM[0]:R_GM[1]]), "xT": xT[c],
                          "proj_a": f(proj_a[l]), "proj_b": f(proj_b[l]), "proj_c": f(proj_c[l]), "w_out": f(w_out[l])}
                         for c in range(8)])
        xT = [r[c]["yT"] for c in range(8)]
        if l < depth - 1:
            r = _run("ffn", [{"xT": xT[c], "g": f(ffn2_norm[l]), "wi": f(ffn2_wi[l]), "wo": f(ffn2_wo[l])} for c in range(8)])
            xT = [r[c]["yT"] for c in range(8)]
        else:
            r = _run("ffn_final", [{"xT": xT[c], "g": f(ffn2_norm[l]), "wi": f(ffn2_wi[l]), "wo": f(ffn2_wo[l]),
                                    "gf": f(final_norm)} for c in range(8)])
            xT = [r[c]["zT"] for c in range(8)]
    out = np.empty((4, 2 * NT, D), np.float32)
    for c in range(8):
        out[c // 2, (c % 2) * NT:(c % 2 + 1) * NT, :] = xT[c].T
    return out


I32 = mybir.dt.int32
CONST_INPUTS = ["ident", "identf", "amask", "pkill", "wmask", "invcnt", "cmask", "Eh", "validblk", "constb", "overlap"]
WEIGHT_SPECS = {
    "ffn1_norm": ([4, D], F32), "ffn1_wi": ([4, D, 2 * DFF], F32R), "ffn1_wo": ([4, DFF, D], F32R),
    "mix_norm": ([4, D], F32), "w_in": ([4, D, IN_COLS], F32R), "pool_w": ([4, 4, 128, 128], F32),
    "pool_scale": ([4, 512], F32), "cmp_pe_k": ([4, 32, 128], F32), "cmp_w1_k": ([4, 4096, 128], F32),
    "cmp_w2_k": ([4, 128, 128], F32), "cmp_pe_v": ([4, 32, 128], F32), "cmp_w1_v": ([4, 4096, 128], F32),
    "cmp_w2_v": ([4, 128, 128], F32), "proj_a": ([4, 256, D], F32R), "proj_b": ([4, 512, D], F32R),
    "proj_c": ([4, 768, D], F32R), "w_out": ([4, D, D], F32R), "ffn2_norm": ([4, D], F32),
    "ffn2_wi": ([4, D, 2 * DFF], F32R), "ffn2_wo": ([4, DFF, D], F32R), "final_norm": ([D], F32),
}


def build_fused(depth=4, upto=None, skip=()):
    import contextlib
    nc = bass.Bass("TRN2", target_bir_lowering=False)
    E = {k: nc.dram_tensor(k, ([depth] + shp[1:]) if len(shp) > 1 else shp, dt, kind="ExternalInput").ap()
         for k, (shp, dt) in WEIGHT_SPECS.items()}
    for k in CONST_INPUTS:
        shp, dt = MIX_INPUTS[k]
        E[k] = nc.dram_tensor(k, shp, dt, kind="ExternalInput").ap()
    E["hflag"] = nc.dram_tensor("hflag", [128, 1], F32, kind="ExternalInput").ap()
    nonce = nc.dram_tensor("nonce", [1, 16], I32, kind="ExternalInput").ap()
    xT = nc.dram_tensor("xT", [D, NT], F32R, kind="ExternalInput").ap()
    yT = nc.dram_tensor("yT", [D, NT], F32, kind="ExternalOutput").ap()
    xs = [nc.dram_tensor(f"xs{i}", [D, NT], F32R).ap() for i in range(2)]
    cols = nc.dram_tensor("cols_s", [NCH * 128, NT], BF16).ap()
    vtm = nc.dram_tensor("vtm_s", [NT, 1408], BF16).ap()
    oTs = nc.dram_tensor("oT_s", [12 * 128, NT], F32R).ap()
    shk = nc.dram_tensor("shk_s", [2 * 4 * SHK_ROWS, NT], BF16, addr_space="Shared").ap()
    shv = nc.dram_tensor("shv_s", [2 * 4 * NT, SHV_COLS], BF16, addr_space="Shared").ap()
    flags = nc.dram_tensor("flags_s", [8, 16], I32, addr_space="Shared").ap()
    S = Sched(nc)

    def par(eng):
        key = ("par", id(eng))
        if key not in S.eng_ctx:
            S.eng_ctx[key] = eng.partition_id() % 2
        return S.eng_ctx[key]

    with contextlib.ExitStack() as top:
        psum = [Tile(f"ps{i}", top.enter_context(nc.psum_tensor(f"ps{i}", [128, 512], F32))) for i in range(8)]
        phase_no = [0]

        def new_ctx(stack, norm=True):
            C = Ctx()
            pn = phase_no[0]
            phase_no[0] += 1

            def sb(name, shape, dt=F32):
                nm = f"p{pn}_{name}"
                return Tile(nm, stack.enter_context(nc.sbuf_tensor(nm, shape, dt)))
            C.sb = sb
            C.wq = WQ
            C.psum = psum
            if norm:
                C.big = sb("big", [128, KC, NT], F32R)
                C.ones = sb("ones", [128, 128], F32R)
                C.ones32 = sb("ones32", [128, 128], F32)
                S.op("dve", lambda e: e.memset(C.ones32.ap[:], 1.0), writes=[C.ones32])
                S.op("dve", lambda e: e.tensor_copy(out=C.ones.ap[:], in_=C.ones32.ap[:]), reads=[C.ones32], writes=[C.ones])
                C.rstd = sb("rstd", [128, NT])
                C.sq = [sb(f"sq{i}", [128, NT], F32R) for i in range(2)]
                C.gv = sb("gv", [128, KC])
                C.epsb = sb("epsb", [128, 1])
                S.op("dve", lambda e: e.memset(C.epsb.ap[:], EPS), writes=[C.epsb])
            return C

        def ffn_tiles(C, JP=None):
            JP = JP or JPASS
            C.wslots = [C.sb(f"wslot{i}", [128, 4096], F32R) for i in range(2)]
            C.gbuf = C.sb("gbuf", [128, JP, NT], F32R)
            C.stmp = [C.sb(f"stmp{i}", [128, 512]) for i in range(2)]
            C.xin = [C.sb(f"xin{i}", [128, NT]) for i in range(2)]
            C.xout = [C.sb(f"xout{i}", [128, NT]) for i in range(2)]

        cur = dram_chunks("xT", xT)
        bufs = [dram_chunks("xs0_", xs[0]), dram_chunks("xs1_", xs[1])]
        bi = 0
        for l in range(depth):
            if "ffn1" not in skip:
                with contextlib.ExitStack() as ph:
                    C = new_ctx(ph)
                    ffn_tiles(C)
                    phase_ffn(S, C, cur, bufs[bi], E["ffn1_norm"][l], E["ffn1_wi"][l], E["ffn1_wo"][l])
                    S.end_phase()
                cur = bufs[bi]
                bi ^= 1
            if "ffn_twice" in skip:
                with contextlib.ExitStack() as ph:
                    C = new_ctx(ph)
                    ffn_tiles(C)
                    phase_ffn(S, C, cur, bufs[bi], E["ffn1_norm"][l], E["ffn1_wi"][l], E["ffn1_wo"][l])
                    S.end_phase()
                cur = bufs[bi]
                bi ^= 1
            if upto == "ffn1":
                break
            t_shared, t_vtm, t_cols, t_prev = Tile("t_shared"), Tile("t_vtm"), Tile("t_cols"), Tile("t_prev")
            DEV = {
                "cols": cols, "vtm": vtm, "t_shared": t_shared, "t_vtm": t_vtm, "t_cols": t_cols, "t_prev": t_prev,
                "nonce": nonce,
                "shk": lambda p, l=l: shk[(p * 4 + l) * SHK_ROWS:(p * 4 + l + 1) * SHK_ROWS, :],
                "shv": lambda p, l=l: shv[(p * 4 + l) * NT:(p * 4 + l + 1) * NT, :],
                "flag": lambda p, l=l: flags[p * 4 + l:p * 4 + l + 1, :],
            }

            def poll(e, l=l):
                pv = S.par_of(e)
                for p in (0, 1):
                    with (e.If(pv == 0) if p == 0 else e.Else()):
                        with e.register(f"pr{l}_{p}") as r, e.register(f"pn{l}_{p}") as nv, e.register(f"pf{l}_{p}") as fv:
                            e.load(nv, nonce[0:1, 0:1])
                            e.reg_mov(r, 1)
                            with e.While(r):
                                e.load(fv, flags[(1 - p) * 4 + l:(1 - p) * 4 + l + 1, 0:1])
                                e.reg_sub(r, fv, nv)
            DEV["poll"] = poll
            with contextlib.ExitStack() as ph:
                C = new_ctx(ph)
                C.wslots = [C.sb(f"wslot{i}", [128, 4096], F32R) for i in range(2)]
                C.cout = [C.sb(f"cout{i}", [128, NT], BF16) for i in range(2)]
                phase_proj(S, C, cur, E["mix_norm"][l], E["w_in"][l], dram_chunks("cols", cols, NCH), DEV=(None if "nodev" in skip else DEV))
                S.end_phase()
            if "proj_twice" in skip:
                with contextlib.ExitStack() as ph:
                    C = new_ctx(ph)
                    C.wslots = [C.sb(f"wslot{i}", [128, 4096], F32R) for i in range(2)]
                    C.cout = [C.sb(f"cout{i}", [128, NT], BF16) for i in range(2)]
                    phase_proj(S, C, cur, E["mix_norm"][l], E["w_in"][l], dram_chunks("cols", cols, NCH), DEV=None)
                    S.end_phase()
            if upto == "proj":
                break
            with contextlib.ExitStack() as ph:
                C = new_ctx(ph, norm=False)
                I = {k: E[k] for k in CONST_INPUTS}
                I["hflag"] = E["hflag"]
                I.update(pool_w=E["pool_w"][l], pool_scale=E["pool_scale"][l], w1_k=E["cmp_w1_k"][l], w1_v=E["cmp_w1_v"][l],
                         w2_k=E["cmp_w2_k"][l], w2_v=E["cmp_w2_v"][l], pe_k=E["cmp_pe_k"][l], pe_v=E["cmp_pe_v"][l])
                I["_dev"] = DEV
                oT_ch = dram_chunks("oTs", oTs, 12)
                phase_mix(S, C, I, oT_ch)
                S.end_phase()
            if upto == "mix":
                break
            with contextlib.ExitStack() as ph:
                C = new_ctx(ph, norm=False)
                C.big = C.sb("big", [128, KC, NT], F32R)
                C.wslots = [C.sb(f"wslot{i}", [128, 4096], F32R) for i in range(2)]
                C.xin = [C.sb(f"xin{i}", [128, NT]) for i in range(2)]
                C.xout = [C.sb(f"xout{i}", [128, NT]) for i in range(2)]
                sg_ch = [Tile(f"sg{i}", cols[R_GM[0] + i * 128:R_GM[0] + (i + 1) * 128, :]) for i in range(48)]
                phase_out(S, C, dram_chunks("oTs", oTs, 12), sg_ch, cur, bufs[bi], E["proj_a"][l], E["proj_b"][l],
                          E["proj_c"][l], E["w_out"][l])
                S.end_phase()
            cur = bufs[bi]
            bi ^= 1
            with contextlib.ExitStack() as ph:
                C = new_ctx(ph)
                ffn_tiles(C)
                phase_ffn(S, C, cur, bufs[bi], E["ffn2_norm"][l], E["ffn2_wi"][l], E["ffn2_wo"][l])
                S.end_phase()
            cur = bufs[bi]
            bi ^= 1
        with contextlib.ExitStack() as ph:
            C = new_ctx(ph)
            phase_final_norm(S, C, cur, dram_chunks("yT", yT), E["final_norm"])
            S.end_phase()
    return nc


_FUSED = {}


def kernel_fused(depth=4, upto=None, ncores=8, skip=(), **inp):
    f = lambda a: np.ascontiguousarray(np.asarray(a, dtype=np.float32))
    x = f(inp["x"])
    if (depth, upto) not in _FUSED:
        _FUSED[(depth, upto)] = build_fused(depth, upto, skip)
    nc = _FUSED[(depth, upto)]
    shared = {k: (f(inp[k][:depth]) if len(WEIGHT_SPECS[k][0]) > 1 else f(inp[k])) for k in WEIGHT_SPECS}
    consts = [mix_consts(h) for h in range(2)]
    nonce = np.full((1, 16), np.random.randint(1, 2 ** 30), np.int32)
    ims = []
    for c in range(ncores):
        h = c % 2
        d = dict(shared)
        d.update({k: consts[h][k] for k in CONST_INPUTS})
        d["hflag"] = np.full((128, 1), float(h), np.float32)
        d["nonce"] = nonce
        d["xT"] = np.ascontiguousarray(x[c // 2, h * NT:(h + 1) * NT, :].T)
        ims.append(d)
    res = run_bass_kernel_spmd(nc, ims, core_ids=list(range(ncores)))
    out = np.zeros((4, 2 * NT, D), np.float32)
    for c in range(ncores):
        out[c // 2, (c % 2) * NT:(c % 2 + 1) * NT, :] = res.results[c]["yT"].T
    return out


def kernel(**inputs):
    return kernel_fused(depth=4, **inputs)
```
